# Optimizing a Trainium2 kernel written in Bass

```python
import math
import jax, jax.numpy as jnp
from jax import lax
import numpy as np

D_MODEL = 1024
BATCH = 16
SEQ = 2048
DEPTH = 2
DEC_BATCH = 32
DEC_SEQ = 32
PAST_LEN = 1024

CHUNK = 64
SUB = 16
D_MIX = 2 * D_MODEL
SSD_WIDTH = D_MIX // 2
SSD_HEADDIM = 64
SSD_HEADS = SSD_WIDTH // SSD_HEADDIM
SSD_GROUPS = 2
SSD_REP = SSD_HEADS // SSD_GROUPS
SSD_STATE = 128
CONV_W = 4
CONV_CH = SSD_WIDTH + 2 * SSD_GROUPS * SSD_STATE
ML_WIDTH = D_MIX // 4
ML_HEADS = 4
ML_HD = ML_WIDTH // ML_HEADS
HG_WIDTH = D_MIX // 4
HG_HEADS = 4
HG_HD = HG_WIDTH // HG_HEADS
D_FF = -(-8 * D_MODEL // (3 * 256)) * 256
IN_SIZES = (SSD_WIDTH, CONV_CH, SSD_HEADS,
            ML_WIDTH, ML_WIDTH, ML_WIDTH, ML_HEADS, ML_HEADS, ML_WIDTH,
            HG_WIDTH, HG_WIDTH, HG_WIDTH, HG_WIDTH)
IN_COLS = sum(IN_SIZES)
EPS = 1e-6

kernel_name = 'hybrid_ssd_mlstm_hgrn2_stream_step'


def rmsnorm(x, gain):
    xf = x.astype(jnp.float32)
    y = xf * lax.rsqrt(jnp.mean(xf * xf, axis=-1, keepdims=True) + EPS)
    return (y * gain.astype(jnp.float32)).astype(x.dtype)


def split_cols(a, sizes):
    idx, acc = [], 0
    for s in sizes[:-1]:
        acc += s
        idx.append(acc)
    return jnp.split(a, idx, axis=-1)


def to_chunks(a, fill):
    b, t = a.shape[:2]
    tp = -(-t // CHUNK) * CHUNK
    pad = [(0, 0), (0, tp - t)] + [(0, 0)] * (a.ndim - 2)
    a = jnp.pad(a.astype(jnp.float32), pad, constant_values=fill)
    a = a.reshape((b, tp // CHUNK, CHUNK) + a.shape[2:])
    return jnp.moveaxis(a, 1, 0)


def from_chunks(a, t):
    a = jnp.moveaxis(a, 0, 1)
    a = a.reshape((a.shape[0], -1) + a.shape[3:])
    return a[:, :t]


def ssd_chunk(h, inp):
    x, a, bm, cm = inp
    L = x.shape[1]
    acum = jnp.cumsum(a, axis=1)
    causal = jnp.tril(jnp.ones((L, L), dtype=bool))
    seg = acum[:, :, None] - acum[:, None, :]
    decay = jnp.exp(jnp.where(causal[None, :, :, None, None], seg, -jnp.inf))
    cb = jnp.einsum('blgn,bsgn->blsg', cm, bm)
    y = jnp.einsum('blsg,blsgr,bsgrp->blgrp', cb, decay, x)
    y = y + jnp.einsum('blgn,bgrpn,blgr->blgrp', cm, h, jnp.exp(acum))
    dec_end = jnp.exp(acum[:, -1:] - acum)
    h_new = h * jnp.exp(acum[:, -1])[..., None, None] + jnp.einsum('blgn,blgr,blgrp->bgrpn', bm, dec_end, x)
    return h_new, y


def mlstm_chunk(carry, inp):
    c, n, m = carry
    q, k, v, ig, lf = inp
    L = q.shape[1]
    causal = jnp.tril(jnp.ones((L, L), dtype=bool))
    fcum = jnp.cumsum(lf, axis=1)
    dmat = fcum[:, :, None] - fcum[:, None, :] + ig[:, None, :]
    dmat = jnp.where(causal[None, :, :, None], dmat, -jnp.inf)
    inter = fcum + m[:, None]
    m_i = jnp.maximum(jnp.max(dmat, axis=2), inter)
    w_intra = jnp.exp(dmat - m_i[:, :, None])
    w_inter = jnp.exp(inter - m_i)
    s = jnp.einsum('blhk,bshk->blsh', q, k) * w_intra
    num = jnp.einsum('blsh,bshv->blhv', s, v) + w_inter[..., None] * jnp.einsum('blhk,bhkv->blhv', q, c)
    den = jnp.sum(s, axis=2) + w_inter * jnp.einsum('blhk,bhk->blh', q, n)
    h = num / jnp.maximum(jnp.abs(den), jnp.exp(-m_i))[..., None]
    f_end = fcum[:, -1]
    dl = f_end[:, None] - fcum + ig
    m_new = jnp.maximum(f_end + m, jnp.max(dl, axis=1))
    wk = jnp.exp(dl - m_new[:, None])
    sc = jnp.exp(f_end + m - m_new)
    c_new = sc[..., None, None] * c + jnp.einsum('blh,blhk,blhv->bhkv', wk, k, v)
    n_new = sc[..., None] * n + jnp.einsum('blh,blhk->bhk', wk, k)
    return (c_new, n_new, m_new), h


def gla_chunk(state, inp):
    q, k, v, g = inp
    b, L, H, K = q.shape
    V = v.shape[-1]
    ns = L // SUB
    gc = jnp.cumsum(g, axis=1)
    gs = gc.reshape(b, ns, SUB, H, K)
    qs = q.reshape(b, ns, SUB, H, K)
    ks = k.reshape(b, ns, SUB, H, K)
    vs = v.reshape(b, ns, SUB, H, V)
    gref = jnp.concatenate([jnp.zeros_like(gs[:, :1, 0]), gs[:, :-1, -1]], axis=1)
    q_ref = qs * jnp.exp(gs - gref[:, :, None])
    before = jnp.arange(L)[None, :] < (jnp.arange(ns) * SUB)[:, None]
    k_ref = k[:, None] * jnp.exp(jnp.where(before[None, :, :, None, None], gref[:, :, None] - gc[:, None], -jnp.inf))
    a_off = jnp.einsum('bashk,bajhk->bhasj', q_ref, k_ref)
    o = jnp.einsum('bhasj,bjhv->bashv', a_off, v)
    tri = jnp.tril(jnp.ones((SUB, SUB), dtype=bool))
    pair = jnp.exp(jnp.where(tri[None, None, :, :, None, None], gs[:, :, :, None] - gs[:, :, None], -jnp.inf))
    a_diag = jnp.einsum('bashk,basthk,bathk->bhast', qs, pair, ks)
    o = o + jnp.einsum('bhast,bathv->bashv', a_diag, vs)
    o = o.reshape(b, L, H, V) + jnp.einsum('blhk,bhkv->blhv', q * jnp.exp(gc), state)
    s_new = jnp.exp(gc[:, -1])[..., None] * state + jnp.einsum('blhk,blhv->bhkv', k * jnp.exp(gc[:, -1:] - gc), v)
    return s_new, o


def trunk_layer(x, state, p):
    conv_st, ssd_st, mc_st, mn_st, mm_st, hg_st = state
    f32 = jnp.float32
    b, t, _ = x.shape
    h = rmsnorm(x, p['norm_mix'])
    proj = jnp.einsum('btd,dc->btc', h, p['w_in'])
    (z, xbc, dt_raw, mq, mk, mv, mi, mf, mo, hq, hf, hi, hgate) = split_cols(proj, IN_SIZES)

    xpad = jnp.concatenate([conv_st.astype(xbc.dtype), xbc], axis=1)
    conv = p['conv_b'] + sum(xpad[:, w:w + t] * p['conv_w'][w] for w in range(CONV_W))
    new_conv = xpad[:, -(CONV_W - 1):]
    xbc = jax.nn.silu(conv)
    xs, bmat, cmat = split_cols(xbc, (SSD_WIDTH, SSD_GROUPS * SSD_STATE, SSD_GROUPS * SSD_STATE))
    dt = jax.nn.softplus(dt_raw.astype(f32) + p['dt_bias'].astype(f32)).reshape(b, t, SSD_GROUPS, SSD_REP)
    a_neg = -jnp.exp(p['a_log'].astype(f32)).reshape(SSD_GROUPS, SSD_REP)
    xh = xs.astype(f32).reshape(b, t, SSD_GROUPS, SSD_REP, SSD_HEADDIM)
    ssd_in = (to_chunks(xh * dt[..., None], 0.0), to_chunks(dt * a_neg, 0.0),
              to_chunks(bmat.reshape(b, t, SSD_GROUPS, SSD_STATE), 0.0),
              to_chunks(cmat.reshape(b, t, SSD_GROUPS, SSD_STATE), 0.0))
    h0 = ssd_st.astype(f32).reshape(b, SSD_GROUPS, SSD_REP, SSD_HEADDIM, SSD_STATE)
    h_fin, y = lax.scan(ssd_chunk, h0, ssd_in)
    y = from_chunks(y, t) + p['d_skip'].astype(f32).reshape(SSD_GROUPS, SSD_REP)[..., None] * xh
    y_ssd = rmsnorm(y.reshape(b, t, SSD_WIDTH) * jax.nn.silu(z.astype(f32)), p['ssd_gain'])

    q = mq.reshape(b, t, ML_HEADS, ML_HD)
    k = mk.reshape(b, t, ML_HEADS, ML_HD) * (ML_HD ** -0.5)
    v = mv.reshape(b, t, ML_HEADS, ML_HD)
    ig = mi.astype(f32) + p['ml_bi'].astype(f32)
    lf = jax.nn.log_sigmoid(mf.astype(f32) + p['ml_bf'].astype(f32))
    carry0 = (mc_st.astype(f32), mn_st.astype(f32), mm_st.astype(f32))
    ml_in = (to_chunks(q, 0.0), to_chunks(k, 0.0), to_chunks(v, 0.0), to_chunks(ig, -jnp.inf), to_chunks(lf, 0.0))
    (mc, mn, mm), hc = lax.scan(mlstm_chunk, carry0, ml_in)
    hc = rmsnorm(from_chunks(hc, t), p['ml_gain'].reshape(ML_HEADS, ML_HD)).reshape(b, t, ML_WIDTH)
    y_ml = jax.nn.sigmoid(mo.astype(f32)) * hc

    fgate = p['lb'] + (1.0 - p['lb']) * jax.nn.sigmoid(hf.astype(f32))
    hq_ = jax.nn.silu(hq.astype(f32)).reshape(b, t, HG_HEADS, HG_HD)
    hk = (1.0 - fgate).reshape(b, t, HG_HEADS, HG_HD)
    hlog = jnp.log(fgate).reshape(b, t, HG_HEADS, HG_HD)
    hv = hi.reshape(b, t, HG_HEADS, HG_HD)
    hg_in = (to_chunks(hq_, 0.0), to_chunks(hk, 0.0), to_chunks(hv, 0.0), to_chunks(hlog, 0.0))
    s_fin, o = lax.scan(gla_chunk, hg_st.astype(f32), hg_in)
    o = rmsnorm(from_chunks(o, t), p['hg_gain'].reshape(HG_HEADS, HG_HD)).reshape(b, t, HG_WIDTH)
    y_hg = o * jax.nn.silu(hgate.astype(f32))

    mix = jnp.concatenate([y_ssd, y_ml, y_hg], axis=-1).astype(x.dtype)
    x = x + jnp.einsum('btc,cd->btd', mix, p['w_out'])
    h2 = rmsnorm(x, p['norm_ffn'])
    gu = jnp.einsum('btd,df->btf', h2, p['w_ffn_in'])
    g, u = jnp.split(gu, 2, axis=-1)
    x = x + jnp.einsum('btf,fd->btd', jax.nn.silu(g) * u, p['w_ffn_out'])
    new_state = (new_conv, h_fin.reshape(b, SSD_HEADS, SSD_HEADDIM, SSD_STATE), mc, mn, mm, s_fin)
    return x, tuple(s.astype(x.dtype) for s in new_state)


def run_trunk(x, states, params, norm_final):
    new = [[] for _ in states]
    for l in range(DEPTH):
        p = {name: arr[l] for name, arr in params.items()}
        x, st = trunk_layer(x, tuple(s[l] for s in states), p)
        for lst, s in zip(new, st):
            lst.append(s)
    return rmsnorm(x, norm_final), tuple(jnp.stack(lst) for lst in new)


def setup_inputs(seed: int = 0) -> dict:
    key = jax.random.key(seed)
    ks = jax.random.split(key, 32)
    f32 = jnp.float32

    def nrm(k, shape, scale):
        return jax.random.normal(k, shape, f32) * scale

    def gain(k, shape, s=0.02):
        return 1.0 + s * jax.random.normal(k, shape, f32)

    dt0 = jnp.exp(jax.random.uniform(ks[12], (DEPTH, SSD_HEADS), f32, math.log(1e-3), math.log(1e-1)))
    return {
        'x_prompt': nrm(ks[0], (BATCH, SEQ, D_MODEL), 1.0),
        'x_sample': nrm(ks[1], (DEC_BATCH, DEC_SEQ, D_MODEL), 1.0),
        'state_conv': nrm(ks[2], (DEPTH, DEC_BATCH, CONV_W - 1, CONV_CH), 1.0),
        'state_ssd': nrm(ks[3], (DEPTH, DEC_BATCH, SSD_HEADS, SSD_HEADDIM, SSD_STATE), 0.1),
        'state_mlstm_c': nrm(ks[4], (DEPTH, DEC_BATCH, ML_HEADS, ML_HD, ML_HD), 0.1),
        'state_mlstm_n': nrm(ks[5], (DEPTH, DEC_BATCH, ML_HEADS, ML_HD), 0.1),
        'state_mlstm_m': nrm(ks[6], (DEPTH, DEC_BATCH, ML_HEADS), 1.0),
        'state_hgrn': nrm(ks[7], (DEPTH, DEC_BATCH, HG_HEADS, HG_HD, HG_HD), 0.5),
        'norm_mix': gain(ks[8], (DEPTH, D_MODEL)),
        'w_in': nrm(ks[9], (DEPTH, D_MODEL, IN_COLS), D_MODEL ** -0.5),
        'conv_w': nrm(ks[10], (DEPTH, CONV_W, CONV_CH), CONV_W ** -0.5),
        'conv_b': nrm(ks[11], (DEPTH, CONV_CH), 0.01),
        'dt_bias': dt0 + jnp.log(-jnp.expm1(-dt0)),
        'a_log': jnp.log(jax.random.uniform(ks[13], (DEPTH, SSD_HEADS), f32, 1.0, 16.0)),
        'd_skip': gain(ks[14], (DEPTH, SSD_HEADS), 0.1),
        'ssd_gain': gain(ks[15], (DEPTH, SSD_WIDTH)),
        'ml_bi': nrm(ks[16], (DEPTH, ML_HEADS), 0.1),
        'ml_bf': jnp.linspace(3.0, 6.0, ML_HEADS, dtype=f32)[None] + nrm(ks[17], (DEPTH, ML_HEADS), 0.1),
        'ml_gain': gain(ks[18], (DEPTH, ML_WIDTH)),
        'hg_lb': nrm(ks[19], (DEPTH, HG_WIDTH), 0.5),
        'hg_gain': gain(ks[20], (DEPTH, HG_WIDTH)),
        'w_out': nrm(ks[21], (DEPTH, D_MIX, D_MODEL), D_MIX ** -0.5),
        'norm_ffn': gain(ks[22], (DEPTH, D_MODEL)),
        'w_ffn_in': nrm(ks[23], (DEPTH, D_MODEL, 2 * D_FF), D_MODEL ** -0.5),
        'w_ffn_out': nrm(ks[24], (DEPTH, D_FF, D_MODEL), D_FF ** -0.5),
        'norm_final': gain(ks[25], (D_MODEL,)),
    }


def reference(x_prompt, x_sample, state_conv, state_ssd, state_mlstm_c, state_mlstm_n, state_mlstm_m, state_hgrn,
              norm_mix, w_in, conv_w, conv_b, dt_bias, a_log, d_skip, ssd_gain, ml_bi, ml_bf, ml_gain,
              hg_lb, hg_gain, w_out, norm_ffn, w_ffn_in, w_ffn_out, norm_final):
    lb_soft = jax.nn.softmax(hg_lb.astype(jnp.float32), axis=0)
    hg_lower = jnp.cumsum(lb_soft, axis=0) - lb_soft[0]
    params = {'norm_mix': norm_mix, 'w_in': w_in, 'conv_w': conv_w, 'conv_b': conv_b, 'dt_bias': dt_bias,
              'a_log': a_log, 'd_skip': d_skip, 'ssd_gain': ssd_gain, 'ml_bi': ml_bi, 'ml_bf': ml_bf,
              'ml_gain': ml_gain, 'lb': hg_lower, 'hg_gain': hg_gain, 'w_out': w_out, 'norm_ffn': norm_ffn,
              'w_ffn_in': w_ffn_in, 'w_ffn_out': w_ffn_out}
    bp = x_prompt.shape[0]
    f32 = jnp.float32
    zero_state = (jnp.zeros((DEPTH, bp, CONV_W - 1, CONV_CH), x_prompt.dtype),
                  jnp.zeros((DEPTH, bp, SSD_HEADS, SSD_HEADDIM, SSD_STATE), f32),
                  jnp.zeros((DEPTH, bp, ML_HEADS, ML_HD, ML_HD), f32),
                  jnp.zeros((DEPTH, bp, ML_HEADS, ML_HD), f32),
                  jnp.zeros((DEPTH, bp, ML_HEADS), f32),
                  jnp.zeros((DEPTH, bp, HG_HEADS, HG_HD, HG_HD), f32))
    y_prompt, (p_conv, p_ssd, p_mc, p_mn, p_mm, p_hg) = run_trunk(x_prompt, zero_state, params, norm_final)
    sample_state = (state_conv, state_ssd, state_mlstm_c, state_mlstm_n, state_mlstm_m, state_hgrn)
    y_sample, (s_conv, s_ssd, s_mc, s_mn, s_mm, s_hg) = run_trunk(x_sample, sample_state, params, norm_final)
    return (y_prompt, y_sample, p_conv, p_ssd, p_mc, p_mn, p_mm, p_hg, s_conv, s_ssd, s_mc, s_mn, s_mm, s_hg)
```

```python
import contextlib
import numpy as np
import concourse.bass as bass
import concourse.mybir as mybir
from concourse.bass_utils import run_bass_kernel_spmd

F32 = mybir.dt.float32
BF16 = mybir.dt.bfloat16
AF = mybir.ActivationFunctionType
ALU = mybir.AluOpType
AX = mybir.AxisListType

D = 1024
DMIX = 2048
INC = 6680
DFF = 2816
EPS = 1e-6
NEG = -30000.0
ENGS = ("pe", "act", "dve", "pool", "sp")


class Op:
    __slots__ = ("eng", "idx", "fn", "dma", "deps", "sig", "semref")

    def __init__(self, eng, idx, fn, dma):
        self.eng = eng
        self.idx = idx
        self.fn = fn
        self.dma = dma
        self.deps = set()
        self.sig = False
        self.semref = None


class _Rec:
    def __init__(self):
        self.calls = []

    def __getattr__(self, name):
        def m(*a, **k):
            self.calls.append((name, a, k))
            return self
        return m


class Sched:
    def __init__(self, nc, n_dma_sems=20):
        self.nc = nc
        self.ops = {e: [] for e in ENGS}
        self.res = {}
        self.n_dma_sems = n_dma_sems
        self.cap = None

    def add(self, eng, fn, reads=(), writes=(), dma=False):
        rec = _Rec()
        fn(rec)
        if self.cap is not None:
            self.cap.append((eng, rec.calls, tuple(reads), tuple(writes), dma))
            return None
        return self.commit(eng, rec.calls, reads, writes, dma)

    def capture(self, f, *a):
        assert self.cap is None
        self.cap = []
        f(*a)
        out = self.cap
        self.cap = None
        return out

    def commit(self, eng, calls, reads, writes, dma):
        op = Op(eng, len(self.ops[eng]), calls, dma)
        deps = set()
        for r in reads:
            st = self.res.setdefault(r, [None, []])
            if st[0] is not None:
                deps.add(st[0])
            st[1].append(op)
        for w in writes:
            st = self.res.setdefault(w, [None, []])
            if st[0] is not None:
                deps.add(st[0])
            for rd in st[1]:
                deps.add(rd)
            st[0] = op
            st[1] = []
        deps.discard(op)
        op.deps = deps
        for d in deps:
            d.sig = True
        self.ops[eng].append(op)
        return op

    def dma(self, eng, out, in_, reads=(), writes=()):
        return self.add(eng, lambda e: e.dma_start(out=out, in_=in_), reads, writes, dma=True)

    def emit(self):
        nc = self.nc
        with contextlib.ExitStack() as st:
            csem = {e: st.enter_context(nc.semaphore("c_" + e)) for e in ENGS}
            dsem = {e: [st.enter_context(nc.semaphore("d_%s%d" % (e, i))) for i in range(self.n_dma_sems)]
                    for e in ("sp", "act", "pool")}
            for e in ENGS:
                cnt = 0
                dcnt = [0] * self.n_dma_sems
                k = 0
                for op in self.ops[e]:
                    if op.dma:
                        j = k % self.n_dma_sems
                        k += 1
                        prev = dcnt[j]
                        dcnt[j] += 16
                        op.semref = (dsem[e][j], dcnt[j], prev)
                    elif op.sig:
                        cnt += 1
                        op.semref = (csem[e], cnt, None)
            block = st.enter_context(nc.Block())

            def run(e, eng):
                waited = {}
                for op in self.ops[e]:
                    need = {}
                    for d in op.deps:
                        if d.semref is None:
                            continue
                        sem, val = d.semref[0], d.semref[1]
                        key = id(sem)
                        if need.get(key, (None, 0))[1] < val:
                            need[key] = (sem, val)
                    if op.dma:
                        sem, val, prev = op.semref
                        if prev > 0:
                            key = id(sem)
                            if need.get(key, (None, 0))[1] < prev:
                                need[key] = (sem, prev)
                    for key, (sem, val) in need.items():
                        if waited.get(key, 0) < val:
                            eng.wait_ge(sem, val)
                            waited[key] = val
                    ins = None
                    for name, a, k in op.fn:
                        ins = getattr(eng, name)(*a, **k)
                    if op.dma:
                        ins.then_inc(op.semref[0], 16)
                    elif op.sig:
                        ins.then_inc(op.semref[0], 1)
                if e == "sp":
                    for e2 in ("sp", "act", "pool"):
                        last = {}
                        for op in self.ops[e2]:
                            if op.dma:
                                last[id(op.semref[0])] = (op.semref[0], op.semref[1])
                        for sem, val in last.values():
                            eng.wait_ge(sem, val)

            block.tensor(lambda eng: run("pe", eng))
            block.scalar(lambda eng: run("act", eng))
            block.vector(lambda eng: run("dve", eng))
            block.gpsimd(lambda eng: run("pool", eng))
            block.sync(lambda eng: run("sp", eng))


class Arena:
    def __init__(self, nc, st, name, nbytes):
        self.words = nbytes // 4
        self.t = st.enter_context(nc.sbuf_tensor(name, [128, self.words], F32))
        self.off = 0
        self.name = name

    def alloc(self, shape, dt):
        n = int(np.prod(shape))
        words = n if dt == F32 else (n + 1) // 2
        assert self.off + words <= self.words, (self.name, self.off, words, self.words)
        v = self.t[:, self.off:self.off + words]
        self.off += words
        if dt == BF16:
            v = v.bitcast(BF16)[:, 0:n]
        if len(shape) == 2:
            v = v.rearrange("p (a b) -> p a b", a=shape[0])
        elif len(shape) == 3:
            v = v.rearrange("p (a b c) -> p a b c", a=shape[0], b=shape[1])
        return v


def build(NPS, TP, NSS, dbg=False, LAG=8):
    TS = 32
    TT = min(512, TP)
    nc = bass.Bass("TRN2", target_bir_lowering=False)
    din = lambda n, s: nc.dram_tensor(n, list(s), F32, kind="ExternalInput").ap()
    dout = lambda n, s: nc.dram_tensor(n, list(s), F32, kind="ExternalOutput").ap()
    x_prompt = din("x_prompt", [NPS, TP, D])
    x_sample = din("x_sample", [NSS, TS, D])
    state_conv = din("state_conv", [2, NSS, 3, 1536])
    state_ssd = din("state_ssd", [2, NSS, 16, 64, 128])
    state_mc = din("state_mlstm_c", [2, NSS, 4, 128, 128])
    state_mn = din("state_mlstm_n", [2, NSS, 4, 128])
    state_mm = din("state_mlstm_m", [2, NSS, 4])
    state_hg = din("state_hgrn", [2, NSS, 4, 128, 128])
    norm_mix = din("norm_mix", [2, D])
    w_in = din("w_in", [2, D, INC])
    conv_w = din("conv_w", [2, 4, 1536])
    conv_b = din("conv_b", [2, 1536])
    dt_bias = din("dt_bias", [2, 16])
    a_log = din("a_log", [2, 16])
    d_skip = din("d_skip", [2, 16])
    ssd_gain = din("ssd_gain", [2, 1024])
    ml_bi = din("ml_bi", [2, 4])
    ml_bf = din("ml_bf", [2, 4])
    ml_gain = din("ml_gain", [2, 512])
    hg_lb = din("hg_lb", [2, 512])
    hg_gain = din("hg_gain", [2, 512])
    w_out = din("w_out", [2, DMIX, D])
    norm_ffn = din("norm_ffn", [2, D])
    w_ffn_in = din("w_ffn_in", [2, D, 2 * DFF])
    w_ffn_out = din("w_ffn_out", [2, DFF, D])
    norm_final = din("norm_final", [D])

    w_in_b = nc.dram_tensor("w_in_b", [2, D, INC], BF16, kind="Internal").ap()
    w_out_b = nc.dram_tensor("w_out_b", [2, DMIX, D], BF16, kind="Internal").ap()
    w_ffn_in_b = nc.dram_tensor("w_ffn_in_b", [2, D, 2 * DFF], BF16, kind="Internal").ap()
    w_ffn_out_b = nc.dram_tensor("w_ffn_out_b", [2, DFF, D], BF16, kind="Internal").ap()
    O = {}
    for pre, nb_ in (("p", NPS), ("s", NSS)):
        O[pre + "_y"] = dout(pre + "_y", [nb_, TP if pre == "p" else TS, D])
        O[pre + "_conv"] = dout(pre + "_conv", [2, nb_, 3, 1536])
        O[pre + "_ssd"] = dout(pre + "_ssd", [2, nb_, 16, 64, 128])
        O[pre + "_mc"] = dout(pre + "_mc", [2, nb_, 4, 128, 128])
        O[pre + "_mn"] = dout(pre + "_mn", [2, nb_, 4, 128])
        O[pre + "_mm"] = dout(pre + "_mm", [2, nb_, 4])
        O[pre + "_hg"] = dout(pre + "_hg", [2, nb_, 4, 128, 128])
    DBG = {}

    st = contextlib.ExitStack()
    with st:
        S = Sched(nc)
        PS = [st.enter_context(nc.psum_tensor("ps%d" % i, [128, 512], F32)) for i in range(8)]
        psk = [0, 0, 0]
        pspool = [None]

        def nb():
            p = pspool[0]
            if p is None:
                i = psk[2] % 8
                psk[2] += 1
                return i
            i = 4 * p + (psk[p] % 4)
            psk[p] += 1
            return i

        A = Arena(nc, st, "arenaA", 106 * 1024)
        RB = Arena(nc, st, "arenaB", 101 * 1024)

        ident32 = A.alloc((128,), F32)
        identb = A.alloc((128,), BF16)
        onesb = A.alloc((128,), BF16)
        mask01b = A.alloc((128,), BF16)
        negmask4 = A.alloc((4, 128), BF16)
        ones32 = A.alloc((128,), F32)
        V = lambda eng, f, r=(), w=(): S.add(eng, f, r, w)
        V("pool", lambda e: e.memset(ident32, 0.0), (), ["ident32"])
        V("pool", lambda e: e.affine_select(out=ident32, in_=ident32, pattern=[[-1, 128]], compare_op=ALU.not_equal,
                                            fill=1.0, base=0, channel_multiplier=1), ["ident32"], ["ident32"])
        V("dve", lambda e: e.tensor_copy(identb, ident32), ["ident32"], ["identb"])
        V("dve", lambda e: e.memset(onesb, 1.0), (), ["onesb"])
        V("dve", lambda e: e.memset(ones32, 1.0), (), ["ones32"])
        V("pool", lambda e: e.memset(ones32, 1.0), ["ones32"], ["ones32"])
        m01f = A.alloc((128,), F32)
        V("pool", lambda e: e.memset(m01f, 1.0), (), ["m01f"])
        V("pool", lambda e: e.affine_select(out=m01f, in_=m01f, pattern=[[1, 128]], compare_op=ALU.is_ge,
                                            fill=0.0, base=0, channel_multiplier=-1), ["m01f"], ["m01f"])
        V("dve", lambda e: e.tensor_copy(mask01b, m01f), ["m01f"], ["mask01b"])
        for j in range(4):
            V("dve", lambda e, j=j: e.tensor_scalar(negmask4[:, j, :], m01f, 1.0, -NEG, ALU.subtract, ALU.mult),
              ["m01f"], ["negmask4"])
        def sel(nr, hh, L):
            return ident32[0:nr, hh:hh + 1].broadcast_to([nr, L])

        xT = A.alloc((8, TT), F32)
        hT = A.alloc((8, TT), BF16)
        mixT = A.alloc((16, TT), BF16)
        rstd = A.alloc((TT,), F32)
        NSLOT = 2
        wring = [A.alloc((8192,), BF16) for _ in range(NSLOT)]
        wk = [0]
        pv = [A.alloc((96,), F32) for _ in range(2)]
        pvf = A.alloc((8,), F32)
        rowbuf = A.alloc((128,), F32)
        hs = {}
        for l in range(2):
            hs[l] = dict(dtb=A.alloc((1,), F32), A=A.alloc((1,), F32), bi=A.alloc((1,), F32), nbf=A.alloc((1,), F32),
                         Dbc=A.alloc((16,), F32), lb=A.alloc((4,), F32), oml=A.alloc((4,), F32))
        stt = {}
        for l in range(2):
            stt[l] = dict(h32=A.alloc((1024,), F32), C32=A.alloc((4, 129), F32), S32=A.alloc((4, 128), F32),
                          m=A.alloc((1,), F32), conv=A.alloc((12, 3), F32))
        hTb = A.alloc((1024,), BF16)
        Cb = A.alloc((4, 129), BF16)
        Sb = A.alloc((4, 128), BF16)
        fence = A.alloc((4,), F32)
        w32a = A.alloc((1024,), F32)
        h0buf = A.alloc((8,), F32)
        s0buf = A.alloc((4,), F32)

        def load_rows(l):
            segs = [(norm_mix[l], 0, 8), (conv_w[l].rearrange("w c -> (w c)"), 8, 48), (conv_b[l], 56, 12), (ssd_gain[l], 68, 8),
                    (ml_gain[l], 76, 4), (hg_lb[l], 80, 4), (hg_gain[l], 84, 4), (norm_ffn[l], 88, 8)]
            for src, r0, n in segs:
                S.dma("sp", rowbuf[r0:r0 + n, :], src.rearrange("(a b) -> a b", b=128), (), ["rowbuf"])
            b = nb()
            V("pe", lambda e: e.transpose(PS[b][:, 0:96], rowbuf[0:96, :], ident32[0:96, 0:96]), ["rowbuf", "ident32"], ["P%d" % b])
            V("dve", lambda e: e.tensor_copy(pv[l], PS[b][:, 0:96]), ["P%d" % b], ["pv%d" % l])

        for l in range(2):
            load_rows(l)
            h = hs[l]
            S.dma("sp", h["dtb"][0:16, :], dt_bias[l].rearrange("(a b) -> a b", b=1), (), ["hs%d" % l])
            S.dma("sp", h["A"][0:16, :], a_log[l].rearrange("(a b) -> a b", b=1), (), ["hs%d" % l])
            S.dma("sp", h["bi"][0:4, :], ml_bi[l].rearrange("(a b) -> a b", b=1), (), ["hs%d" % l])
            S.dma("sp", h["nbf"][0:4, :], ml_bf[l].rearrange("(a b) -> a b", b=1), (), ["hs%d" % l])
            S.dma("sp", h["Dbc"], d_skip[l].rearrange("(a b) -> a b", a=1).broadcast_to([128, 16]), (), ["hs%d" % l])
            V("act", lambda e, h=h: e.activation(h["A"][0:16, :], h["A"][0:16, :], AF.Exp), ["hs%d" % l], ["hs%d" % l])
            V("dve", lambda e, h=h: e.tensor_scalar(h["A"][0:16, :], h["A"][0:16, :], -1.0, None, ALU.mult), ["hs%d" % l], ["hs%d" % l])
            V("dve", lambda e, h=h: e.tensor_scalar(h["nbf"][0:4, :], h["nbf"][0:4, :], -1.0, None, ALU.mult), ["hs%d" % l], ["hs%d" % l])
        S.dma("sp", rowbuf[0:8, :], norm_final.rearrange("(a b) -> a b", b=128), (), ["rowbuf"])
        b = nb()
        V("pe", lambda e: e.transpose(PS[b][:, 0:8], rowbuf[0:8, :], ident32[0:8, 0:8]), ["rowbuf", "ident32"], ["P%d" % b])
        V("dve", lambda e: e.tensor_copy(pvf, PS[b][:, 0:8]), ["P%d" % b], ["pvf"])
        V("dve", lambda e: e.memset(hs[0]["lb"], 0.0), (), ["hs0"])
        V("dve", lambda e: e.memset(hs[0]["oml"], 1.0), (), ["hs0"])
        V("dve", lambda e: e.tensor_tensor(hs[1]["lb"], pv[1][:, 80:84], pv[0][:, 80:84], ALU.subtract), ["pv0", "pv1"], ["hs1"])
        V("act", lambda e: e.activation(hs[1]["lb"], hs[1]["lb"], AF.Sigmoid), ["hs1"], ["hs1"])
        V("dve", lambda e: e.tensor_scalar(hs[1]["oml"], hs[1]["lb"], -1.0, 1.0, ALU.mult, ALU.add), ["hs1"], ["hs1"])

        converted = set()

        def wload(src3, src3f, a, c, wbkey):
            i = wk[0] % NSLOT
            wk[0] += 1
            v = wring[i][:, 0:a * c].rearrange("p (a c) -> p a c", a=a)
            if wbkey not in converted:
                converted.add(wbkey)
                S.dma("pool", v, src3f, (), ["w%d" % i])
                S.dma("sp", src3, v, ["w%d" % i], [wbkey])
            else:
                S.dma("sp", v, src3, [wbkey], ["w%d" % i])
            return v, "w%d" % i

        NCH = 4
        xp = RB.alloc((TT + 16,), F32)
        cacc = RB.alloc((TT,), F32)
        io_ssd = RB.alloc((8, 128), F32)
        io_conv = RB.alloc((1536,), F32)
        io_n = RB.alloc((128,), F32)
        convin = RB.alloc((4, 12, 3), F32)
        convout = RB.alloc((4, 12, 3), F32)
        CS = [dict(), dict()]
        c_sc1 = RB.alloc((1024,), F32)
        q0row = RB.alloc((512,), F32)
        for par in range(2):
            CS[par]["c_small"] = RB.alloc((256,), F32)
            CS[par]["c_sc"] = c_sc1
        rb0 = RB.off
        xbcT = RB.alloc((12, TT), BF16)
        zs = RB.alloc((NCH, 1024), BF16)
        rows = {}
        rows["dt"] = RB.alloc((TT,), F32)
        rows["a"] = RB.alloc((TT,), F32)
        for par in range(2):
            c = CS[par]
            c["rows"] = {}
            for k in ("acum", "nacum", "decend"):
                c["rows"][k] = RB.alloc((128,), F32)
            c["c_xsTM"] = RB.alloc((1024,), BF16)
            c["c_xdt"] = RB.alloc((1024,), BF16)
            c["c_xdec"] = RB.alloc((1024,), BF16)
            c["c_BTM"] = RB.alloc((256,), BF16)
            c["c_cbT"] = RB.alloc((2, 128), F32)
            c["c_dT"] = RB.alloc((16, 128), BF16)
            c["c_MT"] = RB.alloc((16, 128), BF16)
            c["c_t2"] = RB.alloc((1024,), BF16)
            c["c_y"] = RB.alloc((1024,), F32)
        rb_max = RB.off
        RB.off = rb0
        qT = RB.alloc((4, TT), BF16)
        kT = RB.alloc((4, TT), BF16)
        kTM = RB.alloc((NCH, 512), BF16)
        vTM = RB.alloc((NCH, 4 * 129), BF16)
        oTM = RB.alloc((NCH, 512), BF16)
        rows["ig"] = RB.alloc((TT,), F32)
        rows["lf"] = RB.alloc((TT,), F32)
        c = CS[0]
        for k in ("fcum", "u", "cmm", "ncmm", "winter", "emi", "wkr", "wk2"):
            c["rows"][k] = RB.alloc((128,), F32)
        c["c_wT"] = RB.alloc((4, 128), F32)
        c["c_SwT"] = RB.alloc((4, 128), BF16)
        c["c_kw"] = RB.alloc((4, 128), BF16)
        c["c_int"] = RB.alloc((4, 129), F32)
        c["c_hm"] = RB.alloc((4, 128), F32)
        gqT = RB.alloc((4, TT), BF16)
        gkT = RB.alloc((4, TT), BF16)
        gT = RB.alloc((4, TT), F32)
        gviTM = RB.alloc((NCH, 512), BF16)
        ggTM = RB.alloc((NCH, 512), BF16)
        c = CS[1]
        c["c_gcp"] = RB.alloc((4, 129), F32)
        c["c_e"] = RB.alloc((4, 128), F32)
        c["c_qref"] = RB.alloc((4, 128), BF16)
        c["c_qdec"] = RB.alloc((4, 128), BF16)
        c["c_kref"] = [RB.alloc((4, 128), BF16) for _ in range(4)]
        c["c_kend"] = RB.alloc((4, 128), BF16)
        c["c_kendTM"] = RB.alloc((4, 128), BF16)
        c["c_ATb"] = RB.alloc((4, 128), BF16)
        c["c_o"] = RB.alloc((4, 128), F32)
        rb_max = max(rb_max, RB.off)
        RB.off = rb0
        ff0 = RB.off
        actT = RB.alloc((22, TT), BF16)
        sqs = RB.alloc((8, TT), BF16)
        ostg = [RB.t[:, ff0:ff0 + 1024], RB.t[:, ff0 + 1024:ff0 + 2048]]
        xin = [CS[0]["c_y"], CS[1]["c_y"]]
        rb_max = max(rb_max, RB.off)
        RB.off = rb_max
        CK = ["c_xsTM", "c_xdt", "c_xdec", "c_BTM", "c_cbT", "c_dT", "c_MT", "c_t2", "c_y", "rows",
              "c_wT", "c_SwT", "c_kw", "c_int", "c_hm", "c_gcp", "c_e", "c_qref", "c_qdec", "kref0", "kref1", "kref2", "kref3",
              "c_kend", "c_kendTM", "c_ATb", "c_o"]
        ALLK = ["xbcT", "zs", "rowsT", "qT", "kT", "kTM", "vTM", "oTM", "gqT", "gkT", "gT", "gviTM", "ggTM", "actT", "sqsF", "ostg0", "ostg1", "xbcT_b", "sqsF_b"] + \
               ["p%d_%s" % (par, k) for par in range(2) for k in CK]

        def fence_all():
            V("dve", lambda e: e.memset(fence, 0.0), (), ALLK + ["fence"])

        def dbg_out(name, ap_sb, shape, reads):
            if not dbg:
                return
            d = dout("dbg_" + name, shape)
            DBG[name] = shape
            S.dma("pool", d, ap_sb, reads, ())

        def rmsnorm_fm(n, gain_cols, dst, sq_buf, sqkey):
            V("act", lambda e: e.activation(sq_buf[:, 0:4, 0:n], xT[:, 0:4, 0:n], AF.Square), ["xT"], [sqkey])
            V("dve", lambda e: e.tensor_tensor(sq_buf[:, 4:8, 0:n], xT[:, 4:8, 0:n], xT[:, 4:8, 0:n], ALU.mult), ["xT"], [sqkey + "_b"])
            b = nb()

            def f(e):
                for dc in range(8):
                    ins = e.matmul(PS[b][:, 0:n], onesb, sq_buf[:, dc, 0:n], start=(dc == 0), stop=(dc == 7))
                return ins
            V("pe", f, [sqkey, sqkey + "_b", "onesb"], ["P%d" % b])
            V("act", lambda e: e.activation(rstd[:, 0:n], PS[b][:, 0:n], AF.Ln, bias=EPS, scale=1.0 / D), ["P%d" % b], ["rstd"])
            V("act", lambda e: e.activation(rstd[:, 0:n], rstd[:, 0:n], AF.Exp, scale=-0.5), ["rstd"], ["rstd"])
            for dc in range(8):
                V("dve", lambda e, dc=dc: e.scalar_tensor_tensor(dst[:, dc, 0:n], xT[:, dc, 0:n], gain_cols[:, dc:dc + 1], rstd[:, 0:n],
                                                                  ALU.mult, ALU.mult), ["xT", "rstd"], ["hT"])

        def fm_proj(wv, wkey, c0, M, n, src, srckey, ndc):
            b = nb()

            def f(e):
                for dc in range(ndc):
                    ins = e.matmul(PS[b][0:M, 0:n], wv[:, dc, c0:c0 + M], src[:, dc, 0:n], start=(dc == 0), stop=(dc == ndc - 1))
                return ins
            V("pe", f, [wkey, srckey], ["P%d" % b])
            return b

        def tm_proj(wv, wkey, c0, ncol, t0, L):
            b = nb()

            def f(e):
                for dc in range(8):
                    ins = e.matmul(PS[b][0:L, 0:ncol], hT[:, dc, t0:t0 + L], wv[:, dc, c0:c0 + ncol], start=(dc == 0), stop=(dc == 7))
                return ins
            V("pe", f, [wkey, "hT"], ["P%d" % b])
            return b

        def bcast_rows(src_rows_ap, nrow, key_r, c_small, kp):
            b = nb()
            V("dve", lambda e: e.tensor_scalar(c_small[0:nrow, 108:108 + nrow], ident32[0:nrow, 0:nrow], src_rows_ap, None, ALU.mult),
              [key_r, "ident32"], [kp + "c_small_d"])
            V("pe", lambda e: e.matmul(PS[b][:, 0:nrow], ones32[0:nrow, :], c_small[0:nrow, 108:108 + nrow], start=True, stop=True),
              [kp + "c_small_d", "ones32"], ["P%d" % b])
            return b

        def one_chunk(l, tile, ci, ch, which, fn):
            pspool[0] = {"ssd": ci % 2, "ml": 0, "hg": 1}[which]
            try:
                one_chunk_(l, tile, ci, ch, which, fn)
            finally:
                pspool[0] = None

        def one_chunk_(l, tile, ci, ch, which, fn):
            if ch[3]:
                tile["state_init"](l, ch, which)
            fn(l, tile, ci, ch)
            if ch[4]:
                tile["state_store"](l, ch, which)

        def pe_segs(sl):
            return [[spec] for spec in sl]

        def interleave2(la, lb):
            A_, B_ = pe_segs(la), pe_segs(lb)
            i = j = 0
            while i < len(A_) or j < len(B_):
                if i < len(A_):
                    for spec in A_[i]:
                        S.commit(*spec)
                    i += 1
                want = len(B_) if i >= len(A_) else (i * len(B_)) // max(1, len(A_))
                while j < want:
                    for spec in B_[j]:
                        S.commit(*spec)
                    j += 1

        def pipelined(lists, l):
            skeys = set(["h32_%d" % l, "hTb", "C32_%d" % l, "Cb", "m%d" % l, "S32_%d" % l, "Sb", "io_ssd", "io_n", "io_conv"])

            def segs(sl):
                return [[spec] for spec in sl]
            parts = []
            for sl in lists:
                idx = len(sl)
                for i, spec in enumerate(sl):
                    if skeys.intersection(spec[2]) or skeys.intersection(spec[3]):
                        idx = i
                        break
                if LAG <= 0:
                    idx = 0
                parts.append((segs(sl[:idx]), segs(sl[idx:])))
            for seg in parts[0][0]:
                for spec in seg:
                    S.commit(*spec)
            for c in range(len(parts)):
                Bc = parts[c][1]
                An = parts[c + 1][0] if c + 1 < len(parts) else []
                i = j = 0
                while i < len(Bc) or j < len(An):
                    if i < len(Bc):
                        for spec in Bc[i]:
                            S.commit(*spec)
                        i += 1
                    want = len(An) if i >= len(Bc) else (i * len(An)) // max(1, len(Bc))
                    while j < want:
                        for spec in An[j]:
                            S.commit(*spec)
                        j += 1

        def ml_s0_prep(l):
            V("dve", lambda e: e.tensor_tensor(h0buf, xT[:, :, 0], pv[l][:, 0:8], ALU.mult), ["xT", "pv%d" % l], ["h0buf"])
            V("dve", lambda e: e.tensor_scalar(h0buf, h0buf, rstd[:, 0:1], None, ALU.mult), ["h0buf", "rstd"], ["h0buf"])

        def ml_s0_exact(l):
            bq, bk = nb(), nb()
            wbufs = [(w32a, ["w32a"]), (c_sc1, ["p0_c_sc", "p1_c_sc2"])]
            for dc in range(8):
                wb_, wk_ = wbufs[dc % 2]
                S.dma("sp", wb_, w_in[l, dc * 128:(dc + 1) * 128, 2576:3600], (), wk_)

                def f(e, dc=dc, wb_=wb_):
                    e.matmul(PS[bq][0:1, 0:512], h0buf[:, dc:dc + 1], wb_[:, 0:512], start=(dc == 0), stop=(dc == 7))
                    return e.matmul(PS[bk][0:1, 0:512], h0buf[:, dc:dc + 1], wb_[:, 512:1024], start=(dc == 0), stop=(dc == 7))
                V("pe", f, wk_ + ["h0buf"], ["P%d" % bq, "P%d" % bk])
            V("act", lambda e: e.copy(q0row[0:1, :], PS[bq][0:1, 0:512]), ["P%d" % bq], ["q0row"])
            V("dve", lambda e: e.tensor_tensor(q0row[0:1, :], q0row[0:1, :], PS[bk][0:1, 0:512], ALU.mult), ["q0row", "P%d" % bk], ["q0row"])
            V("dve", lambda e: e.tensor_reduce(s0buf[0:1, :], q0row[0:1, :].rearrange("p (h c) -> p h c", h=4), AX.X, ALU.add), ["q0row"], ["s0buf"])
            V("dve", lambda e: e.tensor_scalar(s0buf[0:1, :], s0buf[0:1, :], 128.0 ** -0.5, None, ALU.mult), ["s0buf"], ["s0buf"])

        def layer(l, tile):
            n = tile["n"]
            segs = tile["segs"]
            chunks = tile["chunks"]
            nseg = len(segs)
            ls = segs[0][1]
            P = pv[l]
            H = hs[l]
            ST = stt[l]
            w_in_v = w_in_b[l].rearrange("(dc p) c -> p dc c", p=128)
            w_in_f = w_in[l].rearrange("(dc p) c -> p dc c", p=128)
            fence_all()
            rmsnorm_fm(n, P[:, 0:8], hT, xbcT[:, 0:8, :], "xbcT")
            seq_start = (tile["kind"] == "p" and chunks[0][3])
            if seq_start:
                ml_s0_prep(l)
            wvz, wkeyz = wload(w_in_v[:, :, 0:1024], w_in_f[:, :, 0:1024], 8, 1024, "wb_in%d_0" % l)
            zjobs = [(ci, ch, hf) for ci, ch in enumerate(chunks) for hf in range(2)]

            def zjob(ci, ch, hf):
                (c0, L, seq, first, last) = ch
                b = tm_proj(wvz, wkeyz, hf * 512, 512, c0, L)
                V("act", lambda e: e.activation(zs[0:L, ci, hf * 512:(hf + 1) * 512], PS[b][0:L, :], AF.Silu), ["P%d" % b], ["zs"])

            def conv_chunk(b, cc):
                xp3 = xp[:, 0:nseg * (ls + 3)].rearrange("p (s t) -> p s t", s=nseg)
                if tile["kind"] == "p":
                    V("dve", lambda e: e.tensor_copy(xp3[:, 0, 0:3], ST["conv"][:, cc, :]), ["conv%d" % l], ["xp"])
                else:
                    V("dve", lambda e: e.tensor_copy(xp3[:, :, 0:3], convin[:, 0:nseg, cc, :]), ["convin"], ["xp"])
                V("act", lambda e: e.copy(xp3[:, :, 3:3 + ls], PS[b][:, 0:n].rearrange("p (s t) -> p s t", s=nseg)), ["P%d" % b], ["xp"])
                if tile["kind"] == "p":
                    V("dve", lambda e: e.tensor_copy(ST["conv"][:, cc, :], xp3[:, 0, ls:ls + 3]), ["xp"], ["conv%d" % l])
                else:
                    V("dve", lambda e: e.tensor_copy(convout[:, 0:nseg, cc, :], xp3[:, :, ls:ls + 3]), ["xp"], ["convout"])
                acc3 = cacc[:, 0:n].rearrange("p (s t) -> p s t", s=nseg)
                V("act", lambda e: e.activation(acc3, xp3[:, :, 0:ls], AF.Identity, scale=P[:, 8 + cc:9 + cc]), ["xp"], ["cacc"])
                for w in range(1, 4):
                    V("dve", lambda e, w=w: e.scalar_tensor_tensor(acc3, xp3[:, :, w:w + ls], P[:, 8 + w * 12 + cc:9 + w * 12 + cc], acc3,
                                                                    ALU.mult, ALU.add), ["xp", "cacc"], ["cacc"])
                V("act", lambda e: e.activation(xbcT[:, cc, 0:n], cacc[:, 0:n], AF.Silu, bias=P[:, 56 + cc:57 + cc]), ["cacc"], ["xbcT"])

            wv, wkey = wload(w_in_v[:, :, 1024:2048], w_in_f[:, :, 1024:2048], 8, 1024, "wb_in%d_1024" % l)
            for cc in range(8):
                b = fm_proj(wv, wkey, cc * 128, 128, n, hT, "hT", 8)
                conv_chunk(b, cc)
                if zjobs:
                    zjob(*zjobs.pop(0))
            while zjobs:
                zjob(*zjobs.pop(0))
            wv, wkey = wload(w_in_v[:, :, 2048:2576], w_in_f[:, :, 2048:2576], 8, 528, "wb_in%d_2048" % l)
            for cc in range(8, 12):
                b = fm_proj(wv, wkey, (cc - 8) * 128, 128, n, hT, "hT", 8)
                conv_chunk(b, cc)
            b = fm_proj(wv, wkey, 512, 16, n, hT, "hT", 8)
            R = rows
            V("act", lambda e: e.activation(R["dt"][0:16, 0:n], PS[b][0:16, 0:n], AF.Exp, bias=H["dtb"][0:16, :]), ["P%d" % b, "hs%d" % l], ["rowsT"])
            V("act", lambda e: e.activation(R["dt"][0:16, 0:n], R["dt"][0:16, 0:n], AF.Ln, bias=1.0), ["rowsT"], ["rowsT"])
            V("dve", lambda e: e.tensor_scalar(R["a"][0:16, 0:n], R["dt"][0:16, 0:n], H["A"][0:16, :], None, ALU.mult), ["rowsT", "hs%d" % l], ["rowsT"])
            pipelined([S.capture(one_chunk, l, tile, ci, ch, "ssd", ssd_chunk) for ci, ch in enumerate(chunks)], l)
            if dbg and l == 0 and tile.get("first_tile"):
                dbg_out("xbcT", xbcT[:, :, 0:n], [128, 12, n], ["xbcT"])
                dbg_out("hT", hT[:, :, 0:n], [128, 8, n], ["hT"])
                dbg_out("dtrow", R["dt"][0:16, 0:n], [16, n], ["rowsT"])

            fence_all()
            if seq_start:
                ml_s0_exact(l)
            wv, wkey = wload(w_in_v[:, :, 2576:3600], w_in_f[:, :, 2576:3600], 8, 1024, "wb_in%d_2576" % l)
            for hh in range(4):
                b = fm_proj(wv, wkey, hh * 128, 128, n, hT, "hT", 8)
                V("act", lambda e, b=b, hh=hh: e.copy(qT[:, hh, 0:n], PS[b][:, 0:n]), ["P%d" % b], ["qT"])
                b = fm_proj(wv, wkey, 512 + hh * 128, 128, n, hT, "hT", 8)
                V("act", lambda e, b=b, hh=hh: e.activation(kT[:, hh, 0:n], PS[b][:, 0:n], AF.Identity, scale=128.0 ** -0.5), ["P%d" % b], ["kT"])
            for ci, (c0, L, seq, first, last) in enumerate(chunks):
                b = tm_proj(wv, wkey, 512, 512, c0, L)
                V("act", lambda e, b=b, ci=ci, L=L: e.activation(kTM[0:L, ci, :], PS[b][0:L, :], AF.Identity, scale=128.0 ** -0.5), ["P%d" % b], ["kTM"])
            wv, wkey = wload(w_in_v[:, :, 3600:4120], w_in_f[:, :, 3600:4120], 8, 520, "wb_in%d_3600" % l)
            for ci, (c0, L, seq, first, last) in enumerate(chunks):
                b = tm_proj(wv, wkey, 0, 512, c0, L)
                v4 = vTM[0:L, ci, :].rearrange("p (h c) -> p h c", h=4)
                V("act", lambda e, b=b, v4=v4, L=L: e.copy(v4[:, :, 0:128], PS[b][0:L, :].rearrange("p (h c) -> p h c", h=4)), ["P%d" % b], ["vTM"])
                V("dve", lambda e, v4=v4: e.memset(v4[:, :, 128:129], 1.0), (), ["vTM"])
            b = fm_proj(wv, wkey, 512, 4, n, hT, "hT", 8)
            V("act", lambda e, b=b: e.activation(R["ig"][0:4, 0:n], PS[b][0:4, 0:n], AF.Identity, bias=H["bi"][0:4, :]), ["P%d" % b, "hs%d" % l], ["rowsT"])
            b = fm_proj(wv, wkey, 516, 4, n, hT, "hT", 8)
            V("act", lambda e, b=b: e.activation(R["lf"][0:4, 0:n], PS[b][0:4, 0:n], AF.Exp, bias=H["nbf"][0:4, :], scale=-1.0), ["P%d" % b, "hs%d" % l], ["rowsT"])
            V("act", lambda e: e.activation(R["lf"][0:4, 0:n], R["lf"][0:4, 0:n], AF.Ln, bias=1.0), ["rowsT"], ["rowsT"])
            V("dve", lambda e: e.tensor_scalar(R["lf"][0:4, 0:n], R["lf"][0:4, 0:n], -1.0, None, ALU.mult), ["rowsT"], ["rowsT"])
            wv, wkey = wload(w_in_v[:, :, 4120:4632], w_in_f[:, :, 4120:4632], 8, 512, "wb_in%d_4120" % l)
            for ci, (c0, L, seq, first, last) in enumerate(chunks):
                b = tm_proj(wv, wkey, 0, 512, c0, L)
                V("act", lambda e, b=b, ci=ci, L=L: e.activation(oTM[0:L, ci, :], PS[b][0:L, :], AF.Sigmoid), ["P%d" % b], ["oTM"])

            for a in range(4):
                V("dve", lambda e, a=a: e.memset(CS[1]["c_kref"][a], 0.0), (), ["p1_kref%d" % a])
            wv, wkey = wload(w_in_v[:, :, 4632:5656], w_in_f[:, :, 4632:5656], 8, 1024, "wb_in%d_4632" % l)
            for hh in range(4):
                b = fm_proj(wv, wkey, hh * 128, 128, n, hT, "hT", 8)
                V("act", lambda e, b=b, hh=hh: e.activation(gqT[:, hh, 0:n], PS[b][:, 0:n], AF.Silu), ["P%d" % b], ["gqT"])
            for hh in range(4):
                b = fm_proj(wv, wkey, 512 + hh * 128, 128, n, hT, "hT", 8)
                V("act", lambda e, b=b: e.activation(cacc[:, 0:n], PS[b][:, 0:n], AF.Sigmoid), ["P%d" % b], ["cacc"])
                V("dve", lambda e, hh=hh: e.tensor_scalar(cacc[:, 0:n], cacc[:, 0:n], H["oml"][:, hh:hh + 1], H["lb"][:, hh:hh + 1], ALU.mult, ALU.add),
                  ["cacc", "hs%d" % l], ["cacc"])
                V("dve", lambda e, hh=hh: e.tensor_scalar(gkT[:, hh, 0:n], cacc[:, 0:n], -1.0, 1.0, ALU.mult, ALU.add), ["cacc"], ["gkT"])
                V("act", lambda e, hh=hh: e.activation(gT[:, hh, 0:n], cacc[:, 0:n], AF.Ln), ["cacc"], ["gT"])
            wv, wkey = wload(w_in_v[:, :, 5656:6680], w_in_f[:, :, 5656:6680], 8, 1024, "wb_in%d_5656" % l)
            for ci, (c0, L, seq, first, last) in enumerate(chunks):
                b = tm_proj(wv, wkey, 0, 512, c0, L)
                V("act", lambda e, b=b, ci=ci, L=L: e.copy(gviTM[0:L, ci, :], PS[b][0:L, :]), ["P%d" % b], ["gviTM"])
                b = tm_proj(wv, wkey, 512, 512, c0, L)
                V("act", lambda e, b=b, ci=ci, L=L: e.activation(ggTM[0:L, ci, :], PS[b][0:L, :], AF.Silu), ["P%d" % b], ["ggTM"])
            for ci, ch in enumerate(chunks):
                la = S.capture(one_chunk, l, tile, ci, ch, "ml", mlstm_chunk)
                lb = S.capture(one_chunk, l, tile, ci, ch, "hg", hgrn_chunk)
                interleave2(la, lb)
            if dbg and l == 0 and tile.get("first_tile"):
                dbg_out("mixT", mixT[:, :, 0:n], [128, 16, n], ["mixT"])

            w_out_v = w_out_b[l].rearrange("(cc p) d -> p cc d", p=128)
            w_out_f = w_out[l].rearrange("(cc p) d -> p cc d", p=128)
            for hf in range(2):
                wv, wkey = wload(w_out_v[:, :, hf * 512:(hf + 1) * 512], w_out_f[:, :, hf * 512:(hf + 1) * 512], 16, 512, "wb_out%d_%d" % (l, hf))
                for j in range(4):
                    b = fm_proj(wv, wkey, j * 128, 128, n, mixT, "mixT", 16)
                    dc = hf * 4 + j
                    V("dve", lambda e, b=b, dc=dc: e.tensor_tensor(xT[:, dc, 0:n], xT[:, dc, 0:n], PS[b][:, 0:n], ALU.add), ["P%d" % b, "xT"], ["xT"])
            fence_all()
            rmsnorm_fm(n, P[:, 88:96], hT, sqs, "sqsF")
            w1 = w_ffn_in_b[l].rearrange("(dc p) c -> p dc c", p=128)
            w1f = w_ffn_in[l].rearrange("(dc p) c -> p dc c", p=128)
            for t in range(6):
                c0 = t * 1024
                nc_ = min(1024, 2 * DFF - c0)
                wv, wkey = wload(w1[:, :, c0:c0 + nc_], w1f[:, :, c0:c0 + nc_], 8, nc_, "wb_f1%d_%d" % (l, c0))
                for j in range(nc_ // 128):
                    fch = (c0 // 128) + j
                    b = fm_proj(wv, wkey, j * 128, 128, n, hT, "hT", 8)
                    if fch < 22:
                        V("act", lambda e, b=b, fch=fch: e.activation(actT[:, fch, 0:n], PS[b][:, 0:n], AF.Silu), ["P%d" % b], ["actT"])
                    else:
                        f2 = fch - 22
                        V("dve", lambda e, b=b, f2=f2: e.tensor_tensor(actT[:, f2, 0:n], actT[:, f2, 0:n], PS[b][:, 0:n], ALU.mult),
                          ["P%d" % b, "actT"], ["actT"])
            w2 = w_ffn_out_b[l].rearrange("(fc p) d -> p fc d", p=128)
            w2f = w_ffn_out[l].rearrange("(fc p) d -> p fc d", p=128)
            for t in range(4):
                wv, wkey = wload(w2[:, :, t * 256:(t + 1) * 256], w2f[:, :, t * 256:(t + 1) * 256], 22, 256, "wb_f2%d_%d" % (l, t))
                for j in range(2):
                    b = fm_proj(wv, wkey, j * 128, 128, n, actT, "actT", 22)
                    dc = t * 2 + j
                    V("dve", lambda e, b=b, dc=dc: e.tensor_tensor(xT[:, dc, 0:n], xT[:, dc, 0:n], PS[b][:, 0:n], ALU.add), ["P%d" % b, "xT"], ["xT"])

        def ssd_chunk(l, tile, ci, ch):
            (c0, L, seq, first, last) = ch
            H = hs[l]
            ST = stt[l]
            P = pv[l]
            par = ci % 2
            kp = "p%d_" % par
            c = CS[par]
            R = dict(rows)
            R.update(c["rows"])
            sm = c["c_small"]
            c_sc = c["c_sc"]
            c_xsTM = c["c_xsTM"]
            c_xdt = c["c_xdt"]
            c_xdec = c["c_xdec"]
            c_BTM = c["c_BTM"]
            c_cbT = c["c_cbT"]
            c_dT = c["c_dT"]
            c_MT = c["c_MT"]
            c_t2 = c["c_t2"]
            c_y = c["c_y"]
            V("dve", lambda e: e.tensor_tensor_scan(R["acum"][0:16, 0:L], ones32[0:16, 0:L], R["a"][0:16, c0:c0 + L], 0.0, ALU.mult, ALU.add),
              [kp + "rows", "rowsT", "ones32"], [kp + "rows"])
            hl = sm[0:16, 128:256].bitcast(BF16)
            nhl = R["nacum"][0:16, :].bitcast(BF16)
            a_hi, a_lo, n_hi, n_lo = hl[:, 0:L], hl[:, 128:128 + L], nhl[:, 0:L], nhl[:, 128:128 + L]
            V("dve", lambda e: e.tensor_copy(a_hi, R["acum"][0:16, 0:L]), [kp + "rows"], [kp + "c_small"])
            V("dve", lambda e: e.tensor_tensor(R["decend"][0:16, 0:L], R["acum"][0:16, 0:L], a_hi, ALU.subtract), [kp + "rows", kp + "c_small"], [kp + "rows"])
            V("dve", lambda e: e.tensor_copy(a_lo, R["decend"][0:16, 0:L]), [kp + "rows"], [kp + "c_small"])
            V("dve", lambda e: e.tensor_scalar(n_hi, a_hi, -1.0, None, ALU.mult), [kp + "c_small"], [kp + "rows"])
            V("dve", lambda e: e.tensor_scalar(n_lo, a_lo, -1.0, None, ALU.mult), [kp + "c_small"], [kp + "rows"])
            V("act", lambda e: e.activation(R["decend"][0:16, 0:L], R["acum"][0:16, 0:L], AF.Exp,
                                            bias=R["acum"][0:16, L - 1:L], scale=-1.0), [kp + "rows"], [kp + "rows"])
            b = nb()

            def f(e):
                e.transpose(PS[b][0:L, 0:16], R["dt"][0:16, c0:c0 + L], ident32[0:16, 0:16])
                e.transpose(PS[b][0:L, 16:32], R["acum"][0:16, 0:L], ident32[0:16, 0:16])
                return e.transpose(PS[b][0:L, 32:48], R["decend"][0:16, 0:L], ident32[0:16, 0:16])
            V("pe", f, [kp + "rows", "rowsT", "ident32"], ["P%d" % b])
            V("dve", lambda e: e.tensor_copy(sm[0:L, 0:16], PS[b][0:L, 0:16]), ["P%d" % b], [kp + "c_small"])
            V("act", lambda e: e.activation(sm[0:L, 16:32], PS[b][0:L, 16:32], AF.Exp), ["P%d" % b], [kp + "c_small"])
            V("dve", lambda e: e.tensor_tensor(sm[0:L, 32:48], PS[b][0:L, 32:48], sm[0:L, 0:16], ALU.mult), ["P%d" % b, kp + "c_small"], [kp + "c_small"])
            bb = bcast_rows(R["acum"][0:16, L - 1:L], 16, kp + "rows", sm, kp)
            V("act", lambda e: e.activation(sm[:, 48:64], PS[bb][:, 0:16], AF.Exp), ["P%d" % bb], [kp + "c_small"])
            b1 = nb()
            pb1 = PS[b1].bitcast(BF16)

            def f(e):
                for cc in range(8):
                    ins = e.transpose(pb1[0:L, cc * 128:(cc + 1) * 128], xbcT[:, cc, c0:c0 + L], identb)
                return ins
            V("pe", f, ["xbcT", "identb"], ["P%d" % b1])
            b2 = nb()
            pb2 = PS[b2].bitcast(BF16)

            def f(e):
                for cc in range(2):
                    ins = e.transpose(pb2[0:L, cc * 128:(cc + 1) * 128], xbcT[:, 8 + cc, c0:c0 + L], identb)
                return ins
            V("pe", f, ["xbcT", "identb"], ["P%d" % b2])
            V("act", lambda e: e.copy(c_xsTM[0:L, :], pb1[0:L, :]), ["P%d" % b1], [kp + "c_xsTM"])
            V("act", lambda e: e.copy(c_BTM[0:L, :], pb2[0:L, 0:256]), ["P%d" % b2], [kp + "c_BTM"])
            xs3 = c_xsTM[0:L, :].rearrange("p (h c) -> p h c", h=16)
            V("dve", lambda e: e.tensor_tensor(c_xdt[0:L, :].rearrange("p (h c) -> p h c", h=16), xs3,
                                               sm[0:L, 0:16].unsqueeze(2).broadcast_to([L, 16, 64]), ALU.mult), [kp + "c_xsTM", kp + "c_small"], [kp + "c_xdt"])
            V("dve", lambda e: e.tensor_tensor(c_xdec[0:L, :].rearrange("p (h c) -> p h c", h=16), xs3,
                                               sm[0:L, 32:48].unsqueeze(2).broadcast_to([L, 16, 64]), ALU.mult), [kp + "c_xsTM", kp + "c_small"], [kp + "c_xdec"])
            b = nb()

            def f(e):
                for g in range(2):
                    ins = e.matmul(PS[b][0:L, g * 128:g * 128 + L], xbcT[:, 8 + g, c0:c0 + L], xbcT[:, 10 + g, c0:c0 + L], start=True, stop=True)
                return ins
            V("pe", f, ["xbcT"], ["P%d" % b])
            V("act", lambda e: e.copy(c_cbT[0:L, :, 0:L], PS[b][0:L, 0:256].rearrange("p (g c) -> p g c", g=2)[:, :, 0:L]), ["P%d" % b], [kp + "c_cbT"])
            for q4 in range(4):
                b = nb()

                def f(e, q4=q4, b=b):
                    if L == 128:
                        rhs4 = identb[0:16, 4 * q4:4 * q4 + 4].unsqueeze(2).broadcast_to([16, 4, 128])
                        e.matmul(PS[b][0:L, :], n_hi, rhs4, start=True, stop=False)
                        e.matmul(PS[b][0:L, :], n_lo, rhs4, start=False, stop=False)
                        e.matmul(PS[b][0:L, :], identb[0:L, 0:L], negmask4[0:L, :, :], start=False, stop=False)
                    for j in range(4):
                        hh = q4 * 4 + j
                        o = PS[b][0:L, j * 128:j * 128 + L]
                        selb = identb[0:16, hh:hh + 1].broadcast_to([16, L])
                        if L != 128:
                            e.matmul(o, n_hi, selb, start=True, stop=False)
                            e.matmul(o, n_lo, selb, start=False, stop=False)
                            e.matmul(o, identb[0:L, 0:L], negmask4[0:L, 0, 0:L], start=False, stop=False)
                        e.matmul(o, selb, a_hi, start=False, stop=False)
                        ins = e.matmul(o, selb, a_lo, start=False, stop=(j == 3 or L != 128))
                    return ins
                V("pe", f, [kp + "rows", kp + "c_small", "identb", "negmask4"], ["P%d" % b])
                V("act", lambda e, q4=q4, b=b: e.activation(c_dT[0:L, q4 * 4:(q4 + 1) * 4, 0:L],
                                                            PS[b][0:L, :].rearrange("p (j c) -> p j c", j=4)[:, :, 0:L], AF.Exp), ["P%d" % b], [kp + "c_dT"])
            for g in range(2):
                V("dve", lambda e, g=g: e.tensor_tensor(c_MT[0:L, g * 8:(g + 1) * 8, 0:L], c_dT[0:L, g * 8:(g + 1) * 8, 0:L],
                                                        c_cbT[0:L, g:g + 1, 0:L].broadcast_to([L, 8, L]), ALU.mult), [kp + "c_dT", kp + "c_cbT"], [kp + "c_MT"])
            V("act", lambda e: e.copy(hTb, ST["h32"]), ["h32_%d" % l], ["hTb"])
            byi = [nb(), nb()]
            bys = [nb(), nb()]
            for g in range(2):
                def f(e, g=g):
                    for j in range(8):
                        hh = g * 8 + j
                        ins = e.matmul(PS[byi[g]][0:L, j * 64:(j + 1) * 64], c_MT[0:L, hh, 0:L], c_xdt[0:L, hh * 64:(hh + 1) * 64], start=True, stop=True)
                    return ins
                V("pe", f, [kp + "c_MT", kp + "c_xdt"], ["P%d" % byi[g]])
                V("pe", lambda e, g=g: e.matmul(PS[bys[g]][0:L, :], xbcT[:, 10 + g, c0:c0 + L], hTb[:, g * 512:(g + 1) * 512], start=True, stop=True),
                  ["xbcT", "hTb"], ["P%d" % bys[g]])
            V("dve", lambda e: e.tensor_tensor(c_t2[0:L, :].rearrange("p (h c) -> p h c", h=16), xs3,
                                               H["Dbc"][0:L, :].unsqueeze(2).broadcast_to([L, 16, 64]), ALU.mult), [kp + "c_xsTM", "hs%d" % l], [kp + "c_t2"])
            for g in range(2):
                sl = slice(g * 512, (g + 1) * 512)
                V("dve", lambda e, g=g, sl=sl: e.tensor_tensor(c_y[0:L, sl].rearrange("p (h c) -> p h c", h=8),
                                                               PS[bys[g]][0:L, :].rearrange("p (h c) -> p h c", h=8),
                                                               sm[0:L, 16 + g * 8:24 + g * 8].unsqueeze(2).broadcast_to([L, 8, 64]), ALU.mult),
                  ["P%d" % bys[g], kp + "c_small"], [kp + "c_y"])
                V("dve", lambda e, g=g, sl=sl: e.tensor_tensor(c_y[0:L, sl], c_y[0:L, sl], PS[byi[g]][0:L, :], ALU.add), ["P%d" % byi[g], kp + "c_y"], [kp + "c_y"])
            V("dve", lambda e: e.tensor_tensor(c_y[0:L, :], c_y[0:L, :], c_t2[0:L, :], ALU.add), [kp + "c_y", kp + "c_t2"], [kp + "c_y"])
            V("dve", lambda e: e.tensor_tensor(c_y[0:L, :], c_y[0:L, :], zs[0:L, ci, :], ALU.mult), [kp + "c_y", "zs"], [kp + "c_y"])
            V("act", lambda e: e.activation(c_t2[0:L, :], c_y[0:L, :], AF.Square, accum_out=sm[0:L, 64:65]), [kp + "c_y"], [kp + "c_t2", kp + "c_small"])
            V("act", lambda e: e.activation(sm[0:L, 64:65], sm[0:L, 64:65], AF.Ln, bias=EPS, scale=1.0 / 1024), [kp + "c_small"], [kp + "c_small"])
            V("act", lambda e: e.activation(sm[0:L, 64:65], sm[0:L, 64:65], AF.Exp, scale=-0.5), [kp + "c_small"], [kp + "c_small"])
            V("act", lambda e: e.activation(c_y[0:L, :], c_y[0:L, :], AF.Identity, scale=sm[0:L, 64:65]), [kp + "c_y", kp + "c_small"], [kp + "c_y"])
            for hf in range(2):
                b = nb()

                def f(e, hf=hf, b=b):
                    for j in range(4):
                        cc = hf * 4 + j
                        ins = e.transpose(PS[b][:, j * 128:j * 128 + L], c_y[0:L, cc * 128:(cc + 1) * 128], ident32[0:L, 0:L])
                    return ins
                V("pe", f, [kp + "c_y", "ident32"], ["P%d" % b])
                for j in range(4):
                    cc = hf * 4 + j
                    V("act", lambda e, b=b, j=j, cc=cc: e.activation(mixT[:, cc, c0:c0 + L], PS[b][:, j * 128:j * 128 + L], AF.Copy,
                                                                     scale=pv[l][:, 68 + cc:69 + cc]), ["P%d" % b], ["mixT"])
            bu = [nb(), nb()]
            for g in range(2):
                V("pe", lambda e, g=g: e.matmul(PS[bu[g]][:, :], c_BTM[0:L, g * 128:(g + 1) * 128], c_xdec[0:L, g * 512:(g + 1) * 512], start=True, stop=True),
                  [kp + "c_BTM", kp + "c_xdec"], ["P%d" % bu[g]])
            V("dve", lambda e: e.tensor_tensor(ST["h32"].rearrange("p (h c) -> p h c", h=16), ST["h32"].rearrange("p (h c) -> p h c", h=16),
                                               sm[:, 48:64].unsqueeze(2).broadcast_to([128, 16, 64]), ALU.mult), ["h32_%d" % l, kp + "c_small", "hTb"], ["h32_%d" % l])
            for g in range(2):
                sl = slice(g * 512, (g + 1) * 512)
                V("dve", lambda e, g=g, sl=sl: e.tensor_tensor(ST["h32"][:, sl], ST["h32"][:, sl], PS[bu[g]][:, :], ALU.add),
                  ["h32_%d" % l, "P%d" % bu[g]], ["h32_%d" % l])

        def mlstm_chunk(l, tile, ci, ch):
            (c0, L, seq, first, last) = ch
            H = hs[l]
            ST = stt[l]
            cs = slice(c0, c0 + L)
            par = 0
            kp = "p%d_" % par
            c = CS[par]
            R = dict(rows)
            R.update(c["rows"])
            sm = c["c_small"]
            c_sc = c["c_sc"]
            c_wT = c["c_wT"]
            c_SwT = c["c_SwT"]
            c_kw = c["c_kw"]
            c_int = c["c_int"]
            c_hm = c["c_hm"]
            m0 = ST["m"][0:4, :]
            V("dve", lambda e: e.tensor_tensor_scan(R["fcum"][0:4, 0:L], ones32[0:4, 0:L], R["lf"][0:4, cs], 0.0, ALU.mult, ALU.add), [kp + "rows", "rowsT", "ones32"], [kp + "rows"])
            V("dve", lambda e: e.tensor_tensor(R["u"][0:4, 0:L], R["ig"][0:4, cs], R["fcum"][0:4, 0:L], ALU.subtract), [kp + "rows", "rowsT"], [kp + "rows"])
            V("dve", lambda e: e.tensor_tensor_scan(R["cmm"][0:4, 0:L], ones32[0:4, 0:L], R["u"][0:4, 0:L], m0, ALU.mult, ALU.max), [kp + "rows", "ones32", "m%d" % l], [kp + "rows"])
            V("dve", lambda e: e.tensor_scalar(R["ncmm"][0:4, 0:L], R["cmm"][0:4, 0:L], -1.0, None, ALU.mult), [kp + "rows"], [kp + "rows"])
            hl = sm[0:4, 128:256].bitcast(BF16)
            nhl = R["wkr"][0:4, :].bitcast(BF16)
            u_hi, u_lo, c_hi, c_lo = hl[:, 0:L], hl[:, 128:128 + L], nhl[:, 0:L], nhl[:, 128:128 + L]
            V("act", lambda e: e.activation(R["winter"][0:4, 0:L], R["cmm"][0:4, 0:L], AF.Exp, bias=m0, scale=-1.0), [kp + "rows", "m%d" % l], [kp + "rows"])
            V("dve", lambda e: e.tensor_tensor(R["emi"][0:4, 0:L], R["fcum"][0:4, 0:L], R["cmm"][0:4, 0:L], ALU.add), [kp + "rows"], [kp + "rows"])
            V("act", lambda e: e.activation(R["emi"][0:4, 0:L], R["emi"][0:4, 0:L], AF.Exp, scale=-1.0), [kp + "rows"], [kp + "rows"])
            V("act", lambda e: e.activation(R["wk2"][0:4, 0:L], R["u"][0:4, 0:L], AF.Exp, bias=R["ncmm"][0:4, L - 1:L]), [kp + "rows"], [kp + "rows"])
            V("act", lambda e: e.activation(sm[0:4, 70:71], R["cmm"][0:4, L - 1:L], AF.Exp, bias=m0, scale=-1.0), [kp + "rows", "m%d" % l], [kp + "c_small"])
            V("dve", lambda e: e.tensor_tensor(ST["m"][0:4, :], R["fcum"][0:4, L - 1:L], R["cmm"][0:4, L - 1:L], ALU.add),
              [kp + "rows"], ["m%d" % l])
            b = nb()

            def f(e):
                e.transpose(PS[b][0:L, 0:4], R["winter"][0:4, 0:L], ident32[0:4, 0:4])
                e.transpose(PS[b][0:L, 4:8], R["emi"][0:4, 0:L], ident32[0:4, 0:4])
                return e.transpose(PS[b][0:L, 8:12], R["wk2"][0:4, 0:L], ident32[0:4, 0:4])
            V("pe", f, [kp + "rows", "ident32"], ["P%d" % b])
            V("dve", lambda e: e.tensor_copy(sm[0:L, 72:84], PS[b][0:L, 0:12]), ["P%d" % b], [kp + "c_small"])
            b = nb()

            V("dve", lambda e: e.tensor_copy(u_hi, R["u"][0:4, 0:L]), [kp + "rows"], [kp + "c_small"])
            V("dve", lambda e: e.tensor_tensor(R["fcum"][0:4, 0:L], R["u"][0:4, 0:L], u_hi, ALU.subtract), [kp + "rows", kp + "c_small"], [kp + "rows"])
            V("dve", lambda e: e.tensor_copy(u_lo, R["fcum"][0:4, 0:L]), [kp + "rows"], [kp + "c_small"])
            V("dve", lambda e: e.tensor_copy(c_hi, R["ncmm"][0:4, 0:L]), [kp + "rows"], [kp + "rows"])
            V("dve", lambda e: e.tensor_tensor(R["fcum"][0:4, 0:L], R["ncmm"][0:4, 0:L], c_hi, ALU.subtract), [kp + "rows"], [kp + "rows"])
            V("dve", lambda e: e.tensor_copy(c_lo, R["fcum"][0:4, 0:L]), [kp + "rows"], [kp + "rows"])

            def f(e):
                if L == 128:
                    rhs4 = identb[0:4, 0:4].unsqueeze(2).broadcast_to([4, 4, 128])
                    e.matmul(PS[b][0:L, :], u_hi, rhs4, start=True, stop=False)
                    e.matmul(PS[b][0:L, :], u_lo, rhs4, start=False, stop=False)
                    e.matmul(PS[b][0:L, :], identb[0:L, 0:L], negmask4[0:L, :, :], start=False, stop=False)
                for hh in range(4):
                    o = PS[b][0:L, hh * 128:hh * 128 + L]
                    selb = identb[0:4, hh:hh + 1].broadcast_to([4, L])
                    if L != 128:
                        e.matmul(o, u_hi, selb, start=True, stop=False)
                        e.matmul(o, u_lo, selb, start=False, stop=False)
                        e.matmul(o, identb[0:L, 0:L], negmask4[0:L, 0, 0:L], start=False, stop=False)
                    e.matmul(o, selb, c_hi, start=False, stop=False)
                    ins = e.matmul(o, selb, c_lo, start=False, stop=(hh == 3 or L != 128))
                return ins
            V("pe", f, [kp + "rows", kp + "c_small", "identb", "negmask4"], ["P%d" % b])
            V("act", lambda e: e.activation(c_wT[0:L, :, 0:L], PS[b][0:L, :].rearrange("p (j c) -> p j c", j=4)[:, :, 0:L], AF.Exp), ["P%d" % b], [kp + "c_wT"])
            b = nb()

            def f(e):
                for hh in range(4):
                    ins = e.matmul(PS[b][0:L, hh * 128:hh * 128 + L], kT[:, hh, cs], qT[:, hh, cs], start=True, stop=True)
                return ins
            V("pe", f, ["kT", "qT"], ["P%d" % b])
            if first and tile["kind"] == "p":
                V("dve", lambda e: e.tensor_copy(PS[b][0:1, 0:512:128], s0buf[0:1, :]), ["s0buf", "P%d" % b], ["P%d" % b])
            V("dve", lambda e: e.tensor_tensor(c_SwT[0:L, :, 0:L], PS[b][0:L, :].rearrange("p (j c) -> p j c", j=4)[:, :, 0:L], c_wT[0:L, :, 0:L], ALU.mult),
              ["P%d" % b, kp + "c_wT"], [kp + "c_SwT"])
            V("act", lambda e: e.copy(Cb, ST["C32"]), ["C32_%d" % l], ["Cb"])
            bi_ = [nb(), nb()]
            bx_ = [nb(), nb()]
            v4 = vTM[0:L, ci, :].rearrange("p (h c) -> p h c", h=4)
            for pr in range(2):
                def f(e, pr=pr):
                    for j in range(2):
                        hh = pr * 2 + j
                        ins = e.matmul(PS[bi_[pr]][0:L, j * 256:j * 256 + 129], c_SwT[0:L, hh, 0:L], v4[:, hh, :], start=True, stop=True)
                    return ins
                V("pe", f, [kp + "c_SwT", "vTM"], ["P%d" % bi_[pr]])

                def f(e, pr=pr):
                    for j in range(2):
                        hh = pr * 2 + j
                        ins = e.matmul(PS[bx_[pr]][0:L, j * 256:j * 256 + 129], qT[:, hh, cs], Cb[:, hh, :], start=True, stop=True)
                    return ins
                V("pe", f, ["qT", "Cb"], ["P%d" % bx_[pr]])
                for j in range(2):
                    hh = pr * 2 + j
                    V("act", lambda e, pr=pr, j=j, hh=hh: e.activation(c_int[0:L, hh, :], PS[bx_[pr]][0:L, j * 256:j * 256 + 129], AF.Copy,
                                                                       scale=sm[0:L, 72 + hh:73 + hh]), ["P%d" % bx_[pr], kp + "c_small"], [kp + "c_int"])
                V("dve", lambda e, pr=pr: e.tensor_tensor(c_int[0:L, pr * 2:pr * 2 + 2, :], c_int[0:L, pr * 2:pr * 2 + 2, :],
                                                          PS[bi_[pr]][0:L, :].rearrange("p (j c) -> p j c", j=2)[:, :, 0:129], ALU.add),
                  [kp + "c_int", "P%d" % bi_[pr]], [kp + "c_int"])
            V("act", lambda e: e.activation(sm[0:L, 84:88], c_int[0:L, :, 128], AF.Abs), [kp + "c_int"], [kp + "c_small"])
            V("dve", lambda e: e.tensor_tensor(sm[0:L, 84:88], sm[0:L, 84:88], sm[0:L, 76:80], ALU.max), [kp + "c_small"], [kp + "c_small"])
            V("dve", lambda e: e.reciprocal(sm[0:L, 84:88], sm[0:L, 84:88]), [kp + "c_small"], [kp + "c_small"])
            V("dve", lambda e: e.tensor_tensor(c_hm[0:L, :, :], c_int[0:L, :, 0:128], sm[0:L, 84:88].unsqueeze(2).broadcast_to([L, 4, 128]), ALU.mult),
              [kp + "c_int", kp + "c_small"], [kp + "c_hm"])
            def f(e):
                for hh in range(4):
                    ins = e.activation(c_sc[0:L, hh * 128:(hh + 1) * 128], c_hm[0:L, hh, :], AF.Square, accum_out=sm[0:L, 88 + hh:89 + hh])
                return ins
            V("act", f, [kp + "c_hm"], [kp + "c_sc", kp + "c_small"])
            V("act", lambda e: e.activation(sm[0:L, 88:92], sm[0:L, 88:92], AF.Ln, bias=EPS, scale=1.0 / 128), [kp + "c_small"], [kp + "c_small"])
            V("act", lambda e: e.activation(sm[0:L, 88:92], sm[0:L, 88:92], AF.Exp, scale=-0.5), [kp + "c_small"], [kp + "c_small"])
            def f(e):
                for hh in range(4):
                    ins = e.activation(c_hm[0:L, hh, :], c_hm[0:L, hh, :], AF.Identity, scale=sm[0:L, 88 + hh:89 + hh])
                return ins
            V("act", f, [kp + "c_hm", kp + "c_small"], [kp + "c_hm"])
            V("dve", lambda e: e.tensor_tensor(c_hm[0:L, :, :], c_hm[0:L, :, :], oTM[0:L, ci, :].rearrange("p (h c) -> p h c", h=4), ALU.mult),
              [kp + "c_hm", "oTM"], [kp + "c_hm"])
            b = nb()

            def f(e):
                for hh in range(4):
                    ins = e.transpose(PS[b][:, hh * 128:hh * 128 + L], c_hm[0:L, hh, :], ident32[0:L, 0:L])
                return ins
            V("pe", f, [kp + "c_hm", "ident32"], ["P%d" % b])
            for hh in range(4):
                V("act", lambda e, hh=hh: e.activation(mixT[:, 8 + hh, cs], PS[b][:, hh * 128:hh * 128 + L], AF.Identity, scale=pv[l][:, 76 + hh:77 + hh]),
                  ["P%d" % b], ["mixT"])
            def f(e):
                for hh in range(4):
                    ins = e.activation(c_kw[0:L, hh, :], kTM[0:L, ci, hh * 128:(hh + 1) * 128], AF.Identity, scale=sm[0:L, 80 + hh:81 + hh])
                return ins
            V("act", f, ["kTM", kp + "c_small"], [kp + "c_kw"])
            bu = [nb(), nb()]
            for pr in range(2):
                def f(e, pr=pr):
                    for j in range(2):
                        hh = pr * 2 + j
                        ins = e.matmul(PS[bu[pr]][:, j * 256:j * 256 + 129], c_kw[0:L, hh, :], v4[:, hh, :], start=True, stop=True)
                    return ins
                V("pe", f, [kp + "c_kw", "vTM"], ["P%d" % bu[pr]])
            bb = bcast_rows(sm[0:4, 70:71], 4, kp + "c_small", sm, kp)
            V("dve", lambda e: e.tensor_copy(sm[:, 92:96], PS[bb][:, 0:4]), ["P%d" % bb], [kp + "c_small"])
            V("dve", lambda e: e.tensor_tensor(ST["C32"], ST["C32"], sm[:, 92:96].unsqueeze(2).broadcast_to([128, 4, 129]), ALU.mult),
              ["C32_%d" % l, kp + "c_small", "Cb"], ["C32_%d" % l])
            for pr in range(2):
                V("dve", lambda e, pr=pr: e.tensor_tensor(ST["C32"][:, pr * 2:pr * 2 + 2, :], ST["C32"][:, pr * 2:pr * 2 + 2, :],
                                                          PS[bu[pr]][:, :].rearrange("p (j c) -> p j c", j=2)[:, :, 0:129], ALU.add),
                  ["C32_%d" % l, "P%d" % bu[pr]], ["C32_%d" % l])

        def hgrn_chunk(l, tile, ci, ch):
            (c0, L, seq, first, last) = ch
            ST = stt[l]
            cs = slice(c0, c0 + L)
            par = 1
            kp = "p%d_" % par
            c = CS[par]
            R = dict(rows)
            R.update(c["rows"])
            sm = c["c_small"]
            c_sc = c["c_sc"]
            c_gcp = c["c_gcp"]
            c_e = c["c_e"]
            c_qref = c["c_qref"]
            c_qdec = c["c_qdec"]
            c_kref = c["c_kref"]
            c_kend = c["c_kend"]
            c_kendTM = c["c_kendTM"]
            c_ATb = c["c_ATb"]
            c_o = c["c_o"]
            nsb = max(1, L // 32)
            V("dve", lambda e: e.memset(c_gcp[:, :, 0:1], 0.0), (), [kp + "c_gcp"])
            for hh in range(4):
                V("dve", lambda e, hh=hh: e.tensor_tensor_scan(c_gcp[:, hh, 1:1 + L], ones32[:, 0:L], gT[:, hh, cs], 0.0, ALU.mult, ALU.add),
                  ["gT", "ones32"], [kp + "c_gcp"])
            gc = c_gcp[:, :, 1:1 + L]
            gref_full = c_gcp[:, :, 0:L:32].unsqueeze(3).broadcast_to([128, 4, nsb, 32])
            V("dve", lambda e: e.tensor_tensor(c_e[:, :, 0:L].rearrange("p h (a j) -> p h a j", a=nsb), gc.rearrange("p h (a j) -> p h a j", a=nsb),
                                               gref_full, ALU.subtract), [kp + "c_gcp"], [kp + "c_e"])
            V("act", lambda e: e.activation(c_e[:, :, 0:L], c_e[:, :, 0:L], AF.Exp), [kp + "c_e"], [kp + "c_e"])
            V("dve", lambda e: e.tensor_tensor(c_qref[:, :, 0:L], c_e[:, :, 0:L], gqT[:, :, cs], ALU.mult), [kp + "c_e", "gqT"], [kp + "c_qref"])
            V("act", lambda e: e.activation(c_e[:, :, 0:L], gc, AF.Exp), [kp + "c_gcp", kp + "c_qref"], [kp + "c_e"])
            V("dve", lambda e: e.tensor_tensor(c_qdec[:, :, 0:L], c_e[:, :, 0:L], gqT[:, :, cs], ALU.mult), [kp + "c_e", "gqT"], [kp + "c_qdec"])
            for a in range(nsb):
                na = min(L, 32 * (a + 1))
                V("dve", lambda e, a=a, na=na: e.tensor_tensor(c_e[:, :, 0:na], c_gcp[:, :, 32 * a:32 * a + 1].broadcast_to([128, 4, na]),
                                                               c_gcp[:, :, 1:1 + na], ALU.subtract), [kp + "c_gcp", kp + "c_qdec"], [kp + "c_e"])
                V("act", lambda e, na=na: e.activation(c_e[:, :, 0:na], c_e[:, :, 0:na], AF.Exp), [kp + "c_e"], [kp + "c_e"])
                V("dve", lambda e, a=a, na=na: e.tensor_tensor(c_kref[a][:, :, 0:na], c_e[:, :, 0:na], gkT[:, :, c0:c0 + na], ALU.mult),
                  [kp + "c_e", "gkT"], [kp + "kref%d" % a])
            V("dve", lambda e: e.tensor_tensor(c_e[:, :, 0:L], c_gcp[:, :, L:L + 1].broadcast_to([128, 4, L]), gc, ALU.subtract),
              [kp + "c_gcp", kp + "kref%d" % (nsb - 1)], [kp + "c_e"])
            V("act", lambda e: e.activation(c_e[:, :, 0:L], c_e[:, :, 0:L], AF.Exp), [kp + "c_e"], [kp + "c_e"])
            V("dve", lambda e: e.tensor_tensor(c_kend[:, :, 0:L], c_e[:, :, 0:L], gkT[:, :, cs], ALU.mult), [kp + "c_e", "gkT"], [kp + "c_kend"])
            V("act", lambda e: e.activation(sm[:, 100:104], c_gcp[:, :, L], AF.Exp), [kp + "c_gcp"], [kp + "c_small"])
            b = nb()

            def f(e):
                for hh in range(4):
                    for a in range(nsb):
                        ins = e.matmul(PS[b][0:L, hh * 128 + 32 * a:hh * 128 + 32 * a + 32], c_kref[a][:, hh, 0:L], c_qref[:, hh, 32 * a:32 * a + 32],
                                       start=True, stop=True)
                return ins
            V("pe", f, [kp + "kref%d" % a for a in range(nsb)] + [kp + "c_qref"], ["P%d" % b])
            V("dve", lambda e: e.tensor_tensor(c_ATb[0:L, :, 0:L], PS[b][0:L, :].rearrange("p (j c) -> p j c", j=4)[:, :, 0:L],
                                               mask01b[0:L, 0:L].unsqueeze(1).broadcast_to([L, 4, L]), ALU.mult), ["P%d" % b, "mask01b"], [kp + "c_ATb"])
            V("act", lambda e: e.copy(Sb, ST["S32"]), ["S32_%d" % l], ["Sb"])
            bo = nb()

            def f(e):
                for hh in range(4):
                    o = PS[bo][0:L, hh * 128:(hh + 1) * 128]
                    e.matmul(o, c_ATb[0:L, hh, 0:L], gviTM[0:L, ci, hh * 128:(hh + 1) * 128], start=True, stop=False)
                    ins = e.matmul(o, c_qdec[:, hh, 0:L], Sb[:, hh, :], start=False, stop=True)
                return ins
            V("pe", f, [kp + "c_ATb", "gviTM", kp + "c_qdec", "Sb"], ["P%d" % bo])
            V("act", lambda e: e.copy(c_o[0:L, :, :], PS[bo][0:L, :].rearrange("p (h c) -> p h c", h=4)), ["P%d" % bo], [kp + "c_o"])
            def f(e):
                for hh in range(4):
                    ins = e.activation(c_sc[0:L, 512 + hh * 128:512 + (hh + 1) * 128], c_o[0:L, hh, :], AF.Square, accum_out=sm[0:L, 104 + hh:105 + hh])
                return ins
            V("act", f, [kp + "c_o"], [kp + "c_sc2", kp + "c_small"])
            V("act", lambda e: e.activation(sm[0:L, 104:108], sm[0:L, 104:108], AF.Ln, bias=EPS, scale=1.0 / 128), [kp + "c_small"], [kp + "c_small"])
            V("act", lambda e: e.activation(sm[0:L, 104:108], sm[0:L, 104:108], AF.Exp, scale=-0.5), [kp + "c_small"], [kp + "c_small"])
            def f(e):
                for hh in range(4):
                    ins = e.activation(c_o[0:L, hh, :], c_o[0:L, hh, :], AF.Identity, scale=sm[0:L, 104 + hh:105 + hh])
                return ins
            V("act", f, [kp + "c_o", kp + "c_small"], [kp + "c_o"])
            V("dve", lambda e: e.tensor_tensor(c_o[0:L, :, :], c_o[0:L, :, :], ggTM[0:L, ci, :].rearrange("p (h c) -> p h c", h=4), ALU.mult),
              [kp + "c_o", "ggTM"], [kp + "c_o"])
            b = nb()

            def f(e):
                for hh in range(4):
                    ins = e.transpose(PS[b][:, hh * 128:hh * 128 + L], c_o[0:L, hh, :], ident32[0:L, 0:L])
                return ins
            V("pe", f, [kp + "c_o", "ident32"], ["P%d" % b])
            for hh in range(4):
                V("act", lambda e, hh=hh, b=b: e.activation(mixT[:, 12 + hh, cs], PS[b][:, hh * 128:hh * 128 + L], AF.Identity, scale=pv[l][:, 84 + hh:85 + hh]),
                  ["P%d" % b], ["mixT"])
            b = nb()
            pbt = PS[b].bitcast(BF16)

            def f(e):
                for hh in range(4):
                    ins = e.transpose(pbt[0:L, hh * 128:(hh + 1) * 128], c_kend[:, hh, 0:L], identb)
                return ins
            V("pe", f, [kp + "c_kend", "identb"], ["P%d" % b])
            V("act", lambda e: e.copy(c_kendTM[0:L, :, :].rearrange("p h c -> p (h c)"), pbt[0:L, 0:512]), ["P%d" % b], [kp + "c_kendTM"])
            bu = nb()

            def f(e):
                for hh in range(4):
                    ins = e.matmul(PS[bu][:, hh * 128:(hh + 1) * 128], c_kendTM[0:L, hh, :], gviTM[0:L, ci, hh * 128:(hh + 1) * 128], start=True, stop=True)
                return ins
            V("pe", f, [kp + "c_kendTM", "gviTM"], ["P%d" % bu])
            V("dve", lambda e: e.tensor_tensor(ST["S32"], ST["S32"], sm[:, 100:104].unsqueeze(2).broadcast_to([128, 4, 128]), ALU.mult),
              ["S32_%d" % l, kp + "c_small", "Sb"], ["S32_%d" % l])
            V("dve", lambda e: e.tensor_tensor(ST["S32"], ST["S32"], PS[bu][:, :].rearrange("p (h c) -> p h c", h=4), ALU.add),
              ["S32_%d" % l, "P%d" % bu], ["S32_%d" % l])

        def state_init_prompt(l, ch, which):
            ST = stt[l]
            if which == "ssd":
                V("pool", lambda e: e.memset(ST["h32"], 0.0), (), ["h32_%d" % l])
            elif which == "ml":
                V("pool", lambda e: e.memset(ST["C32"], 0.0), (), ["C32_%d" % l])
                V("pool", lambda e: e.memset(ST["m"], 0.0), (), ["m%d" % l])
            else:
                V("pool", lambda e: e.memset(ST["S32"], 0.0), (), ["S32_%d" % l])

        def state_init_sample(l, ch, which):
            (c0, L, seq, first, last) = ch
            ST = stt[l]
            if which == "ssd":
                S.dma("sp", io_ssd, state_ssd[l, seq].rearrange("(pr h2) p n -> (h2 p) pr n", h2=2), (), ["io_ssd"])
                for hf in range(2):
                    b = nb()

                    def f(e, hf=hf, b=b):
                        for j in range(4):
                            ins = e.transpose(PS[b][:, j * 128:(j + 1) * 128], io_ssd[:, hf * 4 + j, :], ident32)
                        return ins
                    V("pe", f, ["io_ssd", "ident32"], ["P%d" % b])
                    V("dve", lambda e, hf=hf, b=b: e.tensor_copy(ST["h32"][:, hf * 512:(hf + 1) * 512], PS[b][:, :]), ["P%d" % b], ["h32_%d" % l])
            elif which == "ml":
                S.dma("sp", ST["C32"][:, :, 0:128], state_mc[l, seq].rearrange("h k v -> k h v"), (), ["C32_%d" % l])
                S.dma("sp", io_n[0:4, :], state_mn[l, seq], (), ["io_n"])
                b = nb()
                V("pe", lambda e: e.transpose(PS[b][:, 0:4], io_n[0:4, :], ident32[0:4, 0:4]), ["io_n", "ident32"], ["P%d" % b])
                V("dve", lambda e: e.tensor_copy(ST["C32"][:, :, 128], PS[b][:, 0:4]), ["P%d" % b], ["C32_%d" % l])
                S.dma("sp", ST["m"][0:4, :], state_mm[l, seq].rearrange("(a b) -> a b", b=1), (), ["m%d" % l])
            else:
                S.dma("sp", ST["S32"], state_hg[l, seq].rearrange("h k v -> k h v"), (), ["S32_%d" % l])

        def state_store(pre):
            def fn(l, ch, which):
                (c0, L, seq, first, last) = ch
                ST = stt[l]
                if which == "ssd":
                    for hf in range(2):
                        b = nb()

                        def f(e, hf=hf, b=b):
                            for j in range(4):
                                ins = e.transpose(PS[b][:, j * 128:(j + 1) * 128], ST["h32"][:, (hf * 4 + j) * 128:(hf * 4 + j + 1) * 128], ident32)
                            return ins
                        V("pe", f, ["h32_%d" % l, "ident32"], ["P%d" % b])
                        V("dve", lambda e, hf=hf, b=b: e.tensor_copy(io_ssd[:, hf * 4:(hf + 1) * 4, :], PS[b][:, :].rearrange("p (j c) -> p j c", j=4)),
                          ["P%d" % b], ["io_ssd"])
                    S.dma("sp", O[pre + "_ssd"][l, seq].rearrange("(pr h2) p n -> (h2 p) pr n", h2=2), io_ssd, ["io_ssd"], ())
                elif which == "ml":
                    S.dma("sp", O[pre + "_mc"][l, seq].rearrange("h k v -> k h v"), ST["C32"][:, :, 0:128], ["C32_%d" % l], ())
                    b = nb()
                    V("dve", lambda e: e.tensor_copy(CS[0]["c_small"][:, 124:128], ST["C32"][:, :, 128]), ["C32_%d" % l], ["c_small_n"])
                    V("pe", lambda e: e.transpose(PS[b][0:4, 0:128], CS[0]["c_small"][:, 124:128], ident32), ["c_small_n", "ident32"], ["P%d" % b])
                    V("dve", lambda e: e.tensor_copy(io_n[0:4, :], PS[b][0:4, 0:128]), ["P%d" % b], ["io_n"])
                    S.dma("sp", O[pre + "_mn"][l, seq], io_n[0:4, :], ["io_n"], ())
                    S.dma("sp", O[pre + "_mm"][l, seq].rearrange("(a b) -> a b", b=1), ST["m"][0:4, :], ["m%d" % l], ())
                else:
                    S.dma("sp", O[pre + "_hg"][l, seq].rearrange("h k v -> k h v"), ST["S32"], ["S32_%d" % l], ())
            return fn

        def conv_store(pre, l, seq, src3, key):
            for q in range(3):
                b = nb()

                def f(e, q=q, b=b):
                    for j in range(4):
                        cc = q * 4 + j
                        ins = e.transpose(PS[b][0:3, j * 128:(j + 1) * 128], src3[:, cc, :], ident32)
                    return ins
                V("pe", f, [key, "ident32"], ["P%d" % b])
                V("dve", lambda e, q=q, b=b: e.tensor_copy(io_conv[0:3, q * 512:(q + 1) * 512], PS[b][0:3, :]), ["P%d" % b], ["io_conv"])
            S.dma("sp", O[pre + "_conv"][l, seq], io_conv[0:3, :], ["io_conv"], ())

        def run_tile(tile):
            n = tile["n"]
            for ci, (c0, L, seq, first, last) in enumerate(tile["chunks"]):
                xi = xin[ci % 2]
                xk = "p%d_c_y" % (ci % 2)
                S.dma("sp", xi[0:L, :], tile["xsrc"](ci), (), [xk])
                for hf in range(2):
                    b = nb()

                    def f(e, hf=hf, b=b, xi=xi, L=L):
                        for j in range(4):
                            ins = e.transpose(PS[b][:, j * 128:j * 128 + L], xi[0:L, (hf * 4 + j) * 128:(hf * 4 + j + 1) * 128], ident32[0:L, 0:L])
                        return ins
                    V("pe", f, [xk, "ident32"], ["P%d" % b])
                    V("dve", lambda e, hf=hf, b=b, c0=c0, L=L: e.tensor_copy(xT[:, hf * 4:(hf + 1) * 4, c0:c0 + L],
                                                                             PS[b][:, :].rearrange("p (j c) -> p j c", j=4)[:, :, 0:L]),
                      ["P%d" % b], ["xT"])
            if tile["kind"] == "s":
                for l in range(2):
                    pass
            for l in range(2):
                if tile["kind"] == "s":
                    for si, (off, ln, seq) in enumerate(tile["segs"]):
                        S.dma("sp", io_conv[0:3, :], state_conv[l, seq], (), ["io_conv"])
                        b = nb()

                        def f(e, b=b):
                            for cc in range(12):
                                ins = e.transpose(PS[b][:, cc * 3:cc * 3 + 3], io_conv[0:3, cc * 128:(cc + 1) * 128], ident32[0:3, 0:3])
                            return ins
                        V("pe", f, ["io_conv", "ident32"], ["P%d" % b])
                        V("dve", lambda e, si=si, b=b: e.tensor_copy(convin[:, si, :, :], PS[b][:, 0:36].rearrange("p (c t) -> p c t", c=12)),
                          ["P%d" % b], ["convin"])
                layer(l, tile)
                if tile["kind"] == "s":
                    for si, (off, ln, seq) in enumerate(tile["segs"]):
                        conv_store("s", l, seq, convout[:, si, :, :], "convout")
                elif tile["last_tile"]:
                    conv_store("p", l, tile["seq"], stt[l]["conv"], "conv%d" % l)
            fence_all()
            V("act", lambda e: e.activation(sqs[:, :, 0:n], xT[:, :, 0:n], AF.Square), ["xT"], ["sqsF"])
            b = nb()

            def f(e):
                for dc in range(8):
                    ins = e.matmul(PS[b][:, 0:n], onesb, sqs[:, dc, 0:n], start=(dc == 0), stop=(dc == 7))
                return ins
            V("pe", f, ["sqsF", "onesb"], ["P%d" % b])
            V("act", lambda e: e.activation(rstd[:, 0:n], PS[b][:, 0:n], AF.Ln, bias=EPS, scale=1.0 / D), ["P%d" % b], ["rstd"])
            V("act", lambda e: e.activation(rstd[:, 0:n], rstd[:, 0:n], AF.Exp, scale=-0.5), ["rstd"], ["rstd"])
            for dc in range(8):
                V("dve", lambda e, dc=dc: e.scalar_tensor_tensor(xT[:, dc, 0:n], xT[:, dc, 0:n], pvf[:, dc:dc + 1], rstd[:, 0:n], ALU.mult, ALU.mult),
                  ["xT", "rstd", "pvf"], ["xT"])
            for ci, (c0, L, seq, first, last) in enumerate(tile["chunks"]):
                xi = ostg[ci % 2]
                ok = "ostg%d" % (ci % 2)
                for hf in range(2):
                    b = nb()

                    def f(e, hf=hf, b=b, c0=c0, L=L):
                        for j in range(4):
                            ins = e.transpose(PS[b][0:L, j * 128:(j + 1) * 128], xT[:, hf * 4 + j, c0:c0 + L], ident32)
                        return ins
                    V("pe", f, ["xT", "ident32"], ["P%d" % b])
                    V("act", lambda e, hf=hf, b=b, xi=xi, L=L: e.copy(xi[0:L, hf * 512:(hf + 1) * 512], PS[b][0:L, :]), ["P%d" % b], [ok])
                S.dma("sp", tile["ydst"](ci), xi[0:L, :], [ok], ())

        tiles = []
        for p in range(NPS):
            for t0 in range(0, TP, TT):
                n = min(TT, TP - t0)
                chunks = []
                for c0 in range(0, n, 128):
                    L = min(128, n - c0)
                    chunks.append((c0, L, p, (t0 == 0 and c0 == 0), (t0 + c0 + L == TP)))
                tiles.append(dict(kind="p", seq=p, n=n, segs=[(0, n, p)], chunks=chunks, last_tile=(t0 + n == TP),
                                  first_tile=(p == 0 and t0 == 0),
                                  xsrc=(lambda ci, p=p, t0=t0, chunks=chunks: x_prompt[p, t0 + chunks[ci][0]:t0 + chunks[ci][0] + chunks[ci][1], :]),
                                  ydst=(lambda ci, p=p, t0=t0, chunks=chunks: O["p_y"][p, t0 + chunks[ci][0]:t0 + chunks[ci][0] + chunks[ci][1], :]),
                                  state_init=state_init_prompt, state_store=state_store("p")))
        if NSS > 0:
            chunks = [(i * TS, TS, i, True, True) for i in range(NSS)]
            tiles.append(dict(kind="s", seq=None, n=NSS * TS, segs=[(i * TS, TS, i) for i in range(NSS)], chunks=chunks, last_tile=True,
                              first_tile=False, convin=convin, convout=convout,
                              xsrc=(lambda ci: x_sample[ci, :, :]), ydst=(lambda ci: O["s_y"][ci, :, :]),
                              state_init=state_init_sample, state_store=state_store("s")))
        for tile in tiles:
            if tile["kind"] == "p" and tile["chunks"][0][3]:
                for l in range(2):
                    V("pool", lambda e, l=l: e.memset(stt[l]["conv"], 0.0), (), ["conv%d" % l])
            run_tile(tile)
        S.emit()
    return nc, DBG


OUT_ORDER = ["_y", "_conv", "_ssd", "_mc", "_mn", "_mm", "_hg"]
_CACHE = {}


def kernel(**inputs):
    NC = 8
    x_prompt = np.ascontiguousarray(inputs["x_prompt"], dtype=np.float32)
    x_sample = np.ascontiguousarray(inputs["x_sample"], dtype=np.float32)
    B, T, _ = x_prompt.shape
    BS = x_sample.shape[0]
    NPS, NSS = B // NC, BS // NC
    key = (NPS, T, NSS)
    if key not in _CACHE:
        _CACHE[key] = build(NPS, T, NSS)[0]
    nc = _CACHE[key]
    wnames = ["norm_mix", "w_in", "conv_w", "conv_b", "dt_bias", "a_log", "d_skip", "ssd_gain", "ml_bi", "ml_bf", "ml_gain",
              "hg_lb", "hg_gain", "w_out", "norm_ffn", "w_ffn_in", "w_ffn_out", "norm_final"]
    snames = ["state_conv", "state_ssd", "state_mlstm_c", "state_mlstm_n", "state_mlstm_m", "state_hgrn"]
    in_maps = []
    for c in range(NC):
        m = {"x_prompt": x_prompt[c * NPS:(c + 1) * NPS], "x_sample": x_sample[c * NSS:(c + 1) * NSS]}
        for s in snames:
            m[s] = np.ascontiguousarray(np.asarray(inputs[s], dtype=np.float32)[:, c * NSS:(c + 1) * NSS])
        for w in wnames:
            m[w] = np.ascontiguousarray(inputs[w], dtype=np.float32)
        in_maps.append(m)
    res = run_bass_kernel_spmd(nc, in_maps, core_ids=list(range(NC)))
    outs = []
    for pre in ("p", "s"):
        for nm in OUT_ORDER:
            ax = 0 if nm == "_y" else 1
            outs.append(np.concatenate([np.asarray(r[pre + nm]) for r in res.results], axis=ax))
    yp, pc, pssd, pmc, pmn, pmm, phg, ys, sc, sssd, smc, smn, smm, shg = outs
    return (yp, ys, pc, pssd, pmc, pmn, pmm, phg, sc, sssd, smc, smn, smm, shg)
```

```python
import contextlib
import numpy as np
import concourse.bass as bass
import concourse.mybir as mybir
from concourse.bass_utils import run_bass_kernel_spmd

F32 = mybir.dt.float32
BF16 = mybir.dt.bfloat16
AF = mybir.ActivationFunctionType
ALU = mybir.AluOpType
AX = mybir.AxisListType

D = 1024
DMIX = 2048
INC = 6680
DFF = 2816
EPS = 1e-6
NEG = -30000.0
ENGS = ("pe", "act", "dve", "pool", "sp")


class Op:
    __slots__ = ("eng", "idx", "fn", "dma", "deps", "sig", "semref")

    def __init__(self, eng, idx, fn, dma):
        self.eng = eng
        self.idx = idx
        self.fn = fn
        self.dma = dma
        self.deps = set()
        self.sig = False
        self.semref = None


class _Rec:
    def __init__(self):
        self.calls = []

    def __getattr__(self, name):
        def m(*a, **k):
            self.calls.append((name, a, k))
            return self
        return m


class Sched:
    def __init__(self, nc, n_dma_sems=20):
        self.nc = nc
        self.ops = {e: [] for e in ENGS}
        self.res = {}
        self.n_dma_sems = n_dma_sems
        self.cap = None

    def add(self, eng, fn, reads=(), writes=(), dma=False):
        rec = _Rec()
        fn(rec)
        if self.cap is not None:
            self.cap.append((eng, rec.calls, tuple(reads), tuple(writes), dma))
            return None
        return self.commit(eng, rec.calls, reads, writes, dma)

    def capture(self, f, *a):
        assert self.cap is None
        self.cap = []
        f(*a)
        out = self.cap
        self.cap = None
        return out

    def commit(self, eng, calls, reads, writes, dma):
        op = Op(eng, len(self.ops[eng]), calls, dma)
        deps = set()
        for r in reads:
            st = self.res.setdefault(r, [None, []])
            if st[0] is not None:
                deps.add(st[0])
            st[1].append(op)
        for w in writes:
            st = self.res.setdefault(w, [None, []])
            if st[0] is not None:
                deps.add(st[0])
            for rd in st[1]:
                deps.add(rd)
            st[0] = op
            st[1] = []
        deps.discard(op)
        op.deps = deps
        for d in deps:
            d.sig = True
        self.ops[eng].append(op)
        return op

    def dma(self, eng, out, in_, reads=(), writes=()):
        return self.add(eng, lambda e: e.dma_start(out=out, in_=in_), reads, writes, dma=True)

    def emit(self):
        nc = self.nc
        with contextlib.ExitStack() as st:
            csem = {e: st.enter_context(nc.semaphore("c_" + e)) for e in ENGS}
            dsem = {e: [st.enter_context(nc.semaphore("d_%s%d" % (e, i))) for i in range(self.n_dma_sems)]
                    for e in ("sp", "act", "pool")}
            for e in ENGS:
                cnt = 0
                dcnt = [0] * self.n_dma_sems
                k = 0
                for op in self.ops[e]:
                    if op.dma:
                        j = k % self.n_dma_sems
                        k += 1
                        prev = dcnt[j]
                        dcnt[j] += 16
                        op.semref = (dsem[e][j], dcnt[j], prev)
                    elif op.sig:
                        cnt += 1
                        op.semref = (csem[e], cnt, None)
            block = st.enter_context(nc.Block())

            def run(e, eng):
                waited = {}
                for op in self.ops[e]:
                    need = {}
                    for d in op.deps:
                        if d.semref is None:
                            continue
                        sem, val = d.semref[0], d.semref[1]
                        key = id(sem)
                        if need.get(key, (None, 0))[1] < val:
                            need[key] = (sem, val)
                    if op.dma:
                        sem, val, prev = op.semref
                        if prev > 0:
                            key = id(sem)
                            if need.get(key, (None, 0))[1] < prev:
                                need[key] = (sem, prev)
                    for key, (sem, val) in need.items():
                        if waited.get(key, 0) < val:
                            eng.wait_ge(sem, val)
                            waited[key] = val
                    ins = None
                    for name, a, k in op.fn:
                        ins = getattr(eng, name)(*a, **k)
                    if op.dma:
                        ins.then_inc(op.semref[0], 16)
                    elif op.sig:
                        ins.then_inc(op.semref[0], 1)
                if e == "sp":
                    for e2 in ("sp", "act", "pool"):
                        last = {}
                        for op in self.ops[e2]:
                            if op.dma:
                                last[id(op.semref[0])] = (op.semref[0], op.semref[1])
                        for sem, val in last.values():
                            eng.wait_ge(sem, val)

            block.tensor(lambda eng: run("pe", eng))
            block.scalar(lambda eng: run("act", eng))
            block.vector(lambda eng: run("dve", eng))
            block.gpsimd(lambda eng: run("pool", eng))
            block.sync(lambda eng: run("sp", eng))


class Arena:
    def __init__(self, nc, st, name, nbytes):
        self.words = nbytes // 4
        self.t = st.enter_context(nc.sbuf_tensor(name, [128, self.words], F32))
        self.off = 0
        self.name = name

    def alloc(self, shape, dt):
        n = int(np.prod(shape))
        words = n if dt == F32 else (n + 1) // 2
        assert self.off + words <= self.words, (self.name, self.off, words, self.words)
        v = self.t[:, self.off:self.off + words]
        self.off += words
        if dt == BF16:
            v = v.bitcast(BF16)[:, 0:n]
        if len(shape) == 2:
            v = v.rearrange("p (a b) -> p a b", a=shape[0])
        elif len(shape) == 3:
            v = v.rearrange("p (a b c) -> p a b c", a=shape[0], b=shape[1])
        return v


def build(NPS, TP, NSS, dbg=False, LAG=8):
    TS = 32
    TT = min(512, TP)
    nc = bass.Bass("TRN2", target_bir_lowering=False)
    din = lambda n, s: nc.dram_tensor(n, list(s), F32, kind="ExternalInput").ap()
    dout = lambda n, s: nc.dram_tensor(n, list(s), F32, kind="ExternalOutput").ap()
    x_prompt = din("x_prompt", [NPS, TP, D])
    x_sample = din("x_sample", [NSS, TS, D])
    state_conv = din("state_conv", [2, NSS, 3, 1536])
    state_ssd = din("state_ssd", [2, NSS, 16, 64, 128])
    state_mc = din("state_mlstm_c", [2, NSS, 4, 128, 128])
    state_mn = din("state_mlstm_n", [2, NSS, 4, 128])
    state_mm = din("state_mlstm_m", [2, NSS, 4])
    state_hg = din("state_hgrn", [2, NSS, 4, 128, 128])
    norm_mix = din("norm_mix", [2, D])
    w_in = din("w_in", [2, D, INC])
    conv_w = din("conv_w", [2, 4, 1536])
    conv_b = din("conv_b", [2, 1536])
    dt_bias = din("dt_bias", [2, 16])
    a_log = din("a_log", [2, 16])
    d_skip = din("d_skip", [2, 16])
    ssd_gain = din("ssd_gain", [2, 1024])
    ml_bi = din("ml_bi", [2, 4])
    ml_bf = din("ml_bf", [2, 4])
    ml_gain = din("ml_gain", [2, 512])
    hg_lb = din("hg_lb", [2, 512])
    hg_gain = din("hg_gain", [2, 512])
    w_out = din("w_out", [2, DMIX, D])
    norm_ffn = din("norm_ffn", [2, D])
    w_ffn_in = din("w_ffn_in", [2, D, 2 * DFF])
    w_ffn_out = din("w_ffn_out", [2, DFF, D])
    norm_final = din("norm_final", [D])

    w_in_b = nc.dram_tensor("w_in_b", [2, D, INC], BF16, kind="Internal").ap()
    w_out_b = nc.dram_tensor("w_out_b", [2, DMIX, D], BF16, kind="Internal").ap()
    w_ffn_in_b = nc.dram_tensor("w_ffn_in_b", [2, D, 2 * DFF], BF16, kind="Internal").ap()
    w_ffn_out_b = nc.dram_tensor("w_ffn_out_b", [2, DFF, D], BF16, kind="Internal").ap()
    O = {}
    for pre, nb_ in (("p", NPS), ("s", NSS)):
        O[pre + "_y"] = dout(pre + "_y", [nb_, TP if pre == "p" else TS, D])
        O[pre + "_conv"] = dout(pre + "_conv", [2, nb_, 3, 1536])
        O[pre + "_ssd"] = dout(pre + "_ssd", [2, nb_, 16, 64, 128])
        O[pre + "_mc"] = dout(pre + "_mc", [2, nb_, 4, 128, 128])
        O[pre + "_mn"] = dout(pre + "_mn", [2, nb_, 4, 128])
        O[pre + "_mm"] = dout(pre + "_mm", [2, nb_, 4])
        O[pre + "_hg"] = dout(pre + "_hg", [2, nb_, 4, 128, 128])
    DBG = {}

    st = contextlib.ExitStack()
    with st:
        S = Sched(nc)
        PS = [st.enter_context(nc.psum_tensor("ps%d" % i, [128, 512], F32)) for i in range(8)]
        psk = [0, 0, 0]
        pspool = [None]

        def nb():
            p = pspool[0]
            if p is None:
                i = psk[2] % 8
                psk[2] += 1
                return i
            i = 4 * p + (psk[p] % 4)
            psk[p] += 1
            return i

        A = Arena(nc, st, "arenaA", 106 * 1024)
        RB = Arena(nc, st, "arenaB", 101 * 1024)

        ident32 = A.alloc((128,), F32)
        identb = A.alloc((128,), BF16)
        onesb = A.alloc((128,), BF16)
        mask01b = A.alloc((128,), BF16)
        negmask4 = A.alloc((4, 128), BF16)
        ones32 = A.alloc((128,), F32)
        V = lambda eng, f, r=(), w=(): S.add(eng, f, r, w)
        V("pool", lambda e: e.memset(ident32, 0.0), (), ["ident32"])
        V("pool", lambda e: e.affine_select(out=ident32, in_=ident32, pattern=[[-1, 128]], compare_op=ALU.not_equal,
                                            fill=1.0, base=0, channel_multiplier=1), ["ident32"], ["ident32"])
        V("dve", lambda e: e.tensor_copy(identb, ident32), ["ident32"], ["identb"])
        V("dve", lambda e: e.memset(onesb, 1.0), (), ["onesb"])
        V("dve", lambda e: e.memset(ones32, 1.0), (), ["ones32"])
        V("pool", lambda e: e.memset(ones32, 1.0), ["ones32"], ["ones32"])
        m01f = A.alloc((128,), F32)
        V("pool", lambda e: e.memset(m01f, 1.0), (), ["m01f"])
        V("pool", lambda e: e.affine_select(out=m01f, in_=m01f, pattern=[[1, 128]], compare_op=ALU.is_ge,
                                            fill=0.0, base=0, channel_multiplier=-1), ["m01f"], ["m01f"])
        V("dve", lambda e: e.tensor_copy(mask01b, m01f), ["m01f"], ["mask01b"])
        for j in range(4):
            V("dve", lambda e, j=j: e.tensor_scalar(negmask4[:, j, :], m01f, 1.0, -NEG, ALU.subtract, ALU.mult),
              ["m01f"], ["negmask4"])
        def sel(nr, hh, L):
            return ident32[0:nr, hh:hh + 1].broadcast_to([nr, L])

        xT = A.alloc((8, TT), F32)
        hT = A.alloc((8, TT), BF16)
        mixT = A.alloc((16, TT), BF16)
        rstd = A.alloc((TT,), F32)
        NSLOT = 2
        wring = [A.alloc((8192,), BF16) for _ in range(NSLOT)]
        wk = [0]
        pv = [A.alloc((96,), F32) for _ in range(2)]
        pvf = A.alloc((8,), F32)
        rowbuf = A.alloc((128,), F32)
        hs = {}
        for l in range(2):
            hs[l] = dict(dtb=A.alloc((1,), F32), A=A.alloc((1,), F32), bi=A.alloc((1,), F32), nbf=A.alloc((1,), F32),
                         Dbc=A.alloc((16,), F32), lb=A.alloc((4,), F32), oml=A.alloc((4,), F32))
        stt = {}
        for l in range(2):
            stt[l] = dict(h32=A.alloc((1024,), F32), C32=A.alloc((4, 129), F32), S32=A.alloc((4, 128), F32),
                          m=A.alloc((1,), F32), conv=A.alloc((12, 3), F32))
        hTb = A.alloc((1024,), BF16)
        Cb = A.alloc((4, 129), BF16)
        Sb = A.alloc((4, 128), BF16)
        fence = A.alloc((4,), F32)
        w32a = A.alloc((1024,), F32)
        h0buf = A.alloc((8,), F32)
        s0buf = A.alloc((4,), F32)

        def load_rows(l):
            segs = [(norm_mix[l], 0, 8), (conv_w[l].rearrange("w c -> (w c)"), 8, 48), (conv_b[l], 56, 12), (ssd_gain[l], 68, 8),
                    (ml_gain[l], 76, 4), (hg_lb[l], 80, 4), (hg_gain[l], 84, 4), (norm_ffn[l], 88, 8)]
            for src, r0, n in segs:
                S.dma("sp", rowbuf[r0:r0 + n, :], src.rearrange("(a b) -> a b", b=128), (), ["rowbuf"])
            b = nb()
            V("pe", lambda e: e.transpose(PS[b][:, 0:96], rowbuf[0:96, :], ident32[0:96, 0:96]), ["rowbuf", "ident32"], ["P%d" % b])
            V("dve", lambda e: e.tensor_copy(pv[l], PS[b][:, 0:96]), ["P%d" % b], ["pv%d" % l])

        for l in range(2):
            load_rows(l)
            h = hs[l]
            S.dma("sp", h["dtb"][0:16, :], dt_bias[l].rearrange("(a b) -> a b", b=1), (), ["hs%d" % l])
            S.dma("sp", h["A"][0:16, :], a_log[l].rearrange("(a b) -> a b", b=1), (), ["hs%d" % l])
            S.dma("sp", h["bi"][0:4, :], ml_bi[l].rearrange("(a b) -> a b", b=1), (), ["hs%d" % l])
            S.dma("sp", h["nbf"][0:4, :], ml_bf[l].rearrange("(a b) -> a b", b=1), (), ["hs%d" % l])
            S.dma("sp", h["Dbc"], d_skip[l].rearrange("(a b) -> a b", a=1).broadcast_to([128, 16]), (), ["hs%d" % l])
            V("act", lambda e, h=h: e.activation(h["A"][0:16, :], h["A"][0:16, :], AF.Exp), ["hs%d" % l], ["hs%d" % l])
            V("dve", lambda e, h=h: e.tensor_scalar(h["A"][0:16, :], h["A"][0:16, :], -1.0, None, ALU.mult), ["hs%d" % l], ["hs%d" % l])
            V("dve", lambda e, h=h: e.tensor_scalar(h["nbf"][0:4, :], h["nbf"][0:4, :], -1.0, None, ALU.mult), ["hs%d" % l], ["hs%d" % l])
        S.dma("sp", rowbuf[0:8, :], norm_final.rearrange("(a b) -> a b", b=128), (), ["rowbuf"])
        b = nb()
        V("pe", lambda e: e.transpose(PS[b][:, 0:8], rowbuf[0:8, :], ident32[0:8, 0:8]), ["rowbuf", "ident32"], ["P%d" % b])
        V("dve", lambda e: e.tensor_copy(pvf, PS[b][:, 0:8]), ["P%d" % b], ["pvf"])
        V("dve", lambda e: e.memset(hs[0]["lb"], 0.0), (), ["hs0"])
        V("dve", lambda e: e.memset(hs[0]["oml"], 1.0), (), ["hs0"])
        V("dve", lambda e: e.tensor_tensor(hs[1]["lb"], pv[1][:, 80:84], pv[0][:, 80:84], ALU.subtract), ["pv0", "pv1"], ["hs1"])
        V("act", lambda e: e.activation(hs[1]["lb"], hs[1]["lb"], AF.Sigmoid), ["hs1"], ["hs1"])
        V("dve", lambda e: e.tensor_scalar(hs[1]["oml"], hs[1]["lb"], -1.0, 1.0, ALU.mult, ALU.add), ["hs1"], ["hs1"])

        converted = set()

        def wload(src3, src3f, a, c, wbkey):
            i = wk[0] % NSLOT
            wk[0] += 1
            v = wring[i][:, 0:a * c].rearrange("p (a c) -> p a c", a=a)
            if wbkey not in converted:
                converted.add(wbkey)
                S.dma("pool", v, src3f, (), ["w%d" % i])
                S.dma("sp", src3, v, ["w%d" % i], [wbkey])
            else:
                S.dma("sp", v, src3, [wbkey], ["w%d" % i])
            return v, "w%d" % i

        NCH = 4
        xp = RB.alloc((TT + 16,), F32)
        cacc = RB.alloc((TT,), F32)
        io_ssd = RB.alloc((8, 128), F32)
        io_conv = RB.alloc((1536,), F32)
        io_n = RB.alloc((128,), F32)
        convin = RB.alloc((4, 12, 3), F32)
        convout = RB.alloc((4, 12, 3), F32)
        CS = [dict(), dict()]
        c_sc1 = RB.alloc((1024,), F32)
        q0row = RB.alloc((512,), F32)
        for par in range(2):
            CS[par]["c_small"] = RB.alloc((256,), F32)
            CS[par]["c_sc"] = c_sc1
        rb0 = RB.off
        xbcT = RB.alloc((12, TT), BF16)
        zs = RB.alloc((NCH, 1024), BF16)
        rows = {}
        rows["dt"] = RB.alloc((TT,), F32)
        rows["a"] = RB.alloc((TT,), F32)
        for par in range(2):
            c = CS[par]
            c["rows"] = {}
            for k in ("acum", "nacum", "decend"):
                c["rows"][k] = RB.alloc((128,), F32)
            c["c_xsTM"] = RB.alloc((1024,), BF16)
            c["c_xdt"] = RB.alloc((1024,), BF16)
            c["c_xdec"] = RB.alloc((1024,), BF16)
            c["c_BTM"] = RB.alloc((256,), BF16)
            c["c_cbT"] = RB.alloc((2, 128), F32)
            c["c_dT"] = RB.alloc((16, 128), BF16)
            c["c_MT"] = RB.alloc((16, 128), BF16)
            c["c_t2"] = RB.alloc((1024,), BF16)
            c["c_y"] = RB.alloc((1024,), F32)
        rb_max = RB.off
        RB.off = rb0
        qT = RB.alloc((4, TT), BF16)
        kT = RB.alloc((4, TT), BF16)
        kTM = RB.alloc((NCH, 512), BF16)
        vTM = RB.alloc((NCH, 4 * 129), BF16)
        oTM = RB.alloc((NCH, 512), BF16)
        rows["ig"] = RB.alloc((TT,), F32)
        rows["lf"] = RB.alloc((TT,), F32)
        c = CS[0]
        for k in ("fcum", "u", "cmm", "ncmm", "winter", "emi", "wkr", "wk2"):
            c["rows"][k] = RB.alloc((128,), F32)
        c["c_wT"] = RB.alloc((4, 128), F32)
        c["c_SwT"] = RB.alloc((4, 128), BF16)
        c["c_kw"] = RB.alloc((4, 128), BF16)
        c["c_int"] = RB.alloc((4, 129), F32)
        c["c_hm"] = RB.alloc((4, 128), F32)
        gqT = RB.alloc((4, TT), BF16)
        gkT = RB.alloc((4, TT), BF16)
        gT = RB.alloc((4, TT), F32)
        gviTM = RB.alloc((NCH, 512), BF16)
        ggTM = RB.alloc((NCH, 512), BF16)
        c = CS[1]
        c["c_gcp"] = RB.alloc((4, 129), F32)
        c["c_e"] = RB.alloc((4, 128), F32)
        c["c_qref"] = RB.alloc((4, 128), BF16)
        c["c_qdec"] = RB.alloc((4, 128), BF16)
        c["c_kref"] = [RB.alloc((4, 128), BF16) for _ in range(4)]
        c["c_kend"] = RB.alloc((4, 128), BF16)
        c["c_kendTM"] = RB.alloc((4, 128), BF16)
        c["c_ATb"] = RB.alloc((4, 128), BF16)
        c["c_o"] = RB.alloc((4, 128), F32)
        rb_max = max(rb_max, RB.off)
        RB.off = rb0
        ff0 = RB.off
        actT = RB.alloc((22, TT), BF16)
        sqs = RB.alloc((8, TT), BF16)
        ostg = [RB.t[:, ff0:ff0 + 1024], RB.t[:, ff0 + 1024:ff0 + 2048]]
        xin = [CS[0]["c_y"], CS[1]["c_y"]]
        rb_max = max(rb_max, RB.off)
        RB.off = rb_max
        CK = ["c_xsTM", "c_xdt", "c_xdec", "c_BTM", "c_cbT", "c_dT", "c_MT", "c_t2", "c_y", "rows",
              "c_wT", "c_SwT", "c_kw", "c_int", "c_hm", "c_gcp", "c_e", "c_qref", "c_qdec", "kref0", "kref1", "kref2", "kref3",
              "c_kend", "c_kendTM", "c_ATb", "c_o"]
        ALLK = ["xbcT", "zs", "rowsT", "qT", "kT", "kTM", "vTM", "oTM", "gqT", "gkT", "gT", "gviTM", "ggTM", "actT", "sqsF", "ostg0", "ostg1", "xbcT_b", "sqsF_b"] + \
               ["p%d_%s" % (par, k) for par in range(2) for k in CK]

        def fence_all():
            V("dve", lambda e: e.memset(fence, 0.0), (), ALLK + ["fence"])

        def dbg_out(name, ap_sb, shape, reads):
            if not dbg:
                return
            d = dout("dbg_" + name, shape)
            DBG[name] = shape
            S.dma("pool", d, ap_sb, reads, ())

        def rmsnorm_fm(n, gain_cols, dst, sq_buf, sqkey):
            V("act", lambda e: e.activation(sq_buf[:, 0:4, 0:n], xT[:, 0:4, 0:n], AF.Square), ["xT"], [sqkey])
            V("dve", lambda e: e.tensor_tensor(sq_buf[:, 4:8, 0:n], xT[:, 4:8, 0:n], xT[:, 4:8, 0:n], ALU.mult), ["xT"], [sqkey + "_b"])
            b = nb()

            def f(e):
                for dc in range(8):
                    ins = e.matmul(PS[b][:, 0:n], onesb, sq_buf[:, dc, 0:n], start=(dc == 0), stop=(dc == 7))
                return ins
            V("pe", f, [sqkey, sqkey + "_b", "onesb"], ["P%d" % b])
            V("act", lambda e: e.activation(rstd[:, 0:n], PS[b][:, 0:n], AF.Ln, bias=EPS, scale=1.0 / D), ["P%d" % b], ["rstd"])
            V("act", lambda e: e.activation(rstd[:, 0:n], rstd[:, 0:n], AF.Exp, scale=-0.5), ["rstd"], ["rstd"])
            for dc in range(8):
                V("dve", lambda e, dc=dc: e.scalar_tensor_tensor(dst[:, dc, 0:n], xT[:, dc, 0:n], gain_cols[:, dc:dc + 1], rstd[:, 0:n],
                                                                  ALU.mult, ALU.mult), ["xT", "rstd"], ["hT"])

        def fm_proj(wv, wkey, c0, M, n, src, srckey, ndc):
            b = nb()

            def f(e):
                for dc in range(ndc):
                    ins = e.matmul(PS[b][0:M, 0:n], wv[:, dc, c0:c0 + M], src[:, dc, 0:n], start=(dc == 0), stop=(dc == ndc - 1))
                return ins
            V("pe", f, [wkey, srckey], ["P%d" % b])
            return b

        def tm_proj(wv, wkey, c0, ncol, t0, L):
            b = nb()

            def f(e):
                for dc in range(8):
                    ins = e.matmul(PS[b][0:L, 0:ncol], hT[:, dc, t0:t0 + L], wv[:, dc, c0:c0 + ncol], start=(dc == 0), stop=(dc == 7))
                return ins
            V("pe", f, [wkey, "hT"], ["P%d" % b])
            return b

        def bcast_rows(src_rows_ap, nrow, key_r, c_small, kp):
            b = nb()
            V("dve", lambda e: e.tensor_scalar(c_small[0:nrow, 108:108 + nrow], ident32[0:nrow, 0:nrow], src_rows_ap, None, ALU.mult),
              [key_r, "ident32"], [kp + "c_small_d"])
            V("pe", lambda e: e.matmul(PS[b][:, 0:nrow], ones32[0:nrow, :], c_small[0:nrow, 108:108 + nrow], start=True, stop=True),
              [kp + "c_small_d", "ones32"], ["P%d" % b])
            return b

        def one_chunk(l, tile, ci, ch, which, fn):
            pspool[0] = {"ssd": ci % 2, "ml": 0, "hg": 1}[which]
            try:
                one_chunk_(l, tile, ci, ch, which, fn)
            finally:
                pspool[0] = None

        def one_chunk_(l, tile, ci, ch, which, fn):
            if ch[3]:
                tile["state_init"](l, ch, which)
            fn(l, tile, ci, ch)
            if ch[4]:
                tile["state_store"](l, ch, which)

        def pe_segs(sl):
            return [[spec] for spec in sl]

        def interleave2(la, lb):
            A_, B_ = pe_segs(la), pe_segs(lb)
            i = j = 0
            while i < len(A_) or j < len(B_):
                if i < len(A_):
                    for spec in A_[i]:
                        S.commit(*spec)
                    i += 1
                want = len(B_) if i >= len(A_) else (i * len(B_)) // max(1, len(A_))
                while j < want:
                    for spec in B_[j]:
                        S.commit(*spec)
                    j += 1

        def pipelined(lists, l):
            skeys = set(["h32_%d" % l, "hTb", "C32_%d" % l, "Cb", "m%d" % l, "S32_%d" % l, "Sb", "io_ssd", "io_n", "io_conv"])

            def segs(sl):
                return [[spec] for spec in sl]
            parts = []
            for sl in lists:
                idx = len(sl)
                for i, spec in enumerate(sl):
                    if skeys.intersection(spec[2]) or skeys.intersection(spec[3]):
                        idx = i
                        break
                if LAG <= 0:
                    idx = 0
                parts.append((segs(sl[:idx]), segs(sl[idx:])))
            for seg in parts[0][0]:
                for spec in seg:
                    S.commit(*spec)
            for c in range(len(parts)):
                Bc = parts[c][1]
                An = parts[c + 1][0] if c + 1 < len(parts) else []
                i = j = 0
                while i < len(Bc) or j < len(An):
                    if i < len(Bc):
                        for spec in Bc[i]:
                            S.commit(*spec)
                        i += 1
                    want = len(An) if i >= len(Bc) else (i * len(An)) // max(1, len(Bc))
                    while j < want:
                        for spec in An[j]:
                            S.commit(*spec)
                        j += 1

        def ml_s0_prep(l):
            V("dve", lambda e: e.tensor_tensor(h0buf, xT[:, :, 0], pv[l][:, 0:8], ALU.mult), ["xT", "pv%d" % l], ["h0buf"])
            V("dve", lambda e: e.tensor_scalar(h0buf, h0buf, rstd[:, 0:1], None, ALU.mult), ["h0buf", "rstd"], ["h0buf"])

        def ml_s0_exact(l):
            bq, bk = nb(), nb()
            wbufs = [(w32a, ["w32a"]), (c_sc1, ["p0_c_sc", "p1_c_sc2"])]
            for dc in range(8):
                wb_, wk_ = wbufs[dc % 2]
                S.dma("sp", wb_, w_in[l, dc * 128:(dc + 1) * 128, 2576:3600], (), wk_)

                def f(e, dc=dc, wb_=wb_):
                    e.matmul(PS[bq][0:1, 0:512], h0buf[:, dc:dc + 1], wb_[:, 0:512], start=(dc == 0), stop=(dc == 7))
                    return e.matmul(PS[bk][0:1, 0:512], h0buf[:, dc:dc + 1], wb_[:, 512:1024], start=(dc == 0), stop=(dc == 7))
                V("pe", f, wk_ + ["h0buf"], ["P%d" % bq, "P%d" % bk])
            V("act", lambda e: e.copy(q0row[0:1, :], PS[bq][0:1, 0:512]), ["P%d" % bq], ["q0row"])
            V("dve", lambda e: e.tensor_tensor(q0row[0:1, :], q0row[0:1, :], PS[bk][0:1, 0:512], ALU.mult), ["q0row", "P%d" % bk], ["q0row"])
            V("dve", lambda e: e.tensor_reduce(s0buf[0:1, :], q0row[0:1, :].rearrange("p (h c) -> p h c", h=4), AX.X, ALU.add), ["q0row"], ["s0buf"])
            V("dve", lambda e: e.tensor_scalar(s0buf[0:1, :], s0buf[0:1, :], 128.0 ** -0.5, None, ALU.mult), ["s0buf"], ["s0buf"])

        def layer(l, tile):
            n = tile["n"]
            segs = tile["segs"]
            chunks = tile["chunks"]
            nseg = len(segs)
            ls = segs[0][1]
            P = pv[l]
            H = hs[l]
            ST = stt[l]
            w_in_v = w_in_b[l].rearrange("(dc p) c -> p dc c", p=128)
            w_in_f = w_in[l].rearrange("(dc p) c -> p dc c", p=128)
            fence_all()
            rmsnorm_fm(n, P[:, 0:8], hT, xbcT[:, 0:8, :], "xbcT")
            seq_start = (tile["kind"] == "p" and chunks[0][3])
            if seq_start:
                ml_s0_prep(l)
            wvz, wkeyz = wload(w_in_v[:, :, 0:1024], w_in_f[:, :, 0:1024], 8, 1024, "wb_in%d_0" % l)
            zjobs = [(ci, ch, hf) for ci, ch in enumerate(chunks) for hf in range(2)]

            def zjob(ci, ch, hf):
                (c0, L, seq, first, last) = ch
                b = tm_proj(wvz, wkeyz, hf * 512, 512, c0, L)
                V("act", lambda e: e.activation(zs[0:L, ci, hf * 512:(hf + 1) * 512], PS[b][0:L, :], AF.Silu), ["P%d" % b], ["zs"])

            def conv_chunk(b, cc):
                xp3 = xp[:, 0:nseg * (ls + 3)].rearrange("p (s t) -> p s t", s=nseg)
                if tile["kind"] == "p":
                    V("dve", lambda e: e.tensor_copy(xp3[:, 0, 0:3], ST["conv"][:, cc, :]), ["conv%d" % l], ["xp"])
                else:
                    V("dve", lambda e: e.tensor_copy(xp3[:, :, 0:3], convin[:, 0:nseg, cc, :]), ["convin"], ["xp"])
                V("act", lambda e: e.copy(xp3[:, :, 3:3 + ls], PS[b][:, 0:n].rearrange("p (s t) -> p s t", s=nseg)), ["P%d" % b], ["xp"])
                if tile["kind"] == "p":
                    V("dve", lambda e: e.tensor_copy(ST["conv"][:, cc, :], xp3[:, 0, ls:ls + 3]), ["xp"], ["conv%d" % l])
                else:
                    V("dve", lambda e: e.tensor_copy(convout[:, 0:nseg, cc, :], xp3[:, :, ls:ls + 3]), ["xp"], ["convout"])
                acc3 = cacc[:, 0:n].rearrange("p (s t) -> p s t", s=nseg)
                V("act", lambda e: e.activation(acc3, xp3[:, :, 0:ls], AF.Identity, scale=P[:, 8 + cc:9 + cc]), ["xp"], ["cacc"])
                for w in range(1, 4):
                    V("dve", lambda e, w=w: e.scalar_tensor_tensor(acc3, xp3[:, :, w:w + ls], P[:, 8 + w * 12 + cc:9 + w * 12 + cc], acc3,
                                                                    ALU.mult, ALU.add), ["xp", "cacc"], ["cacc"])
                V("act", lambda e: e.activation(xbcT[:, cc, 0:n], cacc[:, 0:n], AF.Silu, bias=P[:, 56 + cc:57 + cc]), ["cacc"], ["xbcT"])

            wv, wkey = wload(w_in_v[:, :, 1024:2048], w_in_f[:, :, 1024:2048], 8, 1024, "wb_in%d_1024" % l)
            for cc in range(8):
                b = fm_proj(wv, wkey, cc * 128, 128, n, hT, "hT", 8)
                conv_chunk(b, cc)
                if zjobs:
                    zjob(*zjobs.pop(0))
            while zjobs:
                zjob(*zjobs.pop(0))
            wv, wkey = wload(w_in_v[:, :, 2048:2576], w_in_f[:, :, 2048:2576], 8, 528, "wb_in%d_2048" % l)
            for cc in range(8, 12):
                b = fm_proj(wv, wkey, (cc - 8) * 128, 128, n, hT, "hT", 8)
                conv_chunk(b, cc)
            b = fm_proj(wv, wkey, 512, 16, n, hT, "hT", 8)
            R = rows
            V("act", lambda e: e.activation(R["dt"][0:16, 0:n], PS[b][0:16, 0:n], AF.Exp, bias=H["dtb"][0:16, :]), ["P%d" % b, "hs%d" % l], ["rowsT"])
            V("act", lambda e: e.activation(R["dt"][0:16, 0:n], R["dt"][0:16, 0:n], AF.Ln, bias=1.0), ["rowsT"], ["rowsT"])
            V("dve", lambda e: e.tensor_scalar(R["a"][0:16, 0:n], R["dt"][0:16, 0:n], H["A"][0:16, :], None, ALU.mult), ["rowsT", "hs%d" % l], ["rowsT"])
            pipelined([S.capture(one_chunk, l, tile, ci, ch, "ssd", ssd_chunk) for ci, ch in enumerate(chunks)], l)
            if dbg and l == 0 and tile.get("first_tile"):
                dbg_out("xbcT", xbcT[:, :, 0:n], [128, 12, n], ["xbcT"])
                dbg_out("hT", hT[:, :, 0:n], [128, 8, n], ["hT"])
                dbg_out("dtrow", R["dt"][0:16, 0:n], [16, n], ["rowsT"])

            fence_all()
            if seq_start:
                ml_s0_exact(l)
            wv, wkey = wload(w_in_v[:, :, 2576:3600], w_in_f[:, :, 2576:3600], 8, 1024, "wb_in%d_2576" % l)
            for hh in range(4):
                b = fm_proj(wv, wkey, hh * 128, 128, n, hT, "hT", 8)
                V("act", lambda e, b=b, hh=hh: e.copy(qT[:, hh, 0:n], PS[b][:, 0:n]), ["P%d" % b], ["qT"])
                b = fm_proj(wv, wkey, 512 + hh * 128, 128, n, hT, "hT", 8)
                V("act", lambda e, b=b, hh=hh: e.activation(kT[:, hh, 0:n], PS[b][:, 0:n], AF.Identity, scale=128.0 ** -0.5), ["P%d" % b], ["kT"])
            for ci, (c0, L, seq, first, last) in enumerate(chunks):
                b = tm_proj(wv, wkey, 512, 512, c0, L)
                V("act", lambda e, b=b, ci=ci, L=L: e.activation(kTM[0:L, ci, :], PS[b][0:L, :], AF.Identity, scale=128.0 ** -0.5), ["P%d" % b], ["kTM"])
            wv, wkey = wload(w_in_v[:, :, 3600:4120], w_in_f[:, :, 3600:4120], 8, 520, "wb_in%d_3600" % l)
            for ci, (c0, L, seq, first, last) in enumerate(chunks):
                b = tm_proj(wv, wkey, 0, 512, c0, L)
                v4 = vTM[0:L, ci, :].rearrange("p (h c) -> p h c", h=4)
                V("act", lambda e, b=b, v4=v4, L=L: e.copy(v4[:, :, 0:128], PS[b][0:L, :].rearrange("p (h c) -> p h c", h=4)), ["P%d" % b], ["vTM"])
                V("dve", lambda e, v4=v4: e.memset(v4[:, :, 128:129], 1.0), (), ["vTM"])
            b = fm_proj(wv, wkey, 512, 4, n, hT, "hT", 8)
            V("act", lambda e, b=b: e.activation(R["ig"][0:4, 0:n], PS[b][0:4, 0:n], AF.Identity, bias=H["bi"][0:4, :]), ["P%d" % b, "hs%d" % l], ["rowsT"])
            b = fm_proj(wv, wkey, 516, 4, n, hT, "hT", 8)
            V("act", lambda e, b=b: e.activation(R["lf"][0:4, 0:n], PS[b][0:4, 0:n], AF.Exp, bias=H["nbf"][0:4, :], scale=-1.0), ["P%d" % b, "hs%d" % l], ["rowsT"])
            V("act", lambda e: e.activation(R["lf"][0:4, 0:n], R["lf"][0:4, 0:n], AF.Ln, bias=1.0), ["rowsT"], ["rowsT"])
            V("dve", lambda e: e.tensor_scalar(R["lf"][0:4, 0:n], R["lf"][0:4, 0:n], -1.0, None, ALU.mult), ["rowsT"], ["rowsT"])
            wv, wkey = wload(w_in_v[:, :, 4120:4632], w_in_f[:, :, 4120:4632], 8, 512, "wb_in%d_4120" % l)
            for ci, (c0, L, seq, first, last) in enumerate(chunks):
                b = tm_proj(wv, wkey, 0, 512, c0, L)
                V("act", lambda e, b=b, ci=ci, L=L: e.activation(oTM[0:L, ci, :], PS[b][0:L, :], AF.Sigmoid), ["P%d" % b], ["oTM"])

            for a in range(4):
                V("dve", lambda e, a=a: e.memset(CS[1]["c_kref"][a], 0.0), (), ["p1_kref%d" % a])
            wv, wkey = wload(w_in_v[:, :, 4632:5656], w_in_f[:, :, 4632:5656], 8, 1024, "wb_in%d_4632" % l)
            for hh in range(4):
                b = fm_proj(wv, wkey, hh * 128, 128, n, hT, "hT", 8)
                V("act", lambda e, b=b, hh=hh: e.activation(gqT[:, hh, 0:n], PS[b][:, 0:n], AF.Silu), ["P%d" % b], ["gqT"])
            for hh in range(4):
                b = fm_proj(wv, wkey, 512 + hh * 128, 128, n, hT, "hT", 8)
                V("act", lambda e, b=b: e.activation(cacc[:, 0:n], PS[b][:, 0:n], AF.Sigmoid), ["P%d" % b], ["cacc"])
                V("dve", lambda e, hh=hh: e.tensor_scalar(cacc[:, 0:n], cacc[:, 0:n], H["oml"][:, hh:hh + 1], H["lb"][:, hh:hh + 1], ALU.mult, ALU.add),
                  ["cacc", "hs%d" % l], ["cacc"])
                V("dve", lambda e, hh=hh: e.tensor_scalar(gkT[:, hh, 0:n], cacc[:, 0:n], -1.0, 1.0, ALU.mult, ALU.add), ["cacc"], ["gkT"])
                V("act", lambda e, hh=hh: e.activation(gT[:, hh, 0:n], cacc[:, 0:n], AF.Ln), ["cacc"], ["gT"])
            wv, wkey = wload(w_in_v[:, :, 5656:6680], w_in_f[:, :, 5656:6680], 8, 1024, "wb_in%d_5656" % l)
            for ci, (c0, L, seq, first, last) in enumerate(chunks):
                b = tm_proj(wv, wkey, 0, 512, c0, L)
                V("act", lambda e, b=b, ci=ci, L=L: e.copy(gviTM[0:L, ci, :], PS[b][0:L, :]), ["P%d" % b], ["gviTM"])
                b = tm_proj(wv, wkey, 512, 512, c0, L)
                V("act", lambda e, b=b, ci=ci, L=L: e.activation(ggTM[0:L, ci, :], PS[b][0:L, :], AF.Silu), ["P%d" % b], ["ggTM"])
            for ci, ch in enumerate(chunks):
                la = S.capture(one_chunk, l, tile, ci, ch, "ml", mlstm_chunk)
                lb = S.capture(one_chunk, l, tile, ci, ch, "hg", hgrn_chunk)
                interleave2(la, lb)
            if dbg and l == 0 and tile.get("first_tile"):
                dbg_out("mixT", mixT[:, :, 0:n], [128, 16, n], ["mixT"])

            w_out_v = w_out_b[l].rearrange("(cc p) d -> p cc d", p=128)
            w_out_f = w_out[l].rearrange("(cc p) d -> p cc d", p=128)
            for hf in range(2):
                wv, wkey = wload(w_out_v[:, :, hf * 512:(hf + 1) * 512], w_out_f[:, :, hf * 512:(hf + 1) * 512], 16, 512, "wb_out%d_%d" % (l, hf))
                for j in range(4):
                    b = fm_proj(wv, wkey, j * 128, 128, n, mixT, "mixT", 16)
                    dc = hf * 4 + j
                    V("dve", lambda e, b=b, dc=dc: e.tensor_tensor(xT[:, dc, 0:n], xT[:, dc, 0:n], PS[b][:, 0:n], ALU.add), ["P%d" % b, "xT"], ["xT"])
            fence_all()
            rmsnorm_fm(n, P[:, 88:96], hT, sqs, "sqsF")
            w1 = w_ffn_in_b[l].rearrange("(dc p) c -> p dc c", p=128)
            w1f = w_ffn_in[l].rearrange("(dc p) c -> p dc c", p=128)
            for t in range(6):
                c0 = t * 1024
                nc_ = min(1024, 2 * DFF - c0)
                wv, wkey = wload(w1[:, :, c0:c0 + nc_], w1f[:, :, c0:c0 + nc_], 8, nc_, "wb_f1%d_%d" % (l, c0))
                for j in range(nc_ // 128):
                    fch = (c0 // 128) + j
                    b = fm_proj(wv, wkey, j * 128, 128, n, hT, "hT", 8)
                    if fch < 22:
                        V("act", lambda e, b=b, fch=fch: e.activation(actT[:, fch, 0:n], PS[b][:, 0:n], AF.Silu), ["P%d" % b], ["actT"])
                    else:
                        f2 = fch - 22
                        V("dve", lambda e, b=b, f2=f2: e.tensor_tensor(actT[:, f2, 0:n], actT[:, f2, 0:n], PS[b][:, 0:n], ALU.mult),
                          ["P%d" % b, "actT"], ["actT"])
            w2 = w_ffn_out_b[l].rearrange("(fc p) d -> p fc d", p=128)
            w2f = w_ffn_out[l].rearrange("(fc p) d -> p fc d", p=128)
            for t in range(4):
                wv, wkey = wload(w2[:, :, t * 256:(t + 1) * 256], w2f[:, :, t * 256:(t + 1) * 256], 22, 256, "wb_f2%d_%d" % (l, t))
                for j in range(2):
                    b = fm_proj(wv, wkey, j * 128, 128, n, actT, "actT", 22)
                    dc = t * 2 + j
                    V("dve", lambda e, b=b, dc=dc: e.tensor_tensor(xT[:, dc, 0:n], xT[:, dc, 0:n], PS[b][:, 0:n], ALU.add), ["P%d" % b, "xT"], ["xT"])

        def ssd_chunk(l, tile, ci, ch):
            (c0, L, seq, first, last) = ch
            H = hs[l]
            ST = stt[l]
            P = pv[l]
            par = ci % 2
            kp = "p%d_" % par
            c = CS[par]
            R = dict(rows)
            R.update(c["rows"])
            sm = c["c_small"]
            c_sc = c["c_sc"]
            c_xsTM = c["c_xsTM"]
            c_xdt = c["c_xdt"]
            c_xdec = c["c_xdec"]
            c_BTM = c["c_BTM"]
            c_cbT = c["c_cbT"]
            c_dT = c["c_dT"]
            c_MT = c["c_MT"]
            c_t2 = c["c_t2"]
            c_y = c["c_y"]
            V("dve", lambda e: e.tensor_tensor_scan(R["acum"][0:16, 0:L], ones32[0:16, 0:L], R["a"][0:16, c0:c0 + L], 0.0, ALU.mult, ALU.add),
              [kp + "rows", "rowsT", "ones32"], [kp + "rows"])
            hl = sm[0:16, 128:256].bitcast(BF16)
            nhl = R["nacum"][0:16, :].bitcast(BF16)
            a_hi, a_lo, n_hi, n_lo = hl[:, 0:L], hl[:, 128:128 + L], nhl[:, 0:L], nhl[:, 128:128 + L]
            V("dve", lambda e: e.tensor_copy(a_hi, R["acum"][0:16, 0:L]), [kp + "rows"], [kp + "c_small"])
            V("dve", lambda e: e.tensor_tensor(R["decend"][0:16, 0:L], R["acum"][0:16, 0:L], a_hi, ALU.subtract), [kp + "rows", kp + "c_small"], [kp + "rows"])
            V("dve", lambda e: e.tensor_copy(a_lo, R["decend"][0:16, 0:L]), [kp + "rows"], [kp + "c_small"])
            V("dve", lambda e: e.tensor_scalar(n_hi, a_hi, -1.0, None, ALU.mult), [kp + "c_small"], [kp + "rows"])
            V("dve", lambda e: e.tensor_scalar(n_lo, a_lo, -1.0, None, ALU.mult), [kp + "c_small"], [kp + "rows"])
            V("act", lambda e: e.activation(R["decend"][0:16, 0:L], R["acum"][0:16, 0:L], AF.Exp,
                                            bias=R["acum"][0:16, L - 1:L], scale=-1.0), [kp + "rows"], [kp + "rows"])
            b = nb()

            def f(e):
                e.transpose(PS[b][0:L, 0:16], R["dt"][0:16, c0:c0 + L], ident32[0:16, 0:16])
                e.transpose(PS[b][0:L, 16:32], R["acum"][0:16, 0:L], ident32[0:16, 0:16])
                return e.transpose(PS[b][0:L, 32:48], R["decend"][0:16, 0:L], ident32[0:16, 0:16])
            V("pe", f, [kp + "rows", "rowsT", "ident32"], ["P%d" % b])
            V("dve", lambda e: e.tensor_copy(sm[0:L, 0:16], PS[b][0:L, 0:16]), ["P%d" % b], [kp + "c_small"])
            V("act", lambda e: e.activation(sm[0:L, 16:32], PS[b][0:L, 16:32], AF.Exp), ["P%d" % b], [kp + "c_small"])
            V("dve", lambda e: e.tensor_tensor(sm[0:L, 32:48], PS[b][0:L, 32:48], sm[0:L, 0:16], ALU.mult), ["P%d" % b, kp + "c_small"], [kp + "c_small"])
            bb = bcast_rows(R["acum"][0:16, L - 1:L], 16, kp + "rows", sm, kp)
            V("act", lambda e: e.activation(sm[:, 48:64], PS[bb][:, 0:16], AF.Exp), ["P%d" % bb], [kp + "c_small"])
            b1 = nb()
            pb1 = PS[b1].bitcast(BF16)

            def f(e):
                for cc in range(8):
                    ins = e.transpose(pb1[0:L, cc * 128:(cc + 1) * 128], xbcT[:, cc, c0:c0 + L], identb)
                return ins
            V("pe", f, ["xbcT", "identb"], ["P%d" % b1])
            b2 = nb()
            pb2 = PS[b2].bitcast(BF16)

            def f(e):
                for cc in range(2):
                    ins = e.transpose(pb2[0:L, cc * 128:(cc + 1) * 128], xbcT[:, 8 + cc, c0:c0 + L], identb)
                return ins
            V("pe", f, ["xbcT", "identb"], ["P%d" % b2])
            V("act", lambda e: e.copy(c_xsTM[0:L, :], pb1[0:L, :]), ["P%d" % b1], [kp + "c_xsTM"])
            V("act", lambda e: e.copy(c_BTM[0:L, :], pb2[0:L, 0:256]), ["P%d" % b2], [kp + "c_BTM"])
            xs3 = c_xsTM[0:L, :].rearrange("p (h c) -> p h c", h=16)
            V("dve", lambda e: e.tensor_tensor(c_xdt[0:L, :].rearrange("p (h c) -> p h c", h=16), xs3,
                                               sm[0:L, 0:16].unsqueeze(2).broadcast_to([L, 16, 64]), ALU.mult), [kp + "c_xsTM", kp + "c_small"], [kp + "c_xdt"])
            V("dve", lambda e: e.tensor_tensor(c_xdec[0:L, :].rearrange("p (h c) -> p h c", h=16), xs3,
                                               sm[0:L, 32:48].unsqueeze(2).broadcast_to([L, 16, 64]), ALU.mult), [kp + "c_xsTM", kp + "c_small"], [kp + "c_xdec"])
            b = nb()

            def f(e):
                for g in range(2):
                    ins = e.matmul(PS[b][0:L, g * 128:g * 128 + L], xbcT[:, 8 + g, c0:c0 + L], xbcT[:, 10 + g, c0:c0 + L], start=True, stop=True)
                return ins
            V("pe", f, ["xbcT"], ["P%d" % b])
            V("act", lambda e: e.copy(c_cbT[0:L, :, 0:L], PS[b][0:L, 0:256].rearrange("p (g c) -> p g c", g=2)[:, :, 0:L]), ["P%d" % b], [kp + "c_cbT"])
            for q4 in range(4):
                b = nb()

                def f(e, q4=q4, b=b):
                    if L == 128:
                        rhs4 = identb[0:16, 4 * q4:4 * q4 + 4].unsqueeze(2).broadcast_to([16, 4, 128])
                        e.matmul(PS[b][0:L, :], n_hi, rhs4, start=True, stop=False)
                        e.matmul(PS[b][0:L, :], n_lo, rhs4, start=False, stop=False)
                        e.matmul(PS[b][0:L, :], identb[0:L, 0:L], negmask4[0:L, :, :], start=False, stop=False)
                    for j in range(4):
                        hh = q4 * 4 + j
                        o = PS[b][0:L, j * 128:j * 128 + L]
                        selb = identb[0:16, hh:hh + 1].broadcast_to([16, L])
                        if L != 128:
                            e.matmul(o, n_hi, selb, start=True, stop=False)
                            e.matmul(o, n_lo, selb, start=False, stop=False)
                            e.matmul(o, identb[0:L, 0:L], negmask4[0:L, 0, 0:L], start=False, stop=False)
                        e.matmul(o, selb, a_hi, start=False, stop=False)
                        ins = e.matmul(o, selb, a_lo, start=False, stop=(j == 3 or L != 128))
                    return ins
                V("pe", f, [kp + "rows", kp + "c_small", "identb", "negmask4"], ["P%d" % b])
                V("act", lambda e, q4=q4, b=b: e.activation(c_dT[0:L, q4 * 4:(q4 + 1) * 4, 0:L],
                                                            PS[b][0:L, :].rearrange("p (j c) -> p j c", j=4)[:, :, 0:L], AF.Exp), ["P%d" % b], [kp + "c_dT"])
            for g in range(2):
                V("dve", lambda e, g=g: e.tensor_tensor(c_MT[0:L, g * 8:(g + 1) * 8, 0:L], c_dT[0:L, g * 8:(g + 1) * 8, 0:L],
                                                        c_cbT[0:L, g:g + 1, 0:L].broadcast_to([L, 8, L]), ALU.mult), [kp + "c_dT", kp + "c_cbT"], [kp + "c_MT"])
            V("act", lambda e: e.copy(hTb, ST["h32"]), ["h32_%d" % l], ["hTb"])
            byi = [nb(), nb()]
            bys = [nb(), nb()]
            for g in range(2):
                def f(e, g=g):
                    for j in range(8):
                        hh = g * 8 + j
                        ins = e.matmul(PS[byi[g]][0:L, j * 64:(j + 1) * 64], c_MT[0:L, hh, 0:L], c_xdt[0:L, hh * 64:(hh + 1) * 64], start=True, stop=True)
                    return ins
                V("pe", f, [kp + "c_MT", kp + "c_xdt"], ["P%d" % byi[g]])
                V("pe", lambda e, g=g: e.matmul(PS[bys[g]][0:L, :], xbcT[:, 10 + g, c0:c0 + L], hTb[:, g * 512:(g + 1) * 512], start=True, stop=True),
                  ["xbcT", "hTb"], ["P%d" % bys[g]])
            V("dve", lambda e: e.tensor_tensor(c_t2[0:L, :].rearrange("p (h c) -> p h c", h=16), xs3,
                                               H["Dbc"][0:L, :].unsqueeze(2).broadcast_to([L, 16, 64]), ALU.mult), [kp + "c_xsTM", "hs%d" % l], [kp + "c_t2"])
            for g in range(2):
                sl = slice(g * 512, (g + 1) * 512)
                V("dve", lambda e, g=g, sl=sl: e.tensor_tensor(c_y[0:L, sl].rearrange("p (h c) -> p h c", h=8),
                                                               PS[bys[g]][0:L, :].rearrange("p (h c) -> p h c", h=8),
                                                               sm[0:L, 16 + g * 8:24 + g * 8].unsqueeze(2).broadcast_to([L, 8, 64]), ALU.mult),
                  ["P%d" % bys[g], kp + "c_small"], [kp + "c_y"])
                V("dve", lambda e, g=g, sl=sl: e.tensor_tensor(c_y[0:L, sl], c_y[0:L, sl], PS[byi[g]][0:L, :], ALU.add), ["P%d" % byi[g], kp + "c_y"], [kp + "c_y"])
            V("dve", lambda e: e.tensor_tensor(c_y[0:L, :], c_y[0:L, :], c_t2[0:L, :], ALU.add), [kp + "c_y", kp + "c_t2"], [kp + "c_y"])
            V("dve", lambda e: e.tensor_tensor(c_y[0:L, :], c_y[0:L, :], zs[0:L, ci, :], ALU.mult), [kp + "c_y", "zs"], [kp + "c_y"])
            V("act", lambda e: e.activation(c_t2[0:L, :], c_y[0:L, :], AF.Square, accum_out=sm[0:L, 64:65]), [kp + "c_y"], [kp + "c_t2", kp + "c_small"])
            V("act", lambda e: e.activation(sm[0:L, 64:65], sm[0:L, 64:65], AF.Ln, bias=EPS, scale=1.0 / 1024), [kp + "c_small"], [kp + "c_small"])
            V("act", lambda e: e.activation(sm[0:L, 64:65], sm[0:L, 64:65], AF.Exp, scale=-0.5), [kp + "c_small"], [kp + "c_small"])
            V("act", lambda e: e.activation(c_y[0:L, :], c_y[0:L, :], AF.Identity, scale=sm[0:L, 64:65]), [kp + "c_y", kp + "c_small"], [kp + "c_y"])
            for hf in range(2):
                b = nb()

                def f(e, hf=hf, b=b):
                    for j in range(4):
                        cc = hf * 4 + j
                        ins = e.transpose(PS[b][:, j * 128:j * 128 + L], c_y[0:L, cc * 128:(cc + 1) * 128], ident32[0:L, 0:L])
                    return ins
                V("pe", f, [kp + "c_y", "ident32"], ["P%d" % b])
                if hf == 0:
                    V("dve", lambda e, b=b: e.tensor_tensor(mixT[:, 0:4, c0:c0 + L], PS[b][:, :].rearrange("p (j c) -> p j c", j=4)[:, :, 0:L],
                                                            pv[l][:, 68:72].unsqueeze(2).broadcast_to([128, 4, L]), ALU.mult), ["P%d" % b, "pv%d" % l], ["mixT"])
                else:
                    for j in range(4):
                        cc = hf * 4 + j
                        V("act", lambda e, b=b, j=j, cc=cc: e.activation(mixT[:, cc, c0:c0 + L], PS[b][:, j * 128:j * 128 + L], AF.Copy,
                                                                         scale=pv[l][:, 68 + cc:69 + cc]), ["P%d" % b], ["mixT"])
            bu = [nb(), nb()]
            for g in range(2):
                V("pe", lambda e, g=g: e.matmul(PS[bu[g]][:, :], c_BTM[0:L, g * 128:(g + 1) * 128], c_xdec[0:L, g * 512:(g + 1) * 512], start=True, stop=True),
                  [kp + "c_BTM", kp + "c_xdec"], ["P%d" % bu[g]])
            V("dve", lambda e: e.tensor_tensor(ST["h32"].rearrange("p (h c) -> p h c", h=16), ST["h32"].rearrange("p (h c) -> p h c", h=16),
                                               sm[:, 48:64].unsqueeze(2).broadcast_to([128, 16, 64]), ALU.mult), ["h32_%d" % l, kp + "c_small", "hTb"], ["h32_%d" % l])
            for g in range(2):
                sl = slice(g * 512, (g + 1) * 512)
                V("dve", lambda e, g=g, sl=sl: e.tensor_tensor(ST["h32"][:, sl], ST["h32"][:, sl], PS[bu[g]][:, :], ALU.add),
                  ["h32_%d" % l, "P%d" % bu[g]], ["h32_%d" % l])

        def mlstm_chunk(l, tile, ci, ch):
            (c0, L, seq, first, last) = ch
            H = hs[l]
            ST = stt[l]
            cs = slice(c0, c0 + L)
            par = 0
            kp = "p%d_" % par
            c = CS[par]
            R = dict(rows)
            R.update(c["rows"])
            sm = c["c_small"]
            c_sc = c["c_sc"]
            c_wT = c["c_wT"]
            c_SwT = c["c_SwT"]
            c_kw = c["c_kw"]
            c_int = c["c_int"]
            c_hm = c["c_hm"]
            m0 = ST["m"][0:4, :]
            V("dve", lambda e: e.tensor_tensor_scan(R["fcum"][0:4, 0:L], ones32[0:4, 0:L], R["lf"][0:4, cs], 0.0, ALU.mult, ALU.add), [kp + "rows", "rowsT", "ones32"], [kp + "rows"])
            V("dve", lambda e: e.tensor_tensor(R["u"][0:4, 0:L], R["ig"][0:4, cs], R["fcum"][0:4, 0:L], ALU.subtract), [kp + "rows", "rowsT"], [kp + "rows"])
            V("dve", lambda e: e.tensor_tensor_scan(R["cmm"][0:4, 0:L], ones32[0:4, 0:L], R["u"][0:4, 0:L], m0, ALU.mult, ALU.max), [kp + "rows", "ones32", "m%d" % l], [kp + "rows"])
            V("dve", lambda e: e.tensor_scalar(R["ncmm"][0:4, 0:L], R["cmm"][0:4, 0:L], -1.0, None, ALU.mult), [kp + "rows"], [kp + "rows"])
            hl = sm[0:4, 128:256].bitcast(BF16)
            nhl = R["wkr"][0:4, :].bitcast(BF16)
            u_hi, u_lo, c_hi, c_lo = hl[:, 0:L], hl[:, 128:128 + L], nhl[:, 0:L], nhl[:, 128:128 + L]
            V("act", lambda e: e.activation(R["winter"][0:4, 0:L], R["cmm"][0:4, 0:L], AF.Exp, bias=m0, scale=-1.0), [kp + "rows", "m%d" % l], [kp + "rows"])
            V("dve", lambda e: e.tensor_tensor(R["emi"][0:4, 0:L], R["fcum"][0:4, 0:L], R["cmm"][0:4, 0:L], ALU.add), [kp + "rows"], [kp + "rows"])
            V("act", lambda e: e.activation(R["emi"][0:4, 0:L], R["emi"][0:4, 0:L], AF.Exp, scale=-1.0), [kp + "rows"], [kp + "rows"])
            V("act", lambda e: e.activation(R["wk2"][0:4, 0:L], R["u"][0:4, 0:L], AF.Exp, bias=R["ncmm"][0:4, L - 1:L]), [kp + "rows"], [kp + "rows"])
            V("act", lambda e: e.activation(sm[0:4, 70:71], R["cmm"][0:4, L - 1:L], AF.Exp, bias=m0, scale=-1.0), [kp + "rows", "m%d" % l], [kp + "c_small"])
            V("dve", lambda e: e.tensor_tensor(ST["m"][0:4, :], R["fcum"][0:4, L - 1:L], R["cmm"][0:4, L - 1:L], ALU.add),
              [kp + "rows"], ["m%d" % l])
            b = nb()

            def f(e):
                e.transpose(PS[b][0:L, 0:4], R["winter"][0:4, 0:L], ident32[0:4, 0:4])
                e.transpose(PS[b][0:L, 4:8], R["emi"][0:4, 0:L], ident32[0:4, 0:4])
                return e.transpose(PS[b][0:L, 8:12], R["wk2"][0:4, 0:L], ident32[0:4, 0:4])
            V("pe", f, [kp + "rows", "ident32"], ["P%d" % b])
            V("dve", lambda e: e.tensor_copy(sm[0:L, 72:84], PS[b][0:L, 0:12]), ["P%d" % b], [kp + "c_small"])
            b = nb()

            V("dve", lambda e: e.tensor_copy(u_hi, R["u"][0:4, 0:L]), [kp + "rows"], [kp + "c_small"])
            V("dve", lambda e: e.tensor_tensor(R["fcum"][0:4, 0:L], R["u"][0:4, 0:L], u_hi, ALU.subtract), [kp + "rows", kp + "c_small"], [kp + "rows"])
            V("dve", lambda e: e.tensor_copy(u_lo, R["fcum"][0:4, 0:L]), [kp + "rows"], [kp + "c_small"])
            V("dve", lambda e: e.tensor_copy(c_hi, R["ncmm"][0:4, 0:L]), [kp + "rows"], [kp + "rows"])
            V("dve", lambda e: e.tensor_tensor(R["fcum"][0:4, 0:L], R["ncmm"][0:4, 0:L], c_hi, ALU.subtract), [kp + "rows"], [kp + "rows"])
            V("dve", lambda e: e.tensor_copy(c_lo, R["fcum"][0:4, 0:L]), [kp + "rows"], [kp + "rows"])

            def f(e):
                if L == 128:
                    rhs4 = identb[0:4, 0:4].unsqueeze(2).broadcast_to([4, 4, 128])
                    e.matmul(PS[b][0:L, :], u_hi, rhs4, start=True, stop=False)
                    e.matmul(PS[b][0:L, :], u_lo, rhs4, start=False, stop=False)
                    e.matmul(PS[b][0:L, :], identb[0:L, 0:L], negmask4[0:L, :, :], start=False, stop=False)
                for hh in range(4):
                    o = PS[b][0:L, hh * 128:hh * 128 + L]
                    selb = identb[0:4, hh:hh + 1].broadcast_to([4, L])
                    if L != 128:
                        e.matmul(o, u_hi, selb, start=True, stop=False)
                        e.matmul(o, u_lo, selb, start=False, stop=False)
                        e.matmul(o, identb[0:L, 0:L], negmask4[0:L, 0, 0:L], start=False, stop=False)
                    e.matmul(o, selb, c_hi, start=False, stop=False)
                    ins = e.matmul(o, selb, c_lo, start=False, stop=(hh == 3 or L != 128))
                return ins
            V("pe", f, [kp + "rows", kp + "c_small", "identb", "negmask4"], ["P%d" % b])
            V("act", lambda e: e.activation(c_wT[0:L, :, 0:L], PS[b][0:L, :].rearrange("p (j c) -> p j c", j=4)[:, :, 0:L], AF.Exp), ["P%d" % b], [kp + "c_wT"])
            b = nb()

            def f(e):
                for hh in range(4):
                    ins = e.matmul(PS[b][0:L, hh * 128:hh * 128 + L], kT[:, hh, cs], qT[:, hh, cs], start=True, stop=True)
                return ins
            V("pe", f, ["kT", "qT"], ["P%d" % b])
            if first and tile["kind"] == "p":
                V("dve", lambda e: e.tensor_copy(PS[b][0:1, 0:512:128], s0buf[0:1, :]), ["s0buf", "P%d" % b], ["P%d" % b])
            V("dve", lambda e: e.tensor_tensor(c_SwT[0:L, :, 0:L], PS[b][0:L, :].rearrange("p (j c) -> p j c", j=4)[:, :, 0:L], c_wT[0:L, :, 0:L], ALU.mult),
              ["P%d" % b, kp + "c_wT"], [kp + "c_SwT"])
            V("act", lambda e: e.copy(Cb, ST["C32"]), ["C32_%d" % l], ["Cb"])
            bi_ = [nb(), nb()]
            bx_ = [nb(), nb()]
            v4 = vTM[0:L, ci, :].rearrange("p (h c) -> p h c", h=4)
            for pr in range(2):
                def f(e, pr=pr):
                    for j in range(2):
                        hh = pr * 2 + j
                        ins = e.matmul(PS[bi_[pr]][0:L, j * 256:j * 256 + 129], c_SwT[0:L, hh, 0:L], v4[:, hh, :], start=True, stop=True)
                    return ins
                V("pe", f, [kp + "c_SwT", "vTM"], ["P%d" % bi_[pr]])

                def f(e, pr=pr):
                    for j in range(2):
                        hh = pr * 2 + j
                        ins = e.matmul(PS[bx_[pr]][0:L, j * 256:j * 256 + 129], qT[:, hh, cs], Cb[:, hh, :], start=True, stop=True)
                    return ins
                V("pe", f, ["qT", "Cb"], ["P%d" % bx_[pr]])
                for j in range(2):
                    hh = pr * 2 + j
                    V("act", lambda e, pr=pr, j=j, hh=hh: e.activation(c_int[0:L, hh, :], PS[bx_[pr]][0:L, j * 256:j * 256 + 129], AF.Copy,
                                                                       scale=sm[0:L, 72 + hh:73 + hh]), ["P%d" % bx_[pr], kp + "c_small"], [kp + "c_int"])
                V("dve", lambda e, pr=pr: e.tensor_tensor(c_int[0:L, pr * 2:pr * 2 + 2, :], c_int[0:L, pr * 2:pr * 2 + 2, :],
                                                          PS[bi_[pr]][0:L, :].rearrange("p (j c) -> p j c", j=2)[:, :, 0:129], ALU.add),
                  [kp + "c_int", "P%d" % bi_[pr]], [kp + "c_int"])
            V("act", lambda e: e.activation(sm[0:L, 84:88], c_int[0:L, :, 128], AF.Abs), [kp + "c_int"], [kp + "c_small"])
            V("dve", lambda e: e.tensor_tensor(sm[0:L, 84:88], sm[0:L, 84:88], sm[0:L, 76:80], ALU.max), [kp + "c_small"], [kp + "c_small"])
            V("dve", lambda e: e.reciprocal(sm[0:L, 84:88], sm[0:L, 84:88]), [kp + "c_small"], [kp + "c_small"])
            V("dve", lambda e: e.tensor_tensor(c_hm[0:L, :, :], c_int[0:L, :, 0:128], sm[0:L, 84:88].unsqueeze(2).broadcast_to([L, 4, 128]), ALU.mult),
              [kp + "c_int", kp + "c_small"], [kp + "c_hm"])
            V("act", lambda e: e.activation(c_sc[0:L, 0:512], c_hm[0:L, :, :].rearrange("p h c -> p (h c)"), AF.Square), [kp + "c_hm"], [kp + "c_sc"])
            V("dve", lambda e: e.tensor_reduce(sm[0:L, 88:92], c_sc[0:L, 0:512].rearrange("p (h c) -> p h c", h=4), AX.X, ALU.add), [kp + "c_sc"], [kp + "c_small"])
            V("act", lambda e: e.activation(sm[0:L, 88:92], sm[0:L, 88:92], AF.Ln, bias=EPS, scale=1.0 / 128), [kp + "c_small"], [kp + "c_small"])
            V("act", lambda e: e.activation(sm[0:L, 88:92], sm[0:L, 88:92], AF.Exp, scale=-0.5), [kp + "c_small"], [kp + "c_small"])
            V("dve", lambda e: e.tensor_tensor(c_hm[0:L, :, :], c_hm[0:L, :, :], sm[0:L, 88:92].unsqueeze(2).broadcast_to([L, 4, 128]), ALU.mult),
              [kp + "c_hm", kp + "c_small"], [kp + "c_hm"])
            V("dve", lambda e: e.tensor_tensor(c_hm[0:L, :, :], c_hm[0:L, :, :], oTM[0:L, ci, :].rearrange("p (h c) -> p h c", h=4), ALU.mult),
              [kp + "c_hm", "oTM"], [kp + "c_hm"])
            b = nb()

            def f(e):
                for hh in range(4):
                    ins = e.transpose(PS[b][:, hh * 128:hh * 128 + L], c_hm[0:L, hh, :], ident32[0:L, 0:L])
                return ins
            V("pe", f, [kp + "c_hm", "ident32"], ["P%d" % b])
            for hh in range(4):
                V("act", lambda e, hh=hh: e.activation(mixT[:, 8 + hh, cs], PS[b][:, hh * 128:hh * 128 + L], AF.Identity, scale=pv[l][:, 76 + hh:77 + hh]),
                  ["P%d" % b], ["mixT"])
            V("dve", lambda e: e.tensor_tensor(c_kw[0:L, :, :], kTM[0:L, ci, :].rearrange("p (h c) -> p h c", h=4),
                                               sm[0:L, 80:84].unsqueeze(2).broadcast_to([L, 4, 128]), ALU.mult), ["kTM", kp + "c_small"], [kp + "c_kw"])
            bu = [nb(), nb()]
            for pr in range(2):
                def f(e, pr=pr):
                    for j in range(2):
                        hh = pr * 2 + j
                        ins = e.matmul(PS[bu[pr]][:, j * 256:j * 256 + 129], c_kw[0:L, hh, :], v4[:, hh, :], start=True, stop=True)
                    return ins
                V("pe", f, [kp + "c_kw", "vTM"], ["P%d" % bu[pr]])
            bb = bcast_rows(sm[0:4, 70:71], 4, kp + "c_small", sm, kp)
            V("dve", lambda e: e.tensor_copy(sm[:, 92:96], PS[bb][:, 0:4]), ["P%d" % bb], [kp + "c_small"])
            V("dve", lambda e: e.tensor_tensor(ST["C32"], ST["C32"], sm[:, 92:96].unsqueeze(2).broadcast_to([128, 4, 129]), ALU.mult),
              ["C32_%d" % l, kp + "c_small", "Cb"], ["C32_%d" % l])
            for pr in range(2):
                V("dve", lambda e, pr=pr: e.tensor_tensor(ST["C32"][:, pr * 2:pr * 2 + 2, :], ST["C32"][:, pr * 2:pr * 2 + 2, :],
                                                          PS[bu[pr]][:, :].rearrange("p (j c) -> p j c", j=2)[:, :, 0:129], ALU.add),
                  ["C32_%d" % l, "P%d" % bu[pr]], ["C32_%d" % l])

        def hgrn_chunk(l, tile, ci, ch):
            (c0, L, seq, first, last) = ch
            ST = stt[l]
            cs = slice(c0, c0 + L)
            par = 1
            kp = "p%d_" % par
            c = CS[par]
            R = dict(rows)
            R.update(c["rows"])
            sm = c["c_small"]
            c_sc = c["c_sc"]
            c_gcp = c["c_gcp"]
            c_e = c["c_e"]
            c_qref = c["c_qref"]
            c_qdec = c["c_qdec"]
            c_kref = c["c_kref"]
            c_kend = c["c_kend"]
            c_kendTM = c["c_kendTM"]
            c_ATb = c["c_ATb"]
            c_o = c["c_o"]
            nsb = max(1, L // 32)
            V("dve", lambda e: e.memset(c_gcp[:, :, 0:1], 0.0), (), [kp + "c_gcp"])
            for hh in range(4):
                V("dve", lambda e, hh=hh: e.tensor_tensor_scan(c_gcp[:, hh, 1:1 + L], ones32[:, 0:L], gT[:, hh, cs], 0.0, ALU.mult, ALU.add),
                  ["gT", "ones32"], [kp + "c_gcp"])
            gc = c_gcp[:, :, 1:1 + L]
            gref_full = c_gcp[:, :, 0:L:32].unsqueeze(3).broadcast_to([128, 4, nsb, 32])
            V("dve", lambda e: e.tensor_tensor(c_e[:, :, 0:L].rearrange("p h (a j) -> p h a j", a=nsb), gc.rearrange("p h (a j) -> p h a j", a=nsb),
                                               gref_full, ALU.subtract), [kp + "c_gcp"], [kp + "c_e"])
            V("act", lambda e: e.activation(c_e[:, :, 0:L], c_e[:, :, 0:L], AF.Exp), [kp + "c_e"], [kp + "c_e"])
            V("dve", lambda e: e.tensor_tensor(c_qref[:, :, 0:L], c_e[:, :, 0:L], gqT[:, :, cs], ALU.mult), [kp + "c_e", "gqT"], [kp + "c_qref"])
            V("act", lambda e: e.activation(c_e[:, :, 0:L], gc, AF.Exp), [kp + "c_gcp", kp + "c_qref"], [kp + "c_e"])
            V("dve", lambda e: e.tensor_tensor(c_qdec[:, :, 0:L], c_e[:, :, 0:L], gqT[:, :, cs], ALU.mult), [kp + "c_e", "gqT"], [kp + "c_qdec"])
            for a in range(nsb):
                na = min(L, 32 * (a + 1))
                V("dve", lambda e, a=a, na=na: e.tensor_tensor(c_e[:, :, 0:na], c_gcp[:, :, 32 * a:32 * a + 1].broadcast_to([128, 4, na]),
                                                               c_gcp[:, :, 1:1 + na], ALU.subtract), [kp + "c_gcp", kp + "c_qdec"], [kp + "c_e"])
                V("act", lambda e, na=na: e.activation(c_e[:, :, 0:na], c_e[:, :, 0:na], AF.Exp), [kp + "c_e"], [kp + "c_e"])
                V("dve", lambda e, a=a, na=na: e.tensor_tensor(c_kref[a][:, :, 0:na], c_e[:, :, 0:na], gkT[:, :, c0:c0 + na], ALU.mult),
                  [kp + "c_e", "gkT"], [kp + "kref%d" % a])
            V("dve", lambda e: e.tensor_tensor(c_e[:, :, 0:L], c_gcp[:, :, L:L + 1].broadcast_to([128, 4, L]), gc, ALU.subtract),
              [kp + "c_gcp", kp + "kref%d" % (nsb - 1)], [kp + "c_e"])
            V("act", lambda e: e.activation(c_e[:, :, 0:L], c_e[:, :, 0:L], AF.Exp), [kp + "c_e"], [kp + "c_e"])
            V("dve", lambda e: e.tensor_tensor(c_kend[:, :, 0:L], c_e[:, :, 0:L], gkT[:, :, cs], ALU.mult), [kp + "c_e", "gkT"], [kp + "c_kend"])
            V("act", lambda e: e.activation(sm[:, 100:104], c_gcp[:, :, L], AF.Exp), [kp + "c_gcp"], [kp + "c_small"])
            b = nb()

            def f(e):
                for hh in range(4):
                    for a in range(nsb):
                        ins = e.matmul(PS[b][0:L, hh * 128 + 32 * a:hh * 128 + 32 * a + 32], c_kref[a][:, hh, 0:L], c_qref[:, hh, 32 * a:32 * a + 32],
                                       start=True, stop=True)
                return ins
            V("pe", f, [kp + "kref%d" % a for a in range(nsb)] + [kp + "c_qref"], ["P%d" % b])
            V("dve", lambda e: e.tensor_tensor(c_ATb[0:L, :, 0:L], PS[b][0:L, :].rearrange("p (j c) -> p j c", j=4)[:, :, 0:L],
                                               mask01b[0:L, 0:L].unsqueeze(1).broadcast_to([L, 4, L]), ALU.mult), ["P%d" % b, "mask01b"], [kp + "c_ATb"])
            V("act", lambda e: e.copy(Sb, ST["S32"]), ["S32_%d" % l], ["Sb"])
            bo = nb()

            def f(e):
                for hh in range(4):
                    o = PS[bo][0:L, hh * 128:(hh + 1) * 128]
                    e.matmul(o, c_ATb[0:L, hh, 0:L], gviTM[0:L, ci, hh * 128:(hh + 1) * 128], start=True, stop=False)
                    ins = e.matmul(o, c_qdec[:, hh, 0:L], Sb[:, hh, :], start=False, stop=True)
                return ins
            V("pe", f, [kp + "c_ATb", "gviTM", kp + "c_qdec", "Sb"], ["P%d" % bo])
            V("act", lambda e: e.copy(c_o[0:L, :, :], PS[bo][0:L, :].rearrange("p (h c) -> p h c", h=4)), ["P%d" % bo], [kp + "c_o"])
            V("act", lambda e: e.activation(c_sc[0:L, 512:1024], c_o[0:L, :, :].rearrange("p h c -> p (h c)"), AF.Square), [kp + "c_o"], [kp + "c_sc2"])
            V("dve", lambda e: e.tensor_reduce(sm[0:L, 104:108], c_sc[0:L, 512:1024].rearrange("p (h c) -> p h c", h=4), AX.X, ALU.add), [kp + "c_sc2"], [kp + "c_small"])
            V("act", lambda e: e.activation(sm[0:L, 104:108], sm[0:L, 104:108], AF.Ln, bias=EPS, scale=1.0 / 128), [kp + "c_small"], [kp + "c_small"])
            V("act", lambda e: e.activation(sm[0:L, 104:108], sm[0:L, 104:108], AF.Exp, scale=-0.5), [kp + "c_small"], [kp + "c_small"])
            V("dve", lambda e: e.tensor_tensor(c_o[0:L, :, :], c_o[0:L, :, :], sm[0:L, 104:108].unsqueeze(2).broadcast_to([L, 4, 128]), ALU.mult),
              [kp + "c_o", kp + "c_small"], [kp + "c_o"])
            V("dve", lambda e: e.tensor_tensor(c_o[0:L, :, :], c_o[0:L, :, :], ggTM[0:L, ci, :].rearrange("p (h c) -> p h c", h=4), ALU.mult),
              [kp + "c_o", "ggTM"], [kp + "c_o"])
            b = nb()

            def f(e):
                for hh in range(4):
                    ins = e.transpose(PS[b][:, hh * 128:hh * 128 + L], c_o[0:L, hh, :], ident32[0:L, 0:L])
                return ins
            V("pe", f, [kp + "c_o", "ident32"], ["P%d" % b])
            for hh in range(4):
                V("act", lambda e, hh=hh, b=b: e.activation(mixT[:, 12 + hh, cs], PS[b][:, hh * 128:hh * 128 + L], AF.Identity, scale=pv[l][:, 84 + hh:85 + hh]),
                  ["P%d" % b], ["mixT"])
            b = nb()
            pbt = PS[b].bitcast(BF16)

            def f(e):
                for hh in range(4):
                    ins = e.transpose(pbt[0:L, hh * 128:(hh + 1) * 128], c_kend[:, hh, 0:L], identb)
                return ins
            V("pe", f, [kp + "c_kend", "identb"], ["P%d" % b])
            V("act", lambda e: e.copy(c_kendTM[0:L, :, :].rearrange("p h c -> p (h c)"), pbt[0:L, 0:512]), ["P%d" % b], [kp + "c_kendTM"])
            bu = nb()

            def f(e):
                for hh in range(4):
                    ins = e.matmul(PS[bu][:, hh * 128:(hh + 1) * 128], c_kendTM[0:L, hh, :], gviTM[0:L, ci, hh * 128:(hh + 1) * 128], start=True, stop=True)
                return ins
            V("pe", f, [kp + "c_kendTM", "gviTM"], ["P%d" % bu])
            V("dve", lambda e: e.tensor_tensor(ST["S32"], ST["S32"], sm[:, 100:104].unsqueeze(2).broadcast_to([128, 4, 128]), ALU.mult),
              ["S32_%d" % l, kp + "c_small", "Sb"], ["S32_%d" % l])
            V("dve", lambda e: e.tensor_tensor(ST["S32"], ST["S32"], PS[bu][:, :].rearrange("p (h c) -> p h c", h=4), ALU.add),
              ["S32_%d" % l, "P%d" % bu], ["S32_%d" % l])

        def state_init_prompt(l, ch, which):
            ST = stt[l]
            if which == "ssd":
                V("pool", lambda e: e.memset(ST["h32"], 0.0), (), ["h32_%d" % l])
            elif which == "ml":
                V("pool", lambda e: e.memset(ST["C32"], 0.0), (), ["C32_%d" % l])
                V("pool", lambda e: e.memset(ST["m"], 0.0), (), ["m%d" % l])
            else:
                V("pool", lambda e: e.memset(ST["S32"], 0.0), (), ["S32_%d" % l])

        def state_init_sample(l, ch, which):
            (c0, L, seq, first, last) = ch
            ST = stt[l]
            if which == "ssd":
                S.dma("sp", io_ssd, state_ssd[l, seq].rearrange("(pr h2) p n -> (h2 p) pr n", h2=2), (), ["io_ssd"])
                for hf in range(2):
                    b = nb()

                    def f(e, hf=hf, b=b):
                        for j in range(4):
                            ins = e.transpose(PS[b][:, j * 128:(j + 1) * 128], io_ssd[:, hf * 4 + j, :], ident32)
                        return ins
                    V("pe", f, ["io_ssd", "ident32"], ["P%d" % b])
                    V("dve", lambda e, hf=hf, b=b: e.tensor_copy(ST["h32"][:, hf * 512:(hf + 1) * 512], PS[b][:, :]), ["P%d" % b], ["h32_%d" % l])
            elif which == "ml":
                S.dma("sp", ST["C32"][:, :, 0:128], state_mc[l, seq].rearrange("h k v -> k h v"), (), ["C32_%d" % l])
                S.dma("sp", io_n[0:4, :], state_mn[l, seq], (), ["io_n"])
                b = nb()
                V("pe", lambda e: e.transpose(PS[b][:, 0:4], io_n[0:4, :], ident32[0:4, 0:4]), ["io_n", "ident32"], ["P%d" % b])
                V("dve", lambda e: e.tensor_copy(ST["C32"][:, :, 128], PS[b][:, 0:4]), ["P%d" % b], ["C32_%d" % l])
                S.dma("sp", ST["m"][0:4, :], state_mm[l, seq].rearrange("(a b) -> a b", b=1), (), ["m%d" % l])
            else:
                S.dma("sp", ST["S32"], state_hg[l, seq].rearrange("h k v -> k h v"), (), ["S32_%d" % l])

        def state_store(pre):
            def fn(l, ch, which):
                (c0, L, seq, first, last) = ch
                ST = stt[l]
                if which == "ssd":
                    for hf in range(2):
                        b = nb()

                        def f(e, hf=hf, b=b):
                            for j in range(4):
                                ins = e.transpose(PS[b][:, j * 128:(j + 1) * 128], ST["h32"][:, (hf * 4 + j) * 128:(hf * 4 + j + 1) * 128], ident32)
                            return ins
                        V("pe", f, ["h32_%d" % l, "ident32"], ["P%d" % b])
                        V("dve", lambda e, hf=hf, b=b: e.tensor_copy(io_ssd[:, hf * 4:(hf + 1) * 4, :], PS[b][:, :].rearrange("p (j c) -> p j c", j=4)),
                          ["P%d" % b], ["io_ssd"])
                    S.dma("sp", O[pre + "_ssd"][l, seq].rearrange("(pr h2) p n -> (h2 p) pr n", h2=2), io_ssd, ["io_ssd"], ())
                elif which == "ml":
                    S.dma("sp", O[pre + "_mc"][l, seq].rearrange("h k v -> k h v"), ST["C32"][:, :, 0:128], ["C32_%d" % l], ())
                    b = nb()
                    V("dve", lambda e: e.tensor_copy(CS[0]["c_small"][:, 124:128], ST["C32"][:, :, 128]), ["C32_%d" % l], ["c_small_n"])
                    V("pe", lambda e: e.transpose(PS[b][0:4, 0:128], CS[0]["c_small"][:, 124:128], ident32), ["c_small_n", "ident32"], ["P%d" % b])
                    V("dve", lambda e: e.tensor_copy(io_n[0:4, :], PS[b][0:4, 0:128]), ["P%d" % b], ["io_n"])
                    S.dma("sp", O[pre + "_mn"][l, seq], io_n[0:4, :], ["io_n"], ())
                    S.dma("sp", O[pre + "_mm"][l, seq].rearrange("(a b) -> a b", b=1), ST["m"][0:4, :], ["m%d" % l], ())
                else:
                    S.dma("sp", O[pre + "_hg"][l, seq].rearrange("h k v -> k h v"), ST["S32"], ["S32_%d" % l], ())
            return fn

        def conv_store(pre, l, seq, src3, key):
            for q in range(3):
                b = nb()

                def f(e, q=q, b=b):
                    for j in range(4):
                        cc = q * 4 + j
                        ins = e.transpose(PS[b][0:3, j * 128:(j + 1) * 128], src3[:, cc, :], ident32)
                    return ins
                V("pe", f, [key, "ident32"], ["P%d" % b])
                V("dve", lambda e, q=q, b=b: e.tensor_copy(io_conv[0:3, q * 512:(q + 1) * 512], PS[b][0:3, :]), ["P%d" % b], ["io_conv"])
            S.dma("sp", O[pre + "_conv"][l, seq], io_conv[0:3, :], ["io_conv"], ())

        def run_tile(tile):
            n = tile["n"]
            for ci, (c0, L, seq, first, last) in enumerate(tile["chunks"]):
                xi = xin[ci % 2]
                xk = "p%d_c_y" % (ci % 2)
                S.dma("sp", xi[0:L, :], tile["xsrc"](ci), (), [xk])
                for hf in range(2):
                    b = nb()

                    def f(e, hf=hf, b=b, xi=xi, L=L):
                        for j in range(4):
                            ins = e.transpose(PS[b][:, j * 128:j * 128 + L], xi[0:L, (hf * 4 + j) * 128:(hf * 4 + j + 1) * 128], ident32[0:L, 0:L])
                        return ins
                    V("pe", f, [xk, "ident32"], ["P%d" % b])
                    V("dve", lambda e, hf=hf, b=b, c0=c0, L=L: e.tensor_copy(xT[:, hf * 4:(hf + 1) * 4, c0:c0 + L],
                                                                             PS[b][:, :].rearrange("p (j c) -> p j c", j=4)[:, :, 0:L]),
                      ["P%d" % b], ["xT"])
            if tile["kind"] == "s":
                for l in range(2):
                    pass
            for l in range(2):
                if tile["kind"] == "s":
                    for si, (off, ln, seq) in enumerate(tile["segs"]):
                        S.dma("sp", io_conv[0:3, :], state_conv[l, seq], (), ["io_conv"])
                        b = nb()

                        def f(e, b=b):
                            for cc in range(12):
                                ins = e.transpose(PS[b][:, cc * 3:cc * 3 + 3], io_conv[0:3, cc * 128:(cc + 1) * 128], ident32[0:3, 0:3])
                            return ins
                        V("pe", f, ["io_conv", "ident32"], ["P%d" % b])
                        V("dve", lambda e, si=si, b=b: e.tensor_copy(convin[:, si, :, :], PS[b][:, 0:36].rearrange("p (c t) -> p c t", c=12)),
                          ["P%d" % b], ["convin"])
                layer(l, tile)
                if tile["kind"] == "s":
                    for si, (off, ln, seq) in enumerate(tile["segs"]):
                        conv_store("s", l, seq, convout[:, si, :, :], "convout")
                elif tile["last_tile"]:
                    conv_store("p", l, tile["seq"], stt[l]["conv"], "conv%d" % l)
            fence_all()
            V("act", lambda e: e.activation(sqs[:, 0:4, 0:n], xT[:, 0:4, 0:n], AF.Square), ["xT"], ["sqsF"])
            V("dve", lambda e: e.tensor_tensor(sqs[:, 4:8, 0:n], xT[:, 4:8, 0:n], xT[:, 4:8, 0:n], ALU.mult), ["xT"], ["sqsF_b"])
            b = nb()

            def f(e):
                for dc in range(8):
                    ins = e.matmul(PS[b][:, 0:n], onesb, sqs[:, dc, 0:n], start=(dc == 0), stop=(dc == 7))
                return ins
            V("pe", f, ["sqsF", "sqsF_b", "onesb"], ["P%d" % b])
            V("act", lambda e: e.activation(rstd[:, 0:n], PS[b][:, 0:n], AF.Ln, bias=EPS, scale=1.0 / D), ["P%d" % b], ["rstd"])
            V("act", lambda e: e.activation(rstd[:, 0:n], rstd[:, 0:n], AF.Exp, scale=-0.5), ["rstd"], ["rstd"])
            for dc in range(8):
                V("dve", lambda e, dc=dc: e.scalar_tensor_tensor(xT[:, dc, 0:n], xT[:, dc, 0:n], pvf[:, dc:dc + 1], rstd[:, 0:n], ALU.mult, ALU.mult),
                  ["xT", "rstd", "pvf"], ["xT"])
            for ci, (c0, L, seq, first, last) in enumerate(tile["chunks"]):
                xi = ostg[ci % 2]
                ok = "ostg%d" % (ci % 2)
                for hf in range(2):
                    b = nb()

                    def f(e, hf=hf, b=b, c0=c0, L=L):
                        for j in range(4):
                            ins = e.transpose(PS[b][0:L, j * 128:(j + 1) * 128], xT[:, hf * 4 + j, c0:c0 + L], ident32)
                        return ins
                    V("pe", f, ["xT", "ident32"], ["P%d" % b])
                    V("act", lambda e, hf=hf, b=b, xi=xi, L=L: e.copy(xi[0:L, hf * 512:(hf + 1) * 512], PS[b][0:L, :]), ["P%d" % b], [ok])
                S.dma("sp", tile["ydst"](ci), xi[0:L, :], [ok], ())

        tiles = []
        for p in range(NPS):
            for t0 in range(0, TP, TT):
                n = min(TT, TP - t0)
                chunks = []
                for c0 in range(0, n, 128):
                    L = min(128, n - c0)
                    chunks.append((c0, L, p, (t0 == 0 and c0 == 0), (t0 + c0 + L == TP)))
                tiles.append(dict(kind="p", seq=p, n=n, segs=[(0, n, p)], chunks=chunks, last_tile=(t0 + n == TP),
                                  first_tile=(p == 0 and t0 == 0),
                                  xsrc=(lambda ci, p=p, t0=t0, chunks=chunks: x_prompt[p, t0 + chunks[ci][0]:t0 + chunks[ci][0] + chunks[ci][1], :]),
                                  ydst=(lambda ci, p=p, t0=t0, chunks=chunks: O["p_y"][p, t0 + chunks[ci][0]:t0 + chunks[ci][0] + chunks[ci][1], :]),
                                  state_init=state_init_prompt, state_store=state_store("p")))
        if NSS > 0:
            chunks = [(i * TS, TS, i, True, True) for i in range(NSS)]
            tiles.append(dict(kind="s", seq=None, n=NSS * TS, segs=[(i * TS, TS, i) for i in range(NSS)], chunks=chunks, last_tile=True,
                              first_tile=False, convin=convin, convout=convout,
                              xsrc=(lambda ci: x_sample[ci, :, :]), ydst=(lambda ci: O["s_y"][ci, :, :]),
                              state_init=state_init_sample, state_store=state_store("s")))
        for tile in tiles:
            if tile["kind"] == "p" and tile["chunks"][0][3]:
                for l in range(2):
                    V("pool", lambda e, l=l: e.memset(stt[l]["conv"], 0.0), (), ["conv%d" % l])
            run_tile(tile)
        S.emit()
    return nc, DBG


OUT_ORDER = ["_y", "_conv", "_ssd", "_mc", "_mn", "_mm", "_hg"]
_CACHE = {}


def kernel(**inputs):
    NC = 8
    x_prompt = np.ascontiguousarray(inputs["x_prompt"], dtype=np.float32)
    x_sample = np.ascontiguousarray(inputs["x_sample"], dtype=np.float32)
    B, T, _ = x_prompt.shape
    BS = x_sample.shape[0]
    NPS, NSS = B // NC, BS // NC
    key = (NPS, T, NSS)
    if key not in _CACHE:
        _CACHE[key] = build(NPS, T, NSS)[0]
    nc = _CACHE[key]
    wnames = ["norm_mix", "w_in", "conv_w", "conv_b", "dt_bias", "a_log", "d_skip", "ssd_gain", "ml_bi", "ml_bf", "ml_gain",
              "hg_lb", "hg_gain", "w_out", "norm_ffn", "w_ffn_in", "w_ffn_out", "norm_final"]
    snames = ["state_conv", "state_ssd", "state_mlstm_c", "state_mlstm_n", "state_mlstm_m", "state_hgrn"]
    in_maps = []
    for c in range(NC):
        m = {"x_prompt": x_prompt[c * NPS:(c + 1) * NPS], "x_sample": x_sample[c * NSS:(c + 1) * NSS]}
        for s in snames:
            m[s] = np.ascontiguousarray(np.asarray(inputs[s], dtype=np.float32)[:, c * NSS:(c + 1) * NSS])
        for w in wnames:
            m[w] = np.ascontiguousarray(inputs[w], dtype=np.float32)
        in_maps.append(m)
    res = run_bass_kernel_spmd(nc, in_maps, core_ids=list(range(NC)))
    outs = []
    for pre in ("p", "s"):
        for nm in OUT_ORDER:
            ax = 0 if nm == "_y" else 1
            outs.append(np.concatenate([np.asarray(r[pre + nm]) for r in res.results], axis=ax))
    yp, pc, pssd, pmc, pmn, pmm, phg, ys, sc, sssd, smc, smn, smm, shg = outs
    return (yp, ys, pc, pssd, pmc, pmn, pmm, phg, sc, sssd, smc, smn, smm, shg)
```

```python
import contextlib
import numpy as np
import concourse.bass as bass
import concourse.mybir as mybir
from concourse.bass_utils import run_bass_kernel_spmd

F32 = mybir.dt.float32
BF16 = mybir.dt.bfloat16
AF = mybir.ActivationFunctionType
ALU = mybir.AluOpType
AX = mybir.AxisListType

D = 1024
DMIX = 2048
INC = 6680
DFF = 2816
EPS = 1e-6
NEG = -30000.0
ENGS = ("pe", "act", "dve", "pool", "sp")


class Op:
    __slots__ = ("eng", "idx", "fn", "dma", "deps", "sig", "semref")

    def __init__(self, eng, idx, fn, dma):
        self.eng = eng
        self.idx = idx
        self.fn = fn
        self.dma = dma
        self.deps = set()
        self.sig = False
        self.semref = None


class _Rec:
    def __init__(self):
        self.calls = []

    def __getattr__(self, name):
        def m(*a, **k):
            self.calls.append((name, a, k))
            return self
        return m


class Sched:
    def __init__(self, nc, n_dma_sems=20):
        self.nc = nc
        self.ops = {e: [] for e in ENGS}
        self.res = {}
        self.n_dma_sems = n_dma_sems
        self.cap = None

    def add(self, eng, fn, reads=(), writes=(), dma=False):
        rec = _Rec()
        fn(rec)
        if self.cap is not None:
            self.cap.append((eng, rec.calls, tuple(reads), tuple(writes), dma))
            return None
        return self.commit(eng, rec.calls, reads, writes, dma)

    def capture(self, f, *a):
        assert self.cap is None
        self.cap = []
        f(*a)
        out = self.cap
        self.cap = None
        return out

    def commit(self, eng, calls, reads, writes, dma):
        op = Op(eng, len(self.ops[eng]), calls, dma)
        deps = set()
        for r in reads:
            st = self.res.setdefault(r, [None, []])
            if st[0] is not None:
                deps.add(st[0])
            st[1].append(op)
        for w in writes:
            st = self.res.setdefault(w, [None, []])
            if st[0] is not None:
                deps.add(st[0])
            for rd in st[1]:
                deps.add(rd)
            st[0] = op
            st[1] = []
        deps.discard(op)
        op.deps = deps
        for d in deps:
            d.sig = True
        self.ops[eng].append(op)
        return op

    def dma(self, eng, out, in_, reads=(), writes=()):
        return self.add(eng, lambda e: e.dma_start(out=out, in_=in_), reads, writes, dma=True)

    def emit(self):
        nc = self.nc
        with contextlib.ExitStack() as st:
            csem = {e: st.enter_context(nc.semaphore("c_" + e)) for e in ENGS}
            dsem = {e: [st.enter_context(nc.semaphore("d_%s%d" % (e, i))) for i in range(self.n_dma_sems)]
                    for e in ("sp", "act", "pool")}
            for e in ENGS:
                cnt = 0
                dcnt = [0] * self.n_dma_sems
                k = 0
                for op in self.ops[e]:
                    if op.dma:
                        j = k % self.n_dma_sems
                        k += 1
                        prev = dcnt[j]
                        dcnt[j] += 16
                        op.semref = (dsem[e][j], dcnt[j], prev)
                    elif op.sig:
                        cnt += 1
                        op.semref = (csem[e], cnt, None)
            block = st.enter_context(nc.Block())

            def run(e, eng):
                waited = {}
                for op in self.ops[e]:
                    need = {}
                    for d in op.deps:
                        if d.semref is None:
                            continue
                        sem, val = d.semref[0], d.semref[1]
                        key = id(sem)
                        if need.get(key, (None, 0))[1] < val:
                            need[key] = (sem, val)
                    if op.dma:
                        sem, val, prev = op.semref
                        if prev > 0:
                            key = id(sem)
                            if need.get(key, (None, 0))[1] < prev:
                                need[key] = (sem, prev)
                    for key, (sem, val) in need.items():
                        if waited.get(key, 0) < val:
                            eng.wait_ge(sem, val)
                            waited[key] = val
                    ins = None
                    for name, a, k in op.fn:
                        ins = getattr(eng, name)(*a, **k)
                    if op.dma:
                        ins.then_inc(op.semref[0], 16)
                    elif op.sig:
                        ins.then_inc(op.semref[0], 1)
                if e == "sp":
                    for e2 in ("sp", "act", "pool"):
                        last = {}
                        for op in self.ops[e2]:
                            if op.dma:
                                last[id(op.semref[0])] = (op.semref[0], op.semref[1])
                        for sem, val in last.values():
                            eng.wait_ge(sem, val)

            block.tensor(lambda eng: run("pe", eng))
            block.scalar(lambda eng: run("act", eng))
            block.vector(lambda eng: run("dve", eng))
            block.gpsimd(lambda eng: run("pool", eng))
            block.sync(lambda eng: run("sp", eng))


class Arena:
    def __init__(self, nc, st, name, nbytes):
        self.words = nbytes // 4
        self.t = st.enter_context(nc.sbuf_tensor(name, [128, self.words], F32))
        self.off = 0
        self.name = name

    def alloc(self, shape, dt):
        n = int(np.prod(shape))
        words = n if dt == F32 else (n + 1) // 2
        assert self.off + words <= self.words, (self.name, self.off, words, self.words)
        v = self.t[:, self.off:self.off + words]
        self.off += words
        if dt == BF16:
            v = v.bitcast(BF16)[:, 0:n]
        if len(shape) == 2:
            v = v.rearrange("p (a b) -> p a b", a=shape[0])
        elif len(shape) == 3:
            v = v.rearrange("p (a b c) -> p a b c", a=shape[0], b=shape[1])
        return v


def build(NPS, TP, NSS, dbg=False, LAG=8):
    TS = 32
    TT = min(512, TP)
    nc = bass.Bass("TRN2", target_bir_lowering=False)
    din = lambda n, s: nc.dram_tensor(n, list(s), F32, kind="ExternalInput").ap()
    dout = lambda n, s: nc.dram_tensor(n, list(s), F32, kind="ExternalOutput").ap()
    x_prompt = din("x_prompt", [NPS, TP, D])
    x_sample = din("x_sample", [NSS, TS, D])
    state_conv = din("state_conv", [2, NSS, 3, 1536])
    state_ssd = din("state_ssd", [2, NSS, 16, 64, 128])
    state_mc = din("state_mlstm_c", [2, NSS, 4, 128, 128])
    state_mn = din("state_mlstm_n", [2, NSS, 4, 128])
    state_mm = din("state_mlstm_m", [2, NSS, 4])
    state_hg = din("state_hgrn", [2, NSS, 4, 128, 128])
    norm_mix = din("norm_mix", [2, D])
    w_in = din("w_in", [2, D, INC])
    conv_w = din("conv_w", [2, 4, 1536])
    conv_b = din("conv_b", [2, 1536])
    dt_bias = din("dt_bias", [2, 16])
    a_log = din("a_log", [2, 16])
    d_skip = din("d_skip", [2, 16])
    ssd_gain = din("ssd_gain", [2, 1024])
    ml_bi = din("ml_bi", [2, 4])
    ml_bf = din("ml_bf", [2, 4])
    ml_gain = din("ml_gain", [2, 512])
    hg_lb = din("hg_lb", [2, 512])
    hg_gain = din("hg_gain", [2, 512])
    w_out = din("w_out", [2, DMIX, D])
    norm_ffn = din("norm_ffn", [2, D])
    w_ffn_in = din("w_ffn_in", [2, D, 2 * DFF])
    w_ffn_out = din("w_ffn_out", [2, DFF, D])
    norm_final = din("norm_final", [D])

    w_in_b = nc.dram_tensor("w_in_b", [2, D, INC], BF16, kind="Internal").ap()
    w_out_b = nc.dram_tensor("w_out_b", [2, DMIX, D], BF16, kind="Internal").ap()
    w_ffn_in_b = nc.dram_tensor("w_ffn_in_b", [2, D, 2 * DFF], BF16, kind="Internal").ap()
    w_ffn_out_b = nc.dram_tensor("w_ffn_out_b", [2, DFF, D], BF16, kind="Internal").ap()
    O = {}
    for pre, nb_ in (("p", NPS), ("s", NSS)):
        O[pre + "_y"] = dout(pre + "_y", [nb_, TP if pre == "p" else TS, D])
        O[pre + "_conv"] = dout(pre + "_conv", [2, nb_, 3, 1536])
        O[pre + "_ssd"] = dout(pre + "_ssd", [2, nb_, 16, 64, 128])
        O[pre + "_mc"] = dout(pre + "_mc", [2, nb_, 4, 128, 128])
        O[pre + "_mn"] = dout(pre + "_mn", [2, nb_, 4, 128])
        O[pre + "_mm"] = dout(pre + "_mm", [2, nb_, 4])
        O[pre + "_hg"] = dout(pre + "_hg", [2, nb_, 4, 128, 128])
    DBG = {}

    st = contextlib.ExitStack()
    with st:
        S = Sched(nc)
        PS = [st.enter_context(nc.psum_tensor("ps%d" % i, [128, 512], F32)) for i in range(8)]
        psk = [0, 0, 0]
        pspool = [None]

        def nb():
            p = pspool[0]
            if p is None:
                i = psk[2] % 8
                psk[2] += 1
                return i
            i = 4 * p + (psk[p] % 4)
            psk[p] += 1
            return i

        A = Arena(nc, st, "arenaA", 106 * 1024)
        RB = Arena(nc, st, "arenaB", 101 * 1024)

        ident32 = A.alloc((128,), F32)
        identb = A.alloc((128,), BF16)
        onesb = A.alloc((128,), BF16)
        mask01b = A.alloc((128,), BF16)
        negmask4 = A.alloc((4, 128), BF16)
        ones32 = A.alloc((128,), F32)
        V = lambda eng, f, r=(), w=(): S.add(eng, f, r, w)
        V("pool", lambda e: e.memset(ident32, 0.0), (), ["ident32"])
        V("pool", lambda e: e.affine_select(out=ident32, in_=ident32, pattern=[[-1, 128]], compare_op=ALU.not_equal,
                                            fill=1.0, base=0, channel_multiplier=1), ["ident32"], ["ident32"])
        V("dve", lambda e: e.tensor_copy(identb, ident32), ["ident32"], ["identb"])
        V("dve", lambda e: e.memset(onesb, 1.0), (), ["onesb"])
        V("dve", lambda e: e.memset(ones32, 1.0), (), ["ones32"])
        V("pool", lambda e: e.memset(ones32, 1.0), ["ones32"], ["ones32"])
        m01f = A.alloc((128,), F32)
        V("pool", lambda e: e.memset(m01f, 1.0), (), ["m01f"])
        V("pool", lambda e: e.affine_select(out=m01f, in_=m01f, pattern=[[1, 128]], compare_op=ALU.is_ge,
                                            fill=0.0, base=0, channel_multiplier=-1), ["m01f"], ["m01f"])
        V("dve", lambda e: e.tensor_copy(mask01b, m01f), ["m01f"], ["mask01b"])
        for j in range(4):
            V("dve", lambda e, j=j: e.tensor_scalar(negmask4[:, j, :], m01f, 1.0, -NEG, ALU.subtract, ALU.mult),
              ["m01f"], ["negmask4"])
        def sel(nr, hh, L):
            return ident32[0:nr, hh:hh + 1].broadcast_to([nr, L])

        xT = A.alloc((8, TT), F32)
        hT = A.alloc((8, TT), BF16)
        mixT = A.alloc((16, TT), BF16)
        rstd = A.alloc((TT,), F32)
        NSLOT = 2
        wring = [A.alloc((8192,), BF16) for _ in range(NSLOT)]
        wk = [0]
        pv = [A.alloc((96,), F32) for _ in range(2)]
        pvf = A.alloc((8,), F32)
        rowbuf = A.alloc((128,), F32)
        hs = {}
        for l in range(2):
            hs[l] = dict(dtb=A.alloc((1,), F32), A=A.alloc((1,), F32), bi=A.alloc((1,), F32), nbf=A.alloc((1,), F32),
                         Dbc=A.alloc((16,), F32), lb=A.alloc((4,), F32), oml=A.alloc((4,), F32))
        stt = {}
        for l in range(2):
            stt[l] = dict(h32=A.alloc((1024,), F32), C32=A.alloc((4, 129), F32), S32=A.alloc((4, 128), F32),
                          m=A.alloc((1,), F32), conv=A.alloc((12, 3), F32))
        hTb = A.alloc((1024,), BF16)
        Cb = A.alloc((4, 129), BF16)
        Sb = A.alloc((4, 128), BF16)
        fence = A.alloc((4,), F32)
        w32a = A.alloc((1024,), F32)
        h0buf = A.alloc((8,), F32)
        s0buf = A.alloc((4,), F32)

        def load_rows(l):
            segs = [(norm_mix[l], 0, 8), (conv_w[l].rearrange("w c -> (w c)"), 8, 48), (conv_b[l], 56, 12), (ssd_gain[l], 68, 8),
                    (ml_gain[l], 76, 4), (hg_lb[l], 80, 4), (hg_gain[l], 84, 4), (norm_ffn[l], 88, 8)]
            for src, r0, n in segs:
                S.dma("sp", rowbuf[r0:r0 + n, :], src.rearrange("(a b) -> a b", b=128), (), ["rowbuf"])
            b = nb()
            V("pe", lambda e: e.transpose(PS[b][:, 0:96], rowbuf[0:96, :], ident32[0:96, 0:96]), ["rowbuf", "ident32"], ["P%d" % b])
            V("dve", lambda e: e.tensor_copy(pv[l], PS[b][:, 0:96]), ["P%d" % b], ["pv%d" % l])

        for l in range(2):
            load_rows(l)
            h = hs[l]
            S.dma("sp", h["dtb"][0:16, :], dt_bias[l].rearrange("(a b) -> a b", b=1), (), ["hs%d" % l])
            S.dma("sp", h["A"][0:16, :], a_log[l].rearrange("(a b) -> a b", b=1), (), ["hs%d" % l])
            S.dma("sp", h["bi"][0:4, :], ml_bi[l].rearrange("(a b) -> a b", b=1), (), ["hs%d" % l])
            S.dma("sp", h["nbf"][0:4, :], ml_bf[l].rearrange("(a b) -> a b", b=1), (), ["hs%d" % l])
            S.dma("sp", h["Dbc"], d_skip[l].rearrange("(a b) -> a b", a=1).broadcast_to([128, 16]), (), ["hs%d" % l])
            V("act", lambda e, h=h: e.activation(h["A"][0:16, :], h["A"][0:16, :], AF.Exp), ["hs%d" % l], ["hs%d" % l])
            V("dve", lambda e, h=h: e.tensor_scalar(h["A"][0:16, :], h["A"][0:16, :], -1.0, None, ALU.mult), ["hs%d" % l], ["hs%d" % l])
            V("dve", lambda e, h=h: e.tensor_scalar(h["nbf"][0:4, :], h["nbf"][0:4, :], -1.0, None, ALU.mult), ["hs%d" % l], ["hs%d" % l])
        S.dma("sp", rowbuf[0:8, :], norm_final.rearrange("(a b) -> a b", b=128), (), ["rowbuf"])
        b = nb()
        V("pe", lambda e: e.transpose(PS[b][:, 0:8], rowbuf[0:8, :], ident32[0:8, 0:8]), ["rowbuf", "ident32"], ["P%d" % b])
        V("dve", lambda e: e.tensor_copy(pvf, PS[b][:, 0:8]), ["P%d" % b], ["pvf"])
        V("dve", lambda e: e.memset(hs[0]["lb"], 0.0), (), ["hs0"])
        V("dve", lambda e: e.memset(hs[0]["oml"], 1.0), (), ["hs0"])
        V("dve", lambda e: e.tensor_tensor(hs[1]["lb"], pv[1][:, 80:84], pv[0][:, 80:84], ALU.subtract), ["pv0", "pv1"], ["hs1"])
        V("act", lambda e: e.activation(hs[1]["lb"], hs[1]["lb"], AF.Sigmoid), ["hs1"], ["hs1"])
        V("dve", lambda e: e.tensor_scalar(hs[1]["oml"], hs[1]["lb"], -1.0, 1.0, ALU.mult, ALU.add), ["hs1"], ["hs1"])

        converted = set()

        def wload(src3, src3f, a, c, wbkey):
            i = wk[0] % NSLOT
            wk[0] += 1
            v = wring[i][:, 0:a * c].rearrange("p (a c) -> p a c", a=a)
            if wbkey not in converted:
                converted.add(wbkey)
                S.dma("pool", v, src3f, (), ["w%d" % i])
                S.dma("sp", src3, v, ["w%d" % i], [wbkey])
            else:
                S.dma("sp", v, src3, [wbkey], ["w%d" % i])
            return v, "w%d" % i

        NCH = 4
        xp = RB.alloc((TT + 16,), F32)
        cacc = RB.alloc((TT,), F32)
        io_ssd = RB.alloc((8, 128), F32)
        io_conv = RB.alloc((1536,), F32)
        io_n = RB.alloc((128,), F32)
        convin = RB.alloc((4, 12, 3), F32)
        convout = RB.alloc((4, 12, 3), F32)
        CS = [dict(), dict()]
        c_sc1 = RB.alloc((1024,), F32)
        q0row = RB.alloc((512,), F32)
        for par in range(2):
            CS[par]["c_small"] = RB.alloc((256,), F32)
            CS[par]["c_sc"] = c_sc1
        rb0 = RB.off
        xbcT = RB.alloc((12, TT), BF16)
        zs = RB.alloc((NCH, 1024), BF16)
        rows = {}
        rows["dt"] = RB.alloc((TT,), F32)
        rows["a"] = RB.alloc((TT,), F32)
        for par in range(2):
            c = CS[par]
            c["rows"] = {}
            for k in ("acum", "nacum", "decend"):
                c["rows"][k] = RB.alloc((128,), F32)
            c["c_xsTM"] = RB.alloc((1024,), BF16)
            c["c_xdt"] = RB.alloc((1024,), BF16)
            c["c_xdec"] = RB.alloc((1024,), BF16)
            c["c_BTM"] = RB.alloc((256,), BF16)
            c["c_cbT"] = RB.alloc((2, 128), F32)
            c["c_dT"] = RB.alloc((16, 128), BF16)
            c["c_MT"] = RB.alloc((16, 128), BF16)
            c["c_t2"] = RB.alloc((1024,), BF16)
            c["c_y"] = RB.alloc((1024,), F32)
        rb_max = RB.off
        RB.off = rb0
        qT = RB.alloc((4, TT), BF16)
        kT = RB.alloc((4, TT), BF16)
        kTM = RB.alloc((NCH, 512), BF16)
        vTM = RB.alloc((NCH, 4 * 129), BF16)
        oTM = RB.alloc((NCH, 512), BF16)
        rows["ig"] = RB.alloc((TT,), F32)
        rows["lf"] = RB.alloc((TT,), F32)
        c = CS[0]
        for k in ("fcum", "u", "cmm", "ncmm", "winter", "emi", "wkr", "wk2"):
            c["rows"][k] = RB.alloc((128,), F32)
        c["c_wT"] = RB.alloc((4, 128), F32)
        c["c_SwT"] = RB.alloc((4, 128), BF16)
        c["c_kw"] = RB.alloc((4, 128), BF16)
        c["c_int"] = RB.alloc((4, 129), F32)
        c["c_hm"] = RB.alloc((4, 128), F32)
        gqT = RB.alloc((4, TT), BF16)
        gkT = RB.alloc((4, TT), BF16)
        gT = RB.alloc((4, TT), F32)
        gviTM = RB.alloc((NCH, 512), BF16)
        ggTM = RB.alloc((NCH, 512), BF16)
        c = CS[1]
        c["c_gcp"] = RB.alloc((4, 129), F32)
        c["c_e"] = RB.alloc((4, 128), F32)
        c["c_qref"] = RB.alloc((4, 128), BF16)
        c["c_qdec"] = RB.alloc((4, 128), BF16)
        c["c_kref"] = [RB.alloc((4, 128), BF16) for _ in range(4)]
        c["c_kend"] = RB.alloc((4, 128), BF16)
        c["c_kendTM"] = RB.alloc((4, 128), BF16)
        c["c_ATb"] = RB.alloc((4, 128), BF16)
        c["c_o"] = RB.alloc((4, 128), F32)
        rb_max = max(rb_max, RB.off)
        RB.off = rb0
        ff0 = RB.off
        actT = RB.alloc((22, TT), BF16)
        sqs = RB.alloc((8, TT), BF16)
        ostg = [RB.t[:, ff0:ff0 + 1024], RB.t[:, ff0 + 1024:ff0 + 2048]]
        xin = [CS[0]["c_y"], CS[1]["c_y"]]
        rb_max = max(rb_max, RB.off)
        RB.off = rb_max
        CK = ["c_xsTM", "c_xdt", "c_xdec", "c_BTM", "c_cbT", "c_dT", "c_MT", "c_t2", "c_y", "rows",
              "c_wT", "c_SwT", "c_kw", "c_int", "c_hm", "c_gcp", "c_e", "c_qref", "c_qdec", "kref0", "kref1", "kref2", "kref3",
              "c_kend", "c_kendTM", "c_ATb", "c_o"]
        ALLK = ["xbcT", "zs", "rowsT", "qT", "kT", "kTM", "vTM", "oTM", "gqT", "gkT", "gT", "gviTM", "ggTM", "actT", "sqsF", "ostg0", "ostg1", "xbcT_b", "sqsF_b"] + \
               ["p%d_%s" % (par, k) for par in range(2) for k in CK]

        def fence_all():
            V("dve", lambda e: e.memset(fence, 0.0), (), ALLK + ["fence"])

        def dbg_out(name, ap_sb, shape, reads):
            if not dbg:
                return
            d = dout("dbg_" + name, shape)
            DBG[name] = shape
            S.dma("pool", d, ap_sb, reads, ())

        def rmsnorm_fm(n, gain_cols, dst, sq_buf, sqkey):
            V("act", lambda e: e.activation(sq_buf[:, 0:4, 0:n], xT[:, 0:4, 0:n], AF.Square), ["xT"], [sqkey])
            V("dve", lambda e: e.tensor_tensor(sq_buf[:, 4:8, 0:n], xT[:, 4:8, 0:n], xT[:, 4:8, 0:n], ALU.mult), ["xT"], [sqkey + "_b"])
            b = nb()

            def f(e):
                for dc in range(8):
                    ins = e.matmul(PS[b][:, 0:n], onesb, sq_buf[:, dc, 0:n], start=(dc == 0), stop=(dc == 7))
                return ins
            V("pe", f, [sqkey, sqkey + "_b", "onesb"], ["P%d" % b])
            V("act", lambda e: e.activation(rstd[:, 0:n], PS[b][:, 0:n], AF.Ln, bias=EPS, scale=1.0 / D), ["P%d" % b], ["rstd"])
            V("act", lambda e: e.activation(rstd[:, 0:n], rstd[:, 0:n], AF.Exp, scale=-0.5), ["rstd"], ["rstd"])
            for dc in range(8):
                V("dve", lambda e, dc=dc: e.scalar_tensor_tensor(dst[:, dc, 0:n], xT[:, dc, 0:n], gain_cols[:, dc:dc + 1], rstd[:, 0:n],
                                                                  ALU.mult, ALU.mult), ["xT", "rstd"], ["hT"])

        def fm_proj(wv, wkey, c0, M, n, src, srckey, ndc):
            b = nb()

            def f(e):
                for dc in range(ndc):
                    ins = e.matmul(PS[b][0:M, 0:n], wv[:, dc, c0:c0 + M], src[:, dc, 0:n], start=(dc == 0), stop=(dc == ndc - 1))
                return ins
            V("pe", f, [wkey, srckey], ["P%d" % b])
            return b

        def tm_proj(wv, wkey, c0, ncol, t0, L):
            b = nb()

            def f(e):
                for dc in range(8):
                    ins = e.matmul(PS[b][0:L, 0:ncol], hT[:, dc, t0:t0 + L], wv[:, dc, c0:c0 + ncol], start=(dc == 0), stop=(dc == 7))
                return ins
            V("pe", f, [wkey, "hT"], ["P%d" % b])
            return b

        def bcast_rows(src_rows_ap, nrow, key_r, c_small, kp):
            b = nb()
            V("dve", lambda e: e.tensor_scalar(c_small[0:nrow, 108:108 + nrow], ident32[0:nrow, 0:nrow], src_rows_ap, None, ALU.mult),
              [key_r, "ident32"], [kp + "c_small_d"])
            V("pe", lambda e: e.matmul(PS[b][:, 0:nrow], ones32[0:nrow, :], c_small[0:nrow, 108:108 + nrow], start=True, stop=True),
              [kp + "c_small_d", "ones32"], ["P%d" % b])
            return b

        def one_chunk(l, tile, ci, ch, which, fn):
            pspool[0] = {"ssd": ci % 2, "ml": 0, "hg": 1}[which]
            try:
                one_chunk_(l, tile, ci, ch, which, fn)
            finally:
                pspool[0] = None

        def one_chunk_(l, tile, ci, ch, which, fn):
            if ch[3]:
                tile["state_init"](l, ch, which)
            fn(l, tile, ci, ch)
            if ch[4]:
                tile["state_store"](l, ch, which)

        def pe_segs(sl):
            return [[spec] for spec in sl]

        def interleave2(la, lb):
            A_, B_ = pe_segs(la), pe_segs(lb)
            i = j = 0
            while i < len(A_) or j < len(B_):
                if i < len(A_):
                    for spec in A_[i]:
                        S.commit(*spec)
                    i += 1
                want = len(B_) if i >= len(A_) else (i * len(B_)) // max(1, len(A_))
                while j < want:
                    for spec in B_[j]:
                        S.commit(*spec)
                    j += 1

        def pipelined(lists, l):
            skeys = set(["h32_%d" % l, "hTb", "C32_%d" % l, "Cb", "m%d" % l, "S32_%d" % l, "Sb", "io_ssd", "io_n", "io_conv"])

            def segs(sl):
                return [[spec] for spec in sl]
            parts = []
            for sl in lists:
                idx = len(sl)
                for i, spec in enumerate(sl):
                    if skeys.intersection(spec[2]) or skeys.intersection(spec[3]):
                        idx = i
                        break
                if LAG <= 0:
                    idx = 0
                parts.append((segs(sl[:idx]), segs(sl[idx:])))
            for seg in parts[0][0]:
                for spec in seg:
                    S.commit(*spec)
            for c in range(len(parts)):
                Bc = parts[c][1]
                An = parts[c + 1][0] if c + 1 < len(parts) else []
                i = j = 0
                while i < len(Bc) or j < len(An):
                    if i < len(Bc):
                        for spec in Bc[i]:
                            S.commit(*spec)
                        i += 1
                    want = len(An) if i >= len(Bc) else (i * len(An)) // max(1, len(Bc))
                    while j < want:
                        for spec in An[j]:
                            S.commit(*spec)
                        j += 1

        def ml_s0_prep(l):
            V("dve", lambda e: e.tensor_tensor(h0buf, xT[:, :, 0], pv[l][:, 0:8], ALU.mult), ["xT", "pv%d" % l], ["h0buf"])
            V("dve", lambda e: e.tensor_scalar(h0buf, h0buf, rstd[:, 0:1], None, ALU.mult), ["h0buf", "rstd"], ["h0buf"])

        def ml_s0_exact(l):
            bq, bk = nb(), nb()
            wbufs = [(w32a, ["w32a"]), (c_sc1, ["p0_c_sc", "p1_c_sc2"])]
            for dc in range(8):
                wb_, wk_ = wbufs[dc % 2]
                S.dma("sp", wb_, w_in[l, dc * 128:(dc + 1) * 128, 2576:3600], (), wk_)

                def f(e, dc=dc, wb_=wb_):
                    e.matmul(PS[bq][0:1, 0:512], h0buf[:, dc:dc + 1], wb_[:, 0:512], start=(dc == 0), stop=(dc == 7))
                    return e.matmul(PS[bk][0:1, 0:512], h0buf[:, dc:dc + 1], wb_[:, 512:1024], start=(dc == 0), stop=(dc == 7))
                V("pe", f, wk_ + ["h0buf"], ["P%d" % bq, "P%d" % bk])
            V("act", lambda e: e.copy(q0row[0:1, :], PS[bq][0:1, 0:512]), ["P%d" % bq], ["q0row"])
            V("dve", lambda e: e.tensor_tensor(q0row[0:1, :], q0row[0:1, :], PS[bk][0:1, 0:512], ALU.mult), ["q0row", "P%d" % bk], ["q0row"])
            V("dve", lambda e: e.tensor_reduce(s0buf[0:1, :], q0row[0:1, :].rearrange("p (h c) -> p h c", h=4), AX.X, ALU.add), ["q0row"], ["s0buf"])
            V("dve", lambda e: e.tensor_scalar(s0buf[0:1, :], s0buf[0:1, :], 128.0 ** -0.5, None, ALU.mult), ["s0buf"], ["s0buf"])

        def layer(l, tile):
            n = tile["n"]
            segs = tile["segs"]
            chunks = tile["chunks"]
            nseg = len(segs)
            ls = segs[0][1]
            P = pv[l]
            H = hs[l]
            ST = stt[l]
            w_in_v = w_in_b[l].rearrange("(dc p) c -> p dc c", p=128)
            w_in_f = w_in[l].rearrange("(dc p) c -> p dc c", p=128)
            fence_all()
            rmsnorm_fm(n, P[:, 0:8], hT, xbcT[:, 0:8, :], "xbcT")
            seq_start = (tile["kind"] == "p" and chunks[0][3])
            if seq_start:
                ml_s0_prep(l)
            wvz, wkeyz = wload(w_in_v[:, :, 0:1024], w_in_f[:, :, 0:1024], 8, 1024, "wb_in%d_0" % l)
            zjobs = [(ci, ch, hf) for ci, ch in enumerate(chunks) for hf in range(2)]

            def zjob(ci, ch, hf):
                (c0, L, seq, first, last) = ch
                b = tm_proj(wvz, wkeyz, hf * 512, 512, c0, L)
                V("act", lambda e: e.activation(zs[0:L, ci, hf * 512:(hf + 1) * 512], PS[b][0:L, :], AF.Silu), ["P%d" % b], ["zs"])

            def conv_chunk(b, cc):
                if cc % 2 == 0:
                    xpb, caccb, kx, kc = xp, cacc, "xp", "cacc"
                else:
                    xpb, caccb, kx, kc = CS[0]["c_y"], CS[1]["c_y"], "p0_c_y", "p1_c_y"
                xp3 = xpb[:, 0:nseg * (ls + 3)].rearrange("p (s t) -> p s t", s=nseg)
                if tile["kind"] == "p":
                    V("dve", lambda e: e.tensor_copy(xp3[:, 0, 0:3], ST["conv"][:, cc, :]), ["conv%d" % l], [kx])
                else:
                    V("dve", lambda e: e.tensor_copy(xp3[:, :, 0:3], convin[:, 0:nseg, cc, :]), ["convin"], [kx])
                V("act", lambda e: e.copy(xp3[:, :, 3:3 + ls], PS[b][:, 0:n].rearrange("p (s t) -> p s t", s=nseg)), ["P%d" % b], [kx])
                if tile["kind"] == "p":
                    V("dve", lambda e: e.tensor_copy(ST["conv"][:, cc, :], xp3[:, 0, ls:ls + 3]), [kx], ["conv%d" % l])
                else:
                    V("dve", lambda e: e.tensor_copy(convout[:, 0:nseg, cc, :], xp3[:, :, ls:ls + 3]), [kx], ["convout"])
                acc3 = caccb[:, 0:n].rearrange("p (s t) -> p s t", s=nseg)
                V("act", lambda e: e.activation(acc3, xp3[:, :, 0:ls], AF.Identity, scale=P[:, 8 + cc:9 + cc]), [kx], [kc])
                for w in range(1, 4):
                    V("dve", lambda e, w=w: e.scalar_tensor_tensor(acc3, xp3[:, :, w:w + ls], P[:, 8 + w * 12 + cc:9 + w * 12 + cc], acc3,
                                                                    ALU.mult, ALU.add), [kx, kc], [kc])
                V("act", lambda e: e.activation(xbcT[:, cc, 0:n], caccb[:, 0:n], AF.Silu, bias=P[:, 56 + cc:57 + cc]), [kc], ["xbcT"])

            wv, wkey = wload(w_in_v[:, :, 1024:2048], w_in_f[:, :, 1024:2048], 8, 1024, "wb_in%d_1024" % l)
            for cc in range(8):
                b = fm_proj(wv, wkey, cc * 128, 128, n, hT, "hT", 8)
                conv_chunk(b, cc)
                if zjobs:
                    zjob(*zjobs.pop(0))
            while zjobs:
                zjob(*zjobs.pop(0))
            wv, wkey = wload(w_in_v[:, :, 2048:2576], w_in_f[:, :, 2048:2576], 8, 528, "wb_in%d_2048" % l)
            for cc in range(8, 12):
                b = fm_proj(wv, wkey, (cc - 8) * 128, 128, n, hT, "hT", 8)
                conv_chunk(b, cc)
            b = fm_proj(wv, wkey, 512, 16, n, hT, "hT", 8)
            R = rows
            V("act", lambda e: e.activation(R["dt"][0:16, 0:n], PS[b][0:16, 0:n], AF.Exp, bias=H["dtb"][0:16, :]), ["P%d" % b, "hs%d" % l], ["rowsT"])
            V("act", lambda e: e.activation(R["dt"][0:16, 0:n], R["dt"][0:16, 0:n], AF.Ln, bias=1.0), ["rowsT"], ["rowsT"])
            V("dve", lambda e: e.tensor_scalar(R["a"][0:16, 0:n], R["dt"][0:16, 0:n], H["A"][0:16, :], None, ALU.mult), ["rowsT", "hs%d" % l], ["rowsT"])
            pipelined([S.capture(one_chunk, l, tile, ci, ch, "ssd", ssd_chunk) for ci, ch in enumerate(chunks)], l)
            if dbg and l == 0 and tile.get("first_tile"):
                dbg_out("xbcT", xbcT[:, :, 0:n], [128, 12, n], ["xbcT"])
                dbg_out("hT", hT[:, :, 0:n], [128, 8, n], ["hT"])
                dbg_out("dtrow", R["dt"][0:16, 0:n], [16, n], ["rowsT"])

            fence_all()
            if seq_start:
                ml_s0_exact(l)
            wv, wkey = wload(w_in_v[:, :, 2576:3600], w_in_f[:, :, 2576:3600], 8, 1024, "wb_in%d_2576" % l)
            for hh in range(4):
                b = fm_proj(wv, wkey, hh * 128, 128, n, hT, "hT", 8)
                V("act", lambda e, b=b, hh=hh: e.copy(qT[:, hh, 0:n], PS[b][:, 0:n]), ["P%d" % b], ["qT"])
                b = fm_proj(wv, wkey, 512 + hh * 128, 128, n, hT, "hT", 8)
                V("act", lambda e, b=b, hh=hh: e.activation(kT[:, hh, 0:n], PS[b][:, 0:n], AF.Identity, scale=128.0 ** -0.5), ["P%d" % b], ["kT"])
            for ci, (c0, L, seq, first, last) in enumerate(chunks):
                b = tm_proj(wv, wkey, 512, 512, c0, L)
                V("act", lambda e, b=b, ci=ci, L=L: e.activation(kTM[0:L, ci, :], PS[b][0:L, :], AF.Identity, scale=128.0 ** -0.5), ["P%d" % b], ["kTM"])
            wv, wkey = wload(w_in_v[:, :, 3600:4120], w_in_f[:, :, 3600:4120], 8, 520, "wb_in%d_3600" % l)
            for ci, (c0, L, seq, first, last) in enumerate(chunks):
                b = tm_proj(wv, wkey, 0, 512, c0, L)
                v4 = vTM[0:L, ci, :].rearrange("p (h c) -> p h c", h=4)
                V("act", lambda e, b=b, v4=v4, L=L: e.copy(v4[:, :, 0:128], PS[b][0:L, :].rearrange("p (h c) -> p h c", h=4)), ["P%d" % b], ["vTM"])
                V("dve", lambda e, v4=v4: e.memset(v4[:, :, 128:129], 1.0), (), ["vTM"])
            b = fm_proj(wv, wkey, 512, 4, n, hT, "hT", 8)
            V("act", lambda e, b=b: e.activation(R["ig"][0:4, 0:n], PS[b][0:4, 0:n], AF.Identity, bias=H["bi"][0:4, :]), ["P%d" % b, "hs%d" % l], ["rowsT"])
            b = fm_proj(wv, wkey, 516, 4, n, hT, "hT", 8)
            V("act", lambda e, b=b: e.activation(R["lf"][0:4, 0:n], PS[b][0:4, 0:n], AF.Exp, bias=H["nbf"][0:4, :], scale=-1.0), ["P%d" % b, "hs%d" % l], ["rowsT"])
            V("act", lambda e: e.activation(R["lf"][0:4, 0:n], R["lf"][0:4, 0:n], AF.Ln, bias=1.0), ["rowsT"], ["rowsT"])
            V("dve", lambda e: e.tensor_scalar(R["lf"][0:4, 0:n], R["lf"][0:4, 0:n], -1.0, None, ALU.mult), ["rowsT"], ["rowsT"])
            wv, wkey = wload(w_in_v[:, :, 4120:4632], w_in_f[:, :, 4120:4632], 8, 512, "wb_in%d_4120" % l)
            for ci, (c0, L, seq, first, last) in enumerate(chunks):
                b = tm_proj(wv, wkey, 0, 512, c0, L)
                V("act", lambda e, b=b, ci=ci, L=L: e.activation(oTM[0:L, ci, :], PS[b][0:L, :], AF.Sigmoid), ["P%d" % b], ["oTM"])

            for a in range(4):
                V("dve", lambda e, a=a: e.memset(CS[1]["c_kref"][a], 0.0), (), ["p1_kref%d" % a])
            wv, wkey = wload(w_in_v[:, :, 4632:5656], w_in_f[:, :, 4632:5656], 8, 1024, "wb_in%d_4632" % l)
            for hh in range(4):
                b = fm_proj(wv, wkey, hh * 128, 128, n, hT, "hT", 8)
                V("act", lambda e, b=b, hh=hh: e.activation(gqT[:, hh, 0:n], PS[b][:, 0:n], AF.Silu), ["P%d" % b], ["gqT"])
            for hh in range(4):
                b = fm_proj(wv, wkey, 512 + hh * 128, 128, n, hT, "hT", 8)
                fb, fk = (cacc, "cacc") if hh % 2 == 0 else (xp, "xp")
                V("act", lambda e, b=b, fb=fb: e.activation(fb[:, 0:n], PS[b][:, 0:n], AF.Sigmoid), ["P%d" % b], [fk])
                V("dve", lambda e, hh=hh, fb=fb: e.tensor_scalar(fb[:, 0:n], fb[:, 0:n], H["oml"][:, hh:hh + 1], H["lb"][:, hh:hh + 1], ALU.mult, ALU.add),
                  [fk, "hs%d" % l], [fk])
                V("dve", lambda e, hh=hh, fb=fb: e.tensor_scalar(gkT[:, hh, 0:n], fb[:, 0:n], -1.0, 1.0, ALU.mult, ALU.add), [fk], ["gkT"])
                V("act", lambda e, hh=hh, fb=fb: e.activation(gT[:, hh, 0:n], fb[:, 0:n], AF.Ln), [fk], ["gT"])
            wv, wkey = wload(w_in_v[:, :, 5656:6680], w_in_f[:, :, 5656:6680], 8, 1024, "wb_in%d_5656" % l)
            for ci, (c0, L, seq, first, last) in enumerate(chunks):
                b = tm_proj(wv, wkey, 0, 512, c0, L)
                V("act", lambda e, b=b, ci=ci, L=L: e.copy(gviTM[0:L, ci, :], PS[b][0:L, :]), ["P%d" % b], ["gviTM"])
                b = tm_proj(wv, wkey, 512, 512, c0, L)
                V("act", lambda e, b=b, ci=ci, L=L: e.activation(ggTM[0:L, ci, :], PS[b][0:L, :], AF.Silu), ["P%d" % b], ["ggTM"])
            for ci, ch in enumerate(chunks):
                la = S.capture(one_chunk, l, tile, ci, ch, "ml", mlstm_chunk)
                lb = S.capture(one_chunk, l, tile, ci, ch, "hg", hgrn_chunk)
                interleave2(la, lb)
            if dbg and l == 0 and tile.get("first_tile"):
                dbg_out("mixT", mixT[:, :, 0:n], [128, 16, n], ["mixT"])

            w_out_v = w_out_b[l].rearrange("(cc p) d -> p cc d", p=128)
            w_out_f = w_out[l].rearrange("(cc p) d -> p cc d", p=128)
            for hf in range(2):
                wv, wkey = wload(w_out_v[:, :, hf * 512:(hf + 1) * 512], w_out_f[:, :, hf * 512:(hf + 1) * 512], 16, 512, "wb_out%d_%d" % (l, hf))
                for j in range(4):
                    b = fm_proj(wv, wkey, j * 128, 128, n, mixT, "mixT", 16)
                    dc = hf * 4 + j
                    V("dve", lambda e, b=b, dc=dc: e.tensor_tensor(xT[:, dc, 0:n], xT[:, dc, 0:n], PS[b][:, 0:n], ALU.add), ["P%d" % b, "xT"], ["xT"])
            fence_all()
            rmsnorm_fm(n, P[:, 88:96], hT, sqs, "sqsF")
            w1 = w_ffn_in_b[l].rearrange("(dc p) c -> p dc c", p=128)
            w1f = w_ffn_in[l].rearrange("(dc p) c -> p dc c", p=128)
            for t in range(6):
                c0 = t * 1024
                nc_ = min(1024, 2 * DFF - c0)
                wv, wkey = wload(w1[:, :, c0:c0 + nc_], w1f[:, :, c0:c0 + nc_], 8, nc_, "wb_f1%d_%d" % (l, c0))
                for j in range(nc_ // 128):
                    fch = (c0 // 128) + j
                    b = fm_proj(wv, wkey, j * 128, 128, n, hT, "hT", 8)
                    if fch < 22:
                        V("act", lambda e, b=b, fch=fch: e.activation(actT[:, fch, 0:n], PS[b][:, 0:n], AF.Silu), ["P%d" % b], ["actT"])
                    else:
                        f2 = fch - 22
                        V("dve", lambda e, b=b, f2=f2: e.tensor_tensor(actT[:, f2, 0:n], actT[:, f2, 0:n], PS[b][:, 0:n], ALU.mult),
                          ["P%d" % b, "actT"], ["actT"])
            w2 = w_ffn_out_b[l].rearrange("(fc p) d -> p fc d", p=128)
            w2f = w_ffn_out[l].rearrange("(fc p) d -> p fc d", p=128)
            for t in range(4):
                wv, wkey = wload(w2[:, :, t * 256:(t + 1) * 256], w2f[:, :, t * 256:(t + 1) * 256], 22, 256, "wb_f2%d_%d" % (l, t))
                for j in range(2):
                    b = fm_proj(wv, wkey, j * 128, 128, n, actT, "actT", 22)
                    dc = t * 2 + j
                    V("dve", lambda e, b=b, dc=dc: e.tensor_tensor(xT[:, dc, 0:n], xT[:, dc, 0:n], PS[b][:, 0:n], ALU.add), ["P%d" % b, "xT"], ["xT"])

        def ssd_chunk(l, tile, ci, ch):
            (c0, L, seq, first, last) = ch
            H = hs[l]
            ST = stt[l]
            P = pv[l]
            par = ci % 2
            kp = "p%d_" % par
            c = CS[par]
            R = dict(rows)
            R.update(c["rows"])
            sm = c["c_small"]
            c_sc = c["c_sc"]
            c_xsTM = c["c_xsTM"]
            c_xdt = c["c_xdt"]
            c_xdec = c["c_xdec"]
            c_BTM = c["c_BTM"]
            c_cbT = c["c_cbT"]
            c_dT = c["c_dT"]
            c_MT = c["c_MT"]
            c_t2 = c["c_t2"]
            c_y = c["c_y"]
            V("dve", lambda e: e.tensor_tensor_scan(R["acum"][0:16, 0:L], ones32[0:16, 0:L], R["a"][0:16, c0:c0 + L], 0.0, ALU.mult, ALU.add),
              [kp + "rows", "rowsT", "ones32"], [kp + "rows"])
            hl = sm[0:16, 128:256].bitcast(BF16)
            nhl = R["nacum"][0:16, :].bitcast(BF16)
            a_hi, a_lo, n_hi, n_lo = hl[:, 0:L], hl[:, 128:128 + L], nhl[:, 0:L], nhl[:, 128:128 + L]
            V("dve", lambda e: e.tensor_copy(a_hi, R["acum"][0:16, 0:L]), [kp + "rows"], [kp + "c_small"])
            V("dve", lambda e: e.tensor_tensor(R["decend"][0:16, 0:L], R["acum"][0:16, 0:L], a_hi, ALU.subtract), [kp + "rows", kp + "c_small"], [kp + "rows"])
            V("dve", lambda e: e.tensor_copy(a_lo, R["decend"][0:16, 0:L]), [kp + "rows"], [kp + "c_small"])
            V("dve", lambda e: e.tensor_scalar(n_hi, a_hi, -1.0, None, ALU.mult), [kp + "c_small"], [kp + "rows"])
            V("dve", lambda e: e.tensor_scalar(n_lo, a_lo, -1.0, None, ALU.mult), [kp + "c_small"], [kp + "rows"])
            V("act", lambda e: e.activation(R["decend"][0:16, 0:L], R["acum"][0:16, 0:L], AF.Exp,
                                            bias=R["acum"][0:16, L - 1:L], scale=-1.0), [kp + "rows"], [kp + "rows"])
            b = nb()

            def f(e):
                e.transpose(PS[b][0:L, 0:16], R["dt"][0:16, c0:c0 + L], ident32[0:16, 0:16])
                e.transpose(PS[b][0:L, 16:32], R["acum"][0:16, 0:L], ident32[0:16, 0:16])
                return e.transpose(PS[b][0:L, 32:48], R["decend"][0:16, 0:L], ident32[0:16, 0:16])
            V("pe", f, [kp + "rows", "rowsT", "ident32"], ["P%d" % b])
            V("dve", lambda e: e.tensor_copy(sm[0:L, 0:16], PS[b][0:L, 0:16]), ["P%d" % b], [kp + "c_small"])
            V("act", lambda e: e.activation(sm[0:L, 16:32], PS[b][0:L, 16:32], AF.Exp), ["P%d" % b], [kp + "c_small"])
            V("dve", lambda e: e.tensor_tensor(sm[0:L, 32:48], PS[b][0:L, 32:48], sm[0:L, 0:16], ALU.mult), ["P%d" % b, kp + "c_small"], [kp + "c_small"])
            bb = bcast_rows(R["acum"][0:16, L - 1:L], 16, kp + "rows", sm, kp)
            V("act", lambda e: e.activation(sm[:, 48:64], PS[bb][:, 0:16], AF.Exp), ["P%d" % bb], [kp + "c_small"])
            b1 = nb()
            pb1 = PS[b1].bitcast(BF16)

            def f(e):
                for cc in range(8):
                    ins = e.transpose(pb1[0:L, cc * 128:(cc + 1) * 128], xbcT[:, cc, c0:c0 + L], identb)
                return ins
            V("pe", f, ["xbcT", "identb"], ["P%d" % b1])
            b2 = nb()
            pb2 = PS[b2].bitcast(BF16)

            def f(e):
                for cc in range(2):
                    ins = e.transpose(pb2[0:L, cc * 128:(cc + 1) * 128], xbcT[:, 8 + cc, c0:c0 + L], identb)
                return ins
            V("pe", f, ["xbcT", "identb"], ["P%d" % b2])
            V("act", lambda e: e.copy(c_xsTM[0:L, :], pb1[0:L, :]), ["P%d" % b1], [kp + "c_xsTM"])
            V("act", lambda e: e.copy(c_BTM[0:L, :], pb2[0:L, 0:256]), ["P%d" % b2], [kp + "c_BTM"])
            xs3 = c_xsTM[0:L, :].rearrange("p (h c) -> p h c", h=16)
            V("dve", lambda e: e.tensor_tensor(c_xdt[0:L, :].rearrange("p (h c) -> p h c", h=16), xs3,
                                               sm[0:L, 0:16].unsqueeze(2).broadcast_to([L, 16, 64]), ALU.mult), [kp + "c_xsTM", kp + "c_small"], [kp + "c_xdt"])
            V("dve", lambda e: e.tensor_tensor(c_xdec[0:L, :].rearrange("p (h c) -> p h c", h=16), xs3,
                                               sm[0:L, 32:48].unsqueeze(2).broadcast_to([L, 16, 64]), ALU.mult), [kp + "c_xsTM", kp + "c_small"], [kp + "c_xdec"])
            b = nb()

            def f(e):
                for g in range(2):
                    ins = e.matmul(PS[b][0:L, g * 128:g * 128 + L], xbcT[:, 8 + g, c0:c0 + L], xbcT[:, 10 + g, c0:c0 + L], start=True, stop=True)
                return ins
            V("pe", f, ["xbcT"], ["P%d" % b])
            V("act", lambda e: e.copy(c_cbT[0:L, :, 0:L], PS[b][0:L, 0:256].rearrange("p (g c) -> p g c", g=2)[:, :, 0:L]), ["P%d" % b], [kp + "c_cbT"])
            for q4 in range(4):
                b = nb()

                def f(e, q4=q4, b=b):
                    if L == 128:
                        rhs4 = identb[0:16, 4 * q4:4 * q4 + 4].unsqueeze(2).broadcast_to([16, 4, 128])
                        e.matmul(PS[b][0:L, :], n_hi, rhs4, start=True, stop=False)
                        e.matmul(PS[b][0:L, :], n_lo, rhs4, start=False, stop=False)
                        e.matmul(PS[b][0:L, :], identb[0:L, 0:L], negmask4[0:L, :, :], start=False, stop=False)
                    for j in range(4):
                        hh = q4 * 4 + j
                        o = PS[b][0:L, j * 128:j * 128 + L]
                        selb = identb[0:16, hh:hh + 1].broadcast_to([16, L])
                        if L != 128:
                            e.matmul(o, n_hi, selb, start=True, stop=False)
                            e.matmul(o, n_lo, selb, start=False, stop=False)
                            e.matmul(o, identb[0:L, 0:L], negmask4[0:L, 0, 0:L], start=False, stop=False)
                        e.matmul(o, selb, a_hi, start=False, stop=False)
                        ins = e.matmul(o, selb, a_lo, start=False, stop=(j == 3 or L != 128))
                    return ins
                V("pe", f, [kp + "rows", kp + "c_small", "identb", "negmask4"], ["P%d" % b])
                V("act", lambda e, q4=q4, b=b: e.activation(c_dT[0:L, q4 * 4:(q4 + 1) * 4, 0:L],
                                                            PS[b][0:L, :].rearrange("p (j c) -> p j c", j=4)[:, :, 0:L], AF.Exp), ["P%d" % b], [kp + "c_dT"])
            for g in range(2):
                V("dve", lambda e, g=g: e.tensor_tensor(c_MT[0:L, g * 8:(g + 1) * 8, 0:L], c_dT[0:L, g * 8:(g + 1) * 8, 0:L],
                                                        c_cbT[0:L, g:g + 1, 0:L].broadcast_to([L, 8, L]), ALU.mult), [kp + "c_dT", kp + "c_cbT"], [kp + "c_MT"])
            V("act", lambda e: e.copy(hTb, ST["h32"]), ["h32_%d" % l], ["hTb"])
            byi = [nb(), nb()]
            bys = [nb(), nb()]
            for g in range(2):
                def f(e, g=g):
                    for j in range(8):
                        hh = g * 8 + j
                        ins = e.matmul(PS[byi[g]][0:L, j * 64:(j + 1) * 64], c_MT[0:L, hh, 0:L], c_xdt[0:L, hh * 64:(hh + 1) * 64], start=True, stop=True)
                    return ins
                V("pe", f, [kp + "c_MT", kp + "c_xdt"], ["P%d" % byi[g]])
                V("pe", lambda e, g=g: e.matmul(PS[bys[g]][0:L, :], xbcT[:, 10 + g, c0:c0 + L], hTb[:, g * 512:(g + 1) * 512], start=True, stop=True),
                  ["xbcT", "hTb"], ["P%d" % bys[g]])
            V("dve", lambda e: e.tensor_tensor(c_t2[0:L, :].rearrange("p (h c) -> p h c", h=16), xs3,
                                               H["Dbc"][0:L, :].unsqueeze(2).broadcast_to([L, 16, 64]), ALU.mult), [kp + "c_xsTM", "hs%d" % l], [kp + "c_t2"])
            for g in range(2):
                sl = slice(g * 512, (g + 1) * 512)
                V("dve", lambda e, g=g, sl=sl: e.tensor_tensor(c_y[0:L, sl].rearrange("p (h c) -> p h c", h=8),
                                                               PS[bys[g]][0:L, :].rearrange("p (h c) -> p h c", h=8),
                                                               sm[0:L, 16 + g * 8:24 + g * 8].unsqueeze(2).broadcast_to([L, 8, 64]), ALU.mult),
                  ["P%d" % bys[g], kp + "c_small"], [kp + "c_y"])
                V("dve", lambda e, g=g, sl=sl: e.tensor_tensor(c_y[0:L, sl], c_y[0:L, sl], PS[byi[g]][0:L, :], ALU.add), ["P%d" % byi[g], kp + "c_y"], [kp + "c_y"])
            V("dve", lambda e: e.tensor_tensor(c_y[0:L, :], c_y[0:L, :], c_t2[0:L, :], ALU.add), [kp + "c_y", kp + "c_t2"], [kp + "c_y"])
            V("dve", lambda e: e.tensor_tensor(c_y[0:L, :], c_y[0:L, :], zs[0:L, ci, :], ALU.mult), [kp + "c_y", "zs"], [kp + "c_y"])
            V("act", lambda e: e.activation(c_t2[0:L, :], c_y[0:L, :], AF.Square, accum_out=sm[0:L, 64:65]), [kp + "c_y"], [kp + "c_t2", kp + "c_small"])
            V("act", lambda e: e.activation(sm[0:L, 64:65], sm[0:L, 64:65], AF.Ln, bias=EPS, scale=1.0 / 1024), [kp + "c_small"], [kp + "c_small"])
            V("act", lambda e: e.activation(sm[0:L, 64:65], sm[0:L, 64:65], AF.Exp, scale=-0.5), [kp + "c_small"], [kp + "c_small"])
            V("act", lambda e: e.activation(c_y[0:L, :], c_y[0:L, :], AF.Identity, scale=sm[0:L, 64:65]), [kp + "c_y", kp + "c_small"], [kp + "c_y"])
            for hf in range(2):
                b = nb()

                def f(e, hf=hf, b=b):
                    for j in range(4):
                        cc = hf * 4 + j
                        ins = e.transpose(PS[b][:, j * 128:j * 128 + L], c_y[0:L, cc * 128:(cc + 1) * 128], ident32[0:L, 0:L])
                    return ins
                V("pe", f, [kp + "c_y", "ident32"], ["P%d" % b])
                if hf == 0:
                    V("dve", lambda e, b=b: e.tensor_tensor(mixT[:, 0:4, c0:c0 + L], PS[b][:, :].rearrange("p (j c) -> p j c", j=4)[:, :, 0:L],
                                                            pv[l][:, 68:72].unsqueeze(2).broadcast_to([128, 4, L]), ALU.mult), ["P%d" % b, "pv%d" % l], ["mixT"])
                else:
                    for j in range(4):
                        cc = hf * 4 + j
                        V("act", lambda e, b=b, j=j, cc=cc: e.activation(mixT[:, cc, c0:c0 + L], PS[b][:, j * 128:j * 128 + L], AF.Copy,
                                                                         scale=pv[l][:, 68 + cc:69 + cc]), ["P%d" % b], ["mixT"])
            bu = [nb(), nb()]
            for g in range(2):
                V("pe", lambda e, g=g: e.matmul(PS[bu[g]][:, :], c_BTM[0:L, g * 128:(g + 1) * 128], c_xdec[0:L, g * 512:(g + 1) * 512], start=True, stop=True),
                  [kp + "c_BTM", kp + "c_xdec"], ["P%d" % bu[g]])
            V("dve", lambda e: e.tensor_tensor(ST["h32"].rearrange("p (h c) -> p h c", h=16), ST["h32"].rearrange("p (h c) -> p h c", h=16),
                                               sm[:, 48:64].unsqueeze(2).broadcast_to([128, 16, 64]), ALU.mult), ["h32_%d" % l, kp + "c_small", "hTb"], ["h32_%d" % l])
            for g in range(2):
                sl = slice(g * 512, (g + 1) * 512)
                V("dve", lambda e, g=g, sl=sl: e.tensor_tensor(ST["h32"][:, sl], ST["h32"][:, sl], PS[bu[g]][:, :], ALU.add),
                  ["h32_%d" % l, "P%d" % bu[g]], ["h32_%d" % l])

        def mlstm_chunk(l, tile, ci, ch):
            (c0, L, seq, first, last) = ch
            H = hs[l]
            ST = stt[l]
            cs = slice(c0, c0 + L)
            par = 0
            kp = "p%d_" % par
            c = CS[par]
            R = dict(rows)
            R.update(c["rows"])
            sm = c["c_small"]
            c_sc = c["c_sc"]
            c_wT = c["c_wT"]
            c_SwT = c["c_SwT"]
            c_kw = c["c_kw"]
            c_int = c["c_int"]
            c_hm = c["c_hm"]
            m0 = ST["m"][0:4, :]
            V("dve", lambda e: e.tensor_tensor_scan(R["fcum"][0:4, 0:L], ones32[0:4, 0:L], R["lf"][0:4, cs], 0.0, ALU.mult, ALU.add), [kp + "rows", "rowsT", "ones32"], [kp + "rows"])
            V("dve", lambda e: e.tensor_tensor(R["u"][0:4, 0:L], R["ig"][0:4, cs], R["fcum"][0:4, 0:L], ALU.subtract), [kp + "rows", "rowsT"], [kp + "rows"])
            V("dve", lambda e: e.tensor_tensor_scan(R["cmm"][0:4, 0:L], ones32[0:4, 0:L], R["u"][0:4, 0:L], m0, ALU.mult, ALU.max), [kp + "rows", "ones32", "m%d" % l], [kp + "rows"])
            V("dve", lambda e: e.tensor_scalar(R["ncmm"][0:4, 0:L], R["cmm"][0:4, 0:L], -1.0, None, ALU.mult), [kp + "rows"], [kp + "rows"])
            hl = sm[0:4, 128:256].bitcast(BF16)
            nhl = R["wkr"][0:4, :].bitcast(BF16)
            u_hi, u_lo, c_hi, c_lo = hl[:, 0:L], hl[:, 128:128 + L], nhl[:, 0:L], nhl[:, 128:128 + L]
            V("act", lambda e: e.activation(R["winter"][0:4, 0:L], R["cmm"][0:4, 0:L], AF.Exp, bias=m0, scale=-1.0), [kp + "rows", "m%d" % l], [kp + "rows"])
            V("dve", lambda e: e.tensor_tensor(R["emi"][0:4, 0:L], R["fcum"][0:4, 0:L], R["cmm"][0:4, 0:L], ALU.add), [kp + "rows"], [kp + "rows"])
            V("act", lambda e: e.activation(R["emi"][0:4, 0:L], R["emi"][0:4, 0:L], AF.Exp, scale=-1.0), [kp + "rows"], [kp + "rows"])
            V("act", lambda e: e.activation(R["wk2"][0:4, 0:L], R["u"][0:4, 0:L], AF.Exp, bias=R["ncmm"][0:4, L - 1:L]), [kp + "rows"], [kp + "rows"])
            V("act", lambda e: e.activation(sm[0:4, 70:71], R["cmm"][0:4, L - 1:L], AF.Exp, bias=m0, scale=-1.0), [kp + "rows", "m%d" % l], [kp + "c_small"])
            V("dve", lambda e: e.tensor_tensor(ST["m"][0:4, :], R["fcum"][0:4, L - 1:L], R["cmm"][0:4, L - 1:L], ALU.add),
              [kp + "rows"], ["m%d" % l])
            b = nb()

            def f(e):
                e.transpose(PS[b][0:L, 0:4], R["winter"][0:4, 0:L], ident32[0:4, 0:4])
                e.transpose(PS[b][0:L, 4:8], R["emi"][0:4, 0:L], ident32[0:4, 0:4])
                return e.transpose(PS[b][0:L, 8:12], R["wk2"][0:4, 0:L], ident32[0:4, 0:4])
            V("pe", f, [kp + "rows", "ident32"], ["P%d" % b])
            V("dve", lambda e: e.tensor_copy(sm[0:L, 72:84], PS[b][0:L, 0:12]), ["P%d" % b], [kp + "c_small"])
            b = nb()

            V("dve", lambda e: e.tensor_copy(u_hi, R["u"][0:4, 0:L]), [kp + "rows"], [kp + "c_small"])
            V("dve", lambda e: e.tensor_tensor(R["fcum"][0:4, 0:L], R["u"][0:4, 0:L], u_hi, ALU.subtract), [kp + "rows", kp + "c_small"], [kp + "rows"])
            V("dve", lambda e: e.tensor_copy(u_lo, R["fcum"][0:4, 0:L]), [kp + "rows"], [kp + "c_small"])
            V("dve", lambda e: e.tensor_copy(c_hi, R["ncmm"][0:4, 0:L]), [kp + "rows"], [kp + "rows"])
            V("dve", lambda e: e.tensor_tensor(R["fcum"][0:4, 0:L], R["ncmm"][0:4, 0:L], c_hi, ALU.subtract), [kp + "rows"], [kp + "rows"])
            V("dve", lambda e: e.tensor_copy(c_lo, R["fcum"][0:4, 0:L]), [kp + "rows"], [kp + "rows"])

            def f(e):
                if L == 128:
                    rhs4 = identb[0:4, 0:4].unsqueeze(2).broadcast_to([4, 4, 128])
                    e.matmul(PS[b][0:L, :], u_hi, rhs4, start=True, stop=False)
                    e.matmul(PS[b][0:L, :], u_lo, rhs4, start=False, stop=False)
                    e.matmul(PS[b][0:L, :], identb[0:L, 0:L], negmask4[0:L, :, :], start=False, stop=False)
                for hh in range(4):
                    o = PS[b][0:L, hh * 128:hh * 128 + L]
                    selb = identb[0:4, hh:hh + 1].broadcast_to([4, L])
                    if L != 128:
                        e.matmul(o, u_hi, selb, start=True, stop=False)
                        e.matmul(o, u_lo, selb, start=False, stop=False)
                        e.matmul(o, identb[0:L, 0:L], negmask4[0:L, 0, 0:L], start=False, stop=False)
                    e.matmul(o, selb, c_hi, start=False, stop=False)
                    ins = e.matmul(o, selb, c_lo, start=False, stop=(hh == 3 or L != 128))
                return ins
            V("pe", f, [kp + "rows", kp + "c_small", "identb", "negmask4"], ["P%d" % b])
            V("act", lambda e: e.activation(c_wT[0:L, :, 0:L], PS[b][0:L, :].rearrange("p (j c) -> p j c", j=4)[:, :, 0:L], AF.Exp), ["P%d" % b], [kp + "c_wT"])
            b = nb()

            def f(e):
                for hh in range(4):
                    ins = e.matmul(PS[b][0:L, hh * 128:hh * 128 + L], kT[:, hh, cs], qT[:, hh, cs], start=True, stop=True)
                return ins
            V("pe", f, ["kT", "qT"], ["P%d" % b])
            if first and tile["kind"] == "p":
                V("dve", lambda e: e.tensor_copy(PS[b][0:1, 0:512:128], s0buf[0:1, :]), ["s0buf", "P%d" % b], ["P%d" % b])
            V("dve", lambda e: e.tensor_tensor(c_SwT[0:L, :, 0:L], PS[b][0:L, :].rearrange("p (j c) -> p j c", j=4)[:, :, 0:L], c_wT[0:L, :, 0:L], ALU.mult),
              ["P%d" % b, kp + "c_wT"], [kp + "c_SwT"])
            V("act", lambda e: e.copy(Cb, ST["C32"]), ["C32_%d" % l], ["Cb"])
            bi_ = [nb(), nb()]
            bx_ = [nb(), nb()]
            v4 = vTM[0:L, ci, :].rearrange("p (h c) -> p h c", h=4)
            for pr in range(2):
                def f(e, pr=pr):
                    for j in range(2):
                        hh = pr * 2 + j
                        ins = e.matmul(PS[bi_[pr]][0:L, j * 256:j * 256 + 129], c_SwT[0:L, hh, 0:L], v4[:, hh, :], start=True, stop=True)
                    return ins
                V("pe", f, [kp + "c_SwT", "vTM"], ["P%d" % bi_[pr]])

                def f(e, pr=pr):
                    for j in range(2):
                        hh = pr * 2 + j
                        ins = e.matmul(PS[bx_[pr]][0:L, j * 256:j * 256 + 129], qT[:, hh, cs], Cb[:, hh, :], start=True, stop=True)
                    return ins
                V("pe", f, ["qT", "Cb"], ["P%d" % bx_[pr]])
                for j in range(2):
                    hh = pr * 2 + j
                    V("act", lambda e, pr=pr, j=j, hh=hh: e.activation(c_int[0:L, hh, :], PS[bx_[pr]][0:L, j * 256:j * 256 + 129], AF.Copy,
                                                                       scale=sm[0:L, 72 + hh:73 + hh]), ["P%d" % bx_[pr], kp + "c_small"], [kp + "c_int"])
                V("dve", lambda e, pr=pr: e.tensor_tensor(c_int[0:L, pr * 2:pr * 2 + 2, :], c_int[0:L, pr * 2:pr * 2 + 2, :],
                                                          PS[bi_[pr]][0:L, :].rearrange("p (j c) -> p j c", j=2)[:, :, 0:129], ALU.add),
                  [kp + "c_int", "P%d" % bi_[pr]], [kp + "c_int"])
            V("act", lambda e: e.activation(sm[0:L, 84:88], c_int[0:L, :, 128], AF.Abs), [kp + "c_int"], [kp + "c_small"])
            V("dve", lambda e: e.tensor_tensor(sm[0:L, 84:88], sm[0:L, 84:88], sm[0:L, 76:80], ALU.max), [kp + "c_small"], [kp + "c_small"])
            V("dve", lambda e: e.reciprocal(sm[0:L, 84:88], sm[0:L, 84:88]), [kp + "c_small"], [kp + "c_small"])
            V("dve", lambda e: e.tensor_tensor(c_hm[0:L, :, :], c_int[0:L, :, 0:128], sm[0:L, 84:88].unsqueeze(2).broadcast_to([L, 4, 128]), ALU.mult),
              [kp + "c_int", kp + "c_small"], [kp + "c_hm"])
            V("act", lambda e: e.activation(c_sc[0:L, 0:512], c_hm[0:L, :, :].rearrange("p h c -> p (h c)"), AF.Square), [kp + "c_hm"], [kp + "c_sc"])
            V("dve", lambda e: e.tensor_reduce(sm[0:L, 88:92], c_sc[0:L, 0:512].rearrange("p (h c) -> p h c", h=4), AX.X, ALU.add), [kp + "c_sc"], [kp + "c_small"])
            V("act", lambda e: e.activation(sm[0:L, 88:92], sm[0:L, 88:92], AF.Ln, bias=EPS, scale=1.0 / 128), [kp + "c_small"], [kp + "c_small"])
            V("act", lambda e: e.activation(sm[0:L, 88:92], sm[0:L, 88:92], AF.Exp, scale=-0.5), [kp + "c_small"], [kp + "c_small"])
            V("dve", lambda e: e.tensor_tensor(c_hm[0:L, :, :], c_hm[0:L, :, :], sm[0:L, 88:92].unsqueeze(2).broadcast_to([L, 4, 128]), ALU.mult),
              [kp + "c_hm", kp + "c_small"], [kp + "c_hm"])
            V("dve", lambda e: e.tensor_tensor(c_hm[0:L, :, :], c_hm[0:L, :, :], oTM[0:L, ci, :].rearrange("p (h c) -> p h c", h=4), ALU.mult),
              [kp + "c_hm", "oTM"], [kp + "c_hm"])
            b = nb()

            def f(e):
                for hh in range(4):
                    ins = e.transpose(PS[b][:, hh * 128:hh * 128 + L], c_hm[0:L, hh, :], ident32[0:L, 0:L])
                return ins
            V("pe", f, [kp + "c_hm", "ident32"], ["P%d" % b])
            for hh in range(4):
                V("act", lambda e, hh=hh: e.activation(mixT[:, 8 + hh, cs], PS[b][:, hh * 128:hh * 128 + L], AF.Identity, scale=pv[l][:, 76 + hh:77 + hh]),
                  ["P%d" % b], ["mixT"])
            V("dve", lambda e: e.tensor_tensor(c_kw[0:L, :, :], kTM[0:L, ci, :].rearrange("p (h c) -> p h c", h=4),
                                               sm[0:L, 80:84].unsqueeze(2).broadcast_to([L, 4, 128]), ALU.mult), ["kTM", kp + "c_small"], [kp + "c_kw"])
            bu = [nb(), nb()]
            for pr in range(2):
                def f(e, pr=pr):
                    for j in range(2):
                        hh = pr * 2 + j
                        ins = e.matmul(PS[bu[pr]][:, j * 256:j * 256 + 129], c_kw[0:L, hh, :], v4[:, hh, :], start=True, stop=True)
                    return ins
                V("pe", f, [kp + "c_kw", "vTM"], ["P%d" % bu[pr]])
            bb = bcast_rows(sm[0:4, 70:71], 4, kp + "c_small", sm, kp)
            V("dve", lambda e: e.tensor_copy(sm[:, 92:96], PS[bb][:, 0:4]), ["P%d" % bb], [kp + "c_small"])
            V("dve", lambda e: e.tensor_tensor(ST["C32"], ST["C32"], sm[:, 92:96].unsqueeze(2).broadcast_to([128, 4, 129]), ALU.mult),
              ["C32_%d" % l, kp + "c_small", "Cb"], ["C32_%d" % l])
            for pr in range(2):
                V("dve", lambda e, pr=pr: e.tensor_tensor(ST["C32"][:, pr * 2:pr * 2 + 2, :], ST["C32"][:, pr * 2:pr * 2 + 2, :],
                                                          PS[bu[pr]][:, :].rearrange("p (j c) -> p j c", j=2)[:, :, 0:129], ALU.add),
                  ["C32_%d" % l, "P%d" % bu[pr]], ["C32_%d" % l])

        def hgrn_chunk(l, tile, ci, ch):
            (c0, L, seq, first, last) = ch
            ST = stt[l]
            cs = slice(c0, c0 + L)
            par = 1
            kp = "p%d_" % par
            c = CS[par]
            R = dict(rows)
            R.update(c["rows"])
            sm = c["c_small"]
            c_sc = c["c_sc"]
            c_gcp = c["c_gcp"]
            c_e = c["c_e"]
            c_qref = c["c_qref"]
            c_qdec = c["c_qdec"]
            c_kref = c["c_kref"]
            c_kend = c["c_kend"]
            c_kendTM = c["c_kendTM"]
            c_ATb = c["c_ATb"]
            c_o = c["c_o"]
            nsb = max(1, L // 32)
            V("dve", lambda e: e.memset(c_gcp[:, :, 0:1], 0.0), (), [kp + "c_gcp"])
            for hh in range(4):
                V("dve", lambda e, hh=hh: e.tensor_tensor_scan(c_gcp[:, hh, 1:1 + L], ones32[:, 0:L], gT[:, hh, cs], 0.0, ALU.mult, ALU.add),
                  ["gT", "ones32"], [kp + "c_gcp"])
            gc = c_gcp[:, :, 1:1 + L]
            gref_full = c_gcp[:, :, 0:L:32].unsqueeze(3).broadcast_to([128, 4, nsb, 32])
            V("dve", lambda e: e.tensor_tensor(c_e[:, :, 0:L].rearrange("p h (a j) -> p h a j", a=nsb), gc.rearrange("p h (a j) -> p h a j", a=nsb),
                                               gref_full, ALU.subtract), [kp + "c_gcp"], [kp + "c_e"])
            V("act", lambda e: e.activation(c_e[:, :, 0:L], c_e[:, :, 0:L], AF.Exp), [kp + "c_e"], [kp + "c_e"])
            V("dve", lambda e: e.tensor_tensor(c_qref[:, :, 0:L], c_e[:, :, 0:L], gqT[:, :, cs], ALU.mult), [kp + "c_e", "gqT"], [kp + "c_qref"])
            V("act", lambda e: e.activation(c_e[:, :, 0:L], gc, AF.Exp), [kp + "c_gcp", kp + "c_qref"], [kp + "c_e"])
            V("dve", lambda e: e.tensor_tensor(c_qdec[:, :, 0:L], c_e[:, :, 0:L], gqT[:, :, cs], ALU.mult), [kp + "c_e", "gqT"], [kp + "c_qdec"])
            for a in range(nsb):
                na = min(L, 32 * (a + 1))
                V("dve", lambda e, a=a, na=na: e.tensor_tensor(c_e[:, :, 0:na], c_gcp[:, :, 32 * a:32 * a + 1].broadcast_to([128, 4, na]),
                                                               c_gcp[:, :, 1:1 + na], ALU.subtract), [kp + "c_gcp", kp + "c_qdec"], [kp + "c_e"])
                V("act", lambda e, na=na: e.activation(c_e[:, :, 0:na], c_e[:, :, 0:na], AF.Exp), [kp + "c_e"], [kp + "c_e"])
                V("dve", lambda e, a=a, na=na: e.tensor_tensor(c_kref[a][:, :, 0:na], c_e[:, :, 0:na], gkT[:, :, c0:c0 + na], ALU.mult),
                  [kp + "c_e", "gkT"], [kp + "kref%d" % a])
            V("dve", lambda e: e.tensor_tensor(c_e[:, :, 0:L], c_gcp[:, :, L:L + 1].broadcast_to([128, 4, L]), gc, ALU.subtract),
              [kp + "c_gcp", kp + "kref%d" % (nsb - 1)], [kp + "c_e"])
            V("act", lambda e: e.activation(c_e[:, :, 0:L], c_e[:, :, 0:L], AF.Exp), [kp + "c_e"], [kp + "c_e"])
            V("dve", lambda e: e.tensor_tensor(c_kend[:, :, 0:L], c_e[:, :, 0:L], gkT[:, :, cs], ALU.mult), [kp + "c_e", "gkT"], [kp + "c_kend"])
            V("act", lambda e: e.activation(sm[:, 100:104], c_gcp[:, :, L], AF.Exp), [kp + "c_gcp"], [kp + "c_small"])
            b = nb()

            def f(e):
                for hh in range(4):
                    for a in range(nsb):
                        ins = e.matmul(PS[b][0:L, hh * 128 + 32 * a:hh * 128 + 32 * a + 32], c_kref[a][:, hh, 0:L], c_qref[:, hh, 32 * a:32 * a + 32],
                                       start=True, stop=True)
                return ins
            V("pe", f, [kp + "kref%d" % a for a in range(nsb)] + [kp + "c_qref"], ["P%d" % b])
            V("dve", lambda e: e.tensor_tensor(c_ATb[0:L, :, 0:L], PS[b][0:L, :].rearrange("p (j c) -> p j c", j=4)[:, :, 0:L],
                                               mask01b[0:L, 0:L].unsqueeze(1).broadcast_to([L, 4, L]), ALU.mult), ["P%d" % b, "mask01b"], [kp + "c_ATb"])
            V("act", lambda e: e.copy(Sb, ST["S32"]), ["S32_%d" % l], ["Sb"])
            bo = nb()

            def f(e):
                for hh in range(4):
                    o = PS[bo][0:L, hh * 128:(hh + 1) * 128]
                    e.matmul(o, c_ATb[0:L, hh, 0:L], gviTM[0:L, ci, hh * 128:(hh + 1) * 128], start=True, stop=False)
                    ins = e.matmul(o, c_qdec[:, hh, 0:L], Sb[:, hh, :], start=False, stop=True)
                return ins
            V("pe", f, [kp + "c_ATb", "gviTM", kp + "c_qdec", "Sb"], ["P%d" % bo])
            V("act", lambda e: e.copy(c_o[0:L, :, :], PS[bo][0:L, :].rearrange("p (h c) -> p h c", h=4)), ["P%d" % bo], [kp + "c_o"])
            V("act", lambda e: e.activation(c_sc[0:L, 512:1024], c_o[0:L, :, :].rearrange("p h c -> p (h c)"), AF.Square), [kp + "c_o"], [kp + "c_sc2"])
            V("dve", lambda e: e.tensor_reduce(sm[0:L, 104:108], c_sc[0:L, 512:1024].rearrange("p (h c) -> p h c", h=4), AX.X, ALU.add), [kp + "c_sc2"], [kp + "c_small"])
            V("act", lambda e: e.activation(sm[0:L, 104:108], sm[0:L, 104:108], AF.Ln, bias=EPS, scale=1.0 / 128), [kp + "c_small"], [kp + "c_small"])
            V("act", lambda e: e.activation(sm[0:L, 104:108], sm[0:L, 104:108], AF.Exp, scale=-0.5), [kp + "c_small"], [kp + "c_small"])
            V("dve", lambda e: e.tensor_tensor(c_o[0:L, :, :], c_o[0:L, :, :], sm[0:L, 104:108].unsqueeze(2).broadcast_to([L, 4, 128]), ALU.mult),
              [kp + "c_o", kp + "c_small"], [kp + "c_o"])
            V("dve", lambda e: e.tensor_tensor(c_o[0:L, :, :], c_o[0:L, :, :], ggTM[0:L, ci, :].rearrange("p (h c) -> p h c", h=4), ALU.mult),
              [kp + "c_o", "ggTM"], [kp + "c_o"])
            b = nb()

            def f(e):
                for hh in range(4):
                    ins = e.transpose(PS[b][:, hh * 128:hh * 128 + L], c_o[0:L, hh, :], ident32[0:L, 0:L])
                return ins
            V("pe", f, [kp + "c_o", "ident32"], ["P%d" % b])
            for hh in range(4):
                V("act", lambda e, hh=hh, b=b: e.activation(mixT[:, 12 + hh, cs], PS[b][:, hh * 128:hh * 128 + L], AF.Identity, scale=pv[l][:, 84 + hh:85 + hh]),
                  ["P%d" % b], ["mixT"])
            b = nb()
            pbt = PS[b].bitcast(BF16)

            def f(e):
                for hh in range(4):
                    ins = e.transpose(pbt[0:L, hh * 128:(hh + 1) * 128], c_kend[:, hh, 0:L], identb)
                return ins
            V("pe", f, [kp + "c_kend", "identb"], ["P%d" % b])
            V("act", lambda e: e.copy(c_kendTM[0:L, :, :].rearrange("p h c -> p (h c)"), pbt[0:L, 0:512]), ["P%d" % b], [kp + "c_kendTM"])
            bu = nb()

            def f(e):
                for hh in range(4):
                    ins = e.matmul(PS[bu][:, hh * 128:(hh + 1) * 128], c_kendTM[0:L, hh, :], gviTM[0:L, ci, hh * 128:(hh + 1) * 128], start=True, stop=True)
                return ins
            V("pe", f, [kp + "c_kendTM", "gviTM"], ["P%d" % bu])
            V("dve", lambda e: e.tensor_tensor(ST["S32"], ST["S32"], sm[:, 100:104].unsqueeze(2).broadcast_to([128, 4, 128]), ALU.mult),
              ["S32_%d" % l, kp + "c_small", "Sb"], ["S32_%d" % l])
            V("dve", lambda e: e.tensor_tensor(ST["S32"], ST["S32"], PS[bu][:, :].rearrange("p (h c) -> p h c", h=4), ALU.add),
              ["S32_%d" % l, "P%d" % bu], ["S32_%d" % l])

        def state_init_prompt(l, ch, which):
            ST = stt[l]
            if which == "ssd":
                V("pool", lambda e: e.memset(ST["h32"], 0.0), (), ["h32_%d" % l])
            elif which == "ml":
                V("pool", lambda e: e.memset(ST["C32"], 0.0), (), ["C32_%d" % l])
                V("pool", lambda e: e.memset(ST["m"], 0.0), (), ["m%d" % l])
            else:
                V("pool", lambda e: e.memset(ST["S32"], 0.0), (), ["S32_%d" % l])

        def state_init_sample(l, ch, which):
            (c0, L, seq, first, last) = ch
            ST = stt[l]
            if which == "ssd":
                S.dma("sp", io_ssd, state_ssd[l, seq].rearrange("(pr h2) p n -> (h2 p) pr n", h2=2), (), ["io_ssd"])
                for hf in range(2):
                    b = nb()

                    def f(e, hf=hf, b=b):
                        for j in range(4):
                            ins = e.transpose(PS[b][:, j * 128:(j + 1) * 128], io_ssd[:, hf * 4 + j, :], ident32)
                        return ins
                    V("pe", f, ["io_ssd", "ident32"], ["P%d" % b])
                    V("dve", lambda e, hf=hf, b=b: e.tensor_copy(ST["h32"][:, hf * 512:(hf + 1) * 512], PS[b][:, :]), ["P%d" % b], ["h32_%d" % l])
            elif which == "ml":
                S.dma("sp", ST["C32"][:, :, 0:128], state_mc[l, seq].rearrange("h k v -> k h v"), (), ["C32_%d" % l])
                S.dma("sp", io_n[0:4, :], state_mn[l, seq], (), ["io_n"])
                b = nb()
                V("pe", lambda e: e.transpose(PS[b][:, 0:4], io_n[0:4, :], ident32[0:4, 0:4]), ["io_n", "ident32"], ["P%d" % b])
                V("dve", lambda e: e.tensor_copy(ST["C32"][:, :, 128], PS[b][:, 0:4]), ["P%d" % b], ["C32_%d" % l])
                S.dma("sp", ST["m"][0:4, :], state_mm[l, seq].rearrange("(a b) -> a b", b=1), (), ["m%d" % l])
            else:
                S.dma("sp", ST["S32"], state_hg[l, seq].rearrange("h k v -> k h v"), (), ["S32_%d" % l])

        def state_store(pre):
            def fn(l, ch, which):
                (c0, L, seq, first, last) = ch
                ST = stt[l]
                if which == "ssd":
                    for hf in range(2):
                        b = nb()

                        def f(e, hf=hf, b=b):
                            for j in range(4):
                                ins = e.transpose(PS[b][:, j * 128:(j + 1) * 128], ST["h32"][:, (hf * 4 + j) * 128:(hf * 4 + j + 1) * 128], ident32)
                            return ins
                        V("pe", f, ["h32_%d" % l, "ident32"], ["P%d" % b])
                        V("dve", lambda e, hf=hf, b=b: e.tensor_copy(io_ssd[:, hf * 4:(hf + 1) * 4, :], PS[b][:, :].rearrange("p (j c) -> p j c", j=4)),
                          ["P%d" % b], ["io_ssd"])
                    S.dma("sp", O[pre + "_ssd"][l, seq].rearrange("(pr h2) p n -> (h2 p) pr n", h2=2), io_ssd, ["io_ssd"], ())
                elif which == "ml":
                    S.dma("sp", O[pre + "_mc"][l, seq].rearrange("h k v -> k h v"), ST["C32"][:, :, 0:128], ["C32_%d" % l], ())
                    b = nb()
                    V("dve", lambda e: e.tensor_copy(CS[0]["c_small"][:, 124:128], ST["C32"][:, :, 128]), ["C32_%d" % l], ["c_small_n"])
                    V("pe", lambda e: e.transpose(PS[b][0:4, 0:128], CS[0]["c_small"][:, 124:128], ident32), ["c_small_n", "ident32"], ["P%d" % b])
                    V("dve", lambda e: e.tensor_copy(io_n[0:4, :], PS[b][0:4, 0:128]), ["P%d" % b], ["io_n"])
                    S.dma("sp", O[pre + "_mn"][l, seq], io_n[0:4, :], ["io_n"], ())
                    S.dma("sp", O[pre + "_mm"][l, seq].rearrange("(a b) -> a b", b=1), ST["m"][0:4, :], ["m%d" % l], ())
                else:
                    S.dma("sp", O[pre + "_hg"][l, seq].rearrange("h k v -> k h v"), ST["S32"], ["S32_%d" % l], ())
            return fn

        def conv_store(pre, l, seq, src3, key):
            for q in range(3):
                b = nb()

                def f(e, q=q, b=b):
                    for j in range(4):
                        cc = q * 4 + j
                        ins = e.transpose(PS[b][0:3, j * 128:(j + 1) * 128], src3[:, cc, :], ident32)
                    return ins
                V("pe", f, [key, "ident32"], ["P%d" % b])
                V("dve", lambda e, q=q, b=b: e.tensor_copy(io_conv[0:3, q * 512:(q + 1) * 512], PS[b][0:3, :]), ["P%d" % b], ["io_conv"])
            S.dma("sp", O[pre + "_conv"][l, seq], io_conv[0:3, :], ["io_conv"], ())

        def run_tile(tile):
            n = tile["n"]
            for ci, (c0, L, seq, first, last) in enumerate(tile["chunks"]):
                xi = xin[ci % 2]
                xk = "p%d_c_y" % (ci % 2)
                S.dma("sp", xi[0:L, :], tile["xsrc"](ci), (), [xk])
                for hf in range(2):
                    b = nb()

                    def f(e, hf=hf, b=b, xi=xi, L=L):
                        for j in range(4):
                            ins = e.transpose(PS[b][:, j * 128:j * 128 + L], xi[0:L, (hf * 4 + j) * 128:(hf * 4 + j + 1) * 128], ident32[0:L, 0:L])
                        return ins
                    V("pe", f, [xk, "ident32"], ["P%d" % b])
                    V("dve", lambda e, hf=hf, b=b, c0=c0, L=L: e.tensor_copy(xT[:, hf * 4:(hf + 1) * 4, c0:c0 + L],
                                                                             PS[b][:, :].rearrange("p (j c) -> p j c", j=4)[:, :, 0:L]),
                      ["P%d" % b], ["xT"])
            if tile["kind"] == "s":
                for l in range(2):
                    pass
            for l in range(2):
                if tile["kind"] == "s":
                    for si, (off, ln, seq) in enumerate(tile["segs"]):
                        S.dma("sp", io_conv[0:3, :], state_conv[l, seq], (), ["io_conv"])
                        b = nb()

                        def f(e, b=b):
                            for cc in range(12):
                                ins = e.transpose(PS[b][:, cc * 3:cc * 3 + 3], io_conv[0:3, cc * 128:(cc + 1) * 128], ident32[0:3, 0:3])
                            return ins
                        V("pe", f, ["io_conv", "ident32"], ["P%d" % b])
                        V("dve", lambda e, si=si, b=b: e.tensor_copy(convin[:, si, :, :], PS[b][:, 0:36].rearrange("p (c t) -> p c t", c=12)),
                          ["P%d" % b], ["convin"])
                layer(l, tile)
                if tile["kind"] == "s":
                    for si, (off, ln, seq) in enumerate(tile["segs"]):
                        conv_store("s", l, seq, convout[:, si, :, :], "convout")
                elif tile["last_tile"]:
                    conv_store("p", l, tile["seq"], stt[l]["conv"], "conv%d" % l)
            fence_all()
            V("act", lambda e: e.activation(sqs[:, 0:4, 0:n], xT[:, 0:4, 0:n], AF.Square), ["xT"], ["sqsF"])
            V("dve", lambda e: e.tensor_tensor(sqs[:, 4:8, 0:n], xT[:, 4:8, 0:n], xT[:, 4:8, 0:n], ALU.mult), ["xT"], ["sqsF_b"])
            b = nb()

            def f(e):
                for dc in range(8):
                    ins = e.matmul(PS[b][:, 0:n], onesb, sqs[:, dc, 0:n], start=(dc == 0), stop=(dc == 7))
                return ins
            V("pe", f, ["sqsF", "sqsF_b", "onesb"], ["P%d" % b])
            V("act", lambda e: e.activation(rstd[:, 0:n], PS[b][:, 0:n], AF.Ln, bias=EPS, scale=1.0 / D), ["P%d" % b], ["rstd"])
            V("act", lambda e: e.activation(rstd[:, 0:n], rstd[:, 0:n], AF.Exp, scale=-0.5), ["rstd"], ["rstd"])
            for dc in range(8):
                V("dve", lambda e, dc=dc: e.scalar_tensor_tensor(xT[:, dc, 0:n], xT[:, dc, 0:n], pvf[:, dc:dc + 1], rstd[:, 0:n], ALU.mult, ALU.mult),
                  ["xT", "rstd", "pvf"], ["xT"])
            for ci, (c0, L, seq, first, last) in enumerate(tile["chunks"]):
                xi = ostg[ci % 2]
                ok = "ostg%d" % (ci % 2)
                for hf in range(2):
                    b = nb()

                    def f(e, hf=hf, b=b, c0=c0, L=L):
                        for j in range(4):
                            ins = e.transpose(PS[b][0:L, j * 128:(j + 1) * 128], xT[:, hf * 4 + j, c0:c0 + L], ident32)
                        return ins
                    V("pe", f, ["xT", "ident32"], ["P%d" % b])
                    V("act", lambda e, hf=hf, b=b, xi=xi, L=L: e.copy(xi[0:L, hf * 512:(hf + 1) * 512], PS[b][0:L, :]), ["P%d" % b], [ok])
                S.dma("sp", tile["ydst"](ci), xi[0:L, :], [ok], ())

        tiles = []
        for p in range(NPS):
            for t0 in range(0, TP, TT):
                n = min(TT, TP - t0)
                chunks = []
                for c0 in range(0, n, 128):
                    L = min(128, n - c0)
                    chunks.append((c0, L, p, (t0 == 0 and c0 == 0), (t0 + c0 + L == TP)))
                tiles.append(dict(kind="p", seq=p, n=n, segs=[(0, n, p)], chunks=chunks, last_tile=(t0 + n == TP),
                                  first_tile=(p == 0 and t0 == 0),
                                  xsrc=(lambda ci, p=p, t0=t0, chunks=chunks: x_prompt[p, t0 + chunks[ci][0]:t0 + chunks[ci][0] + chunks[ci][1], :]),
                                  ydst=(lambda ci, p=p, t0=t0, chunks=chunks: O["p_y"][p, t0 + chunks[ci][0]:t0 + chunks[ci][0] + chunks[ci][1], :]),
                                  state_init=state_init_prompt, state_store=state_store("p")))
        if NSS > 0:
            chunks = [(i * TS, TS, i, True, True) for i in range(NSS)]
            tiles.append(dict(kind="s", seq=None, n=NSS * TS, segs=[(i * TS, TS, i) for i in range(NSS)], chunks=chunks, last_tile=True,
                              first_tile=False, convin=convin, convout=convout,
                              xsrc=(lambda ci: x_sample[ci, :, :]), ydst=(lambda ci: O["s_y"][ci, :, :]),
                              state_init=state_init_sample, state_store=state_store("s")))
        for tile in tiles:
            if tile["kind"] == "p" and tile["chunks"][0][3]:
                for l in range(2):
                    V("pool", lambda e, l=l: e.memset(stt[l]["conv"], 0.0), (), ["conv%d" % l])
            run_tile(tile)
        S.emit()
    return nc, DBG


OUT_ORDER = ["_y", "_conv", "_ssd", "_mc", "_mn", "_mm", "_hg"]
_CACHE = {}


def kernel(**inputs):
    NC = 8
    x_prompt = np.ascontiguousarray(inputs["x_prompt"], dtype=np.float32)
    x_sample = np.ascontiguousarray(inputs["x_sample"], dtype=np.float32)
    B, T, _ = x_prompt.shape
    BS = x_sample.shape[0]
    NPS, NSS = B // NC, BS // NC
    key = (NPS, T, NSS)
    if key not in _CACHE:
        _CACHE[key] = build(NPS, T, NSS)[0]
    nc = _CACHE[key]
    wnames = ["norm_mix", "w_in", "conv_w", "conv_b", "dt_bias", "a_log", "d_skip", "ssd_gain", "ml_bi", "ml_bf", "ml_gain",
              "hg_lb", "hg_gain", "w_out", "norm_ffn", "w_ffn_in", "w_ffn_out", "norm_final"]
    snames = ["state_conv", "state_ssd", "state_mlstm_c", "state_mlstm_n", "state_mlstm_m", "state_hgrn"]
    in_maps = []
    for c in range(NC):
        m = {"x_prompt": x_prompt[c * NPS:(c + 1) * NPS], "x_sample": x_sample[c * NSS:(c + 1) * NSS]}
        for s in snames:
            m[s] = np.ascontiguousarray(np.asarray(inputs[s], dtype=np.float32)[:, c * NSS:(c + 1) * NSS])
        for w in wnames:
            m[w] = np.ascontiguousarray(inputs[w], dtype=np.float32)
        in_maps.append(m)
    res = run_bass_kernel_spmd(nc, in_maps, core_ids=list(range(NC)))
    outs = []
    for pre in ("p", "s"):
        for nm in OUT_ORDER:
            ax = 0 if nm == "_y" else 1
            outs.append(np.concatenate([np.asarray(r[pre + nm]) for r in res.results], axis=ax))
    yp, pc, pssd, pmc, pmn, pmm, phg, ys, sc, sssd, smc, smn, smm, shg = outs
    return (yp, ys, pc, pssd, pmc, pmn, pmm, phg, sc, sssd, smc, smn, smm, shg)
```

```python
import contextlib
import numpy as np
import concourse.bass as bass
import concourse.mybir as mybir
from concourse.bass_utils import run_bass_kernel_spmd

F32 = mybir.dt.float32
BF16 = mybir.dt.bfloat16
AF = mybir.ActivationFunctionType
ALU = mybir.AluOpType
AX = mybir.AxisListType

D = 1024
DMIX = 2048
INC = 6680
DFF = 2816
EPS = 1e-6
NEG = -30000.0
ENGS = ("pe", "act", "dve", "pool", "sp")


class Op:
    __slots__ = ("eng", "idx", "fn", "dma", "deps", "sig", "semref")

    def __init__(self, eng, idx, fn, dma):
        self.eng = eng
        self.idx = idx
        self.fn = fn
        self.dma = dma
        self.deps = set()
        self.sig = False
        self.semref = None


class _Rec:
    def __init__(self):
        self.calls = []

    def __getattr__(self, name):
        def m(*a, **k):
            self.calls.append((name, a, k))
            return self
        return m


class Sched:
    def __init__(self, nc, n_dma_sems=20):
        self.nc = nc
        self.ops = {e: [] for e in ENGS}
        self.res = {}
        self.n_dma_sems = n_dma_sems
        self.cap = None

    def add(self, eng, fn, reads=(), writes=(), dma=False):
        rec = _Rec()
        fn(rec)
        if self.cap is not None:
            self.cap.append((eng, rec.calls, tuple(reads), tuple(writes), dma))
            return None
        return self.commit(eng, rec.calls, reads, writes, dma)

    def capture(self, f, *a):
        assert self.cap is None
        self.cap = []
        f(*a)
        out = self.cap
        self.cap = None
        return out

    def commit(self, eng, calls, reads, writes, dma):
        op = Op(eng, len(self.ops[eng]), calls, dma)
        deps = set()
        for r in reads:
            st = self.res.setdefault(r, [None, []])
            if st[0] is not None:
                deps.add(st[0])
            st[1].append(op)
        for w in writes:
            st = self.res.setdefault(w, [None, []])
            if st[0] is not None:
                deps.add(st[0])
            for rd in st[1]:
                deps.add(rd)
            st[0] = op
            st[1] = []
        deps.discard(op)
        op.deps = deps
        for d in deps:
            d.sig = True
        self.ops[eng].append(op)
        return op

    def dma(self, eng, out, in_, reads=(), writes=()):
        return self.add(eng, lambda e: e.dma_start(out=out, in_=in_), reads, writes, dma=True)

    def emit(self):
        nc = self.nc
        with contextlib.ExitStack() as st:
            csem = {e: st.enter_context(nc.semaphore("c_" + e)) for e in ENGS}
            dsem = {e: [st.enter_context(nc.semaphore("d_%s%d" % (e, i))) for i in range(self.n_dma_sems)]
                    for e in ("sp", "act", "pool")}
            for e in ENGS:
                cnt = 0
                dcnt = [0] * self.n_dma_sems
                k = 0
                for op in self.ops[e]:
                    if op.dma:
                        j = k % self.n_dma_sems
                        k += 1
                        prev = dcnt[j]
                        dcnt[j] += 16
                        op.semref = (dsem[e][j], dcnt[j], prev)
                    elif op.sig:
                        cnt += 1
                        op.semref = (csem[e], cnt, None)
            block = st.enter_context(nc.Block())

            def run(e, eng):
                waited = {}
                for op in self.ops[e]:
                    need = {}
                    for d in op.deps:
                        if d.semref is None:
                            continue
                        sem, val = d.semref[0], d.semref[1]
                        key = id(sem)
                        if need.get(key, (None, 0))[1] < val:
                            need[key] = (sem, val)
                    if op.dma:
                        sem, val, prev = op.semref
                        if prev > 0:
                            key = id(sem)
                            if need.get(key, (None, 0))[1] < prev:
                                need[key] = (sem, prev)
                    for key, (sem, val) in need.items():
                        if waited.get(key, 0) < val:
                            eng.wait_ge(sem, val)
                            waited[key] = val
                    ins = None
                    for name, a, k in op.fn:
                        ins = getattr(eng, name)(*a, **k)
                    if op.dma:
                        ins.then_inc(op.semref[0], 16)
                    elif op.sig:
                        ins.then_inc(op.semref[0], 1)
                if e == "sp":
                    for e2 in ("sp", "act", "pool"):
                        last = {}
                        for op in self.ops[e2]:
                            if op.dma:
                                last[id(op.semref[0])] = (op.semref[0], op.semref[1])
                        for sem, val in last.values():
                            eng.wait_ge(sem, val)

            block.tensor(lambda eng: run("pe", eng))
            block.scalar(lambda eng: run("act", eng))
            block.vector(lambda eng: run("dve", eng))
            block.gpsimd(lambda eng: run("pool", eng))
            block.sync(lambda eng: run("sp", eng))


class Arena:
    def __init__(self, nc, st, name, nbytes):
        self.words = nbytes // 4
        self.t = st.enter_context(nc.sbuf_tensor(name, [128, self.words], F32))
        self.off = 0
        self.name = name

    def alloc(self, shape, dt):
        n = int(np.prod(shape))
        words = n if dt == F32 else (n + 1) // 2
        assert self.off + words <= self.words, (self.name, self.off, words, self.words)
        v = self.t[:, self.off:self.off + words]
        self.off += words
        if dt == BF16:
            v = v.bitcast(BF16)[:, 0:n]
        if len(shape) == 2:
            v = v.rearrange("p (a b) -> p a b", a=shape[0])
        elif len(shape) == 3:
            v = v.rearrange("p (a b c) -> p a b c", a=shape[0], b=shape[1])
        return v


def build(NPS, TP, NSS, dbg=False, LAG=8):
    TS = 32
    TT = min(512, TP)
    nc = bass.Bass("TRN2", target_bir_lowering=False)
    din = lambda n, s: nc.dram_tensor(n, list(s), F32, kind="ExternalInput").ap()
    dout = lambda n, s: nc.dram_tensor(n, list(s), F32, kind="ExternalOutput").ap()
    x_prompt = din("x_prompt", [NPS, TP, D])
    x_sample = din("x_sample", [NSS, TS, D])
    state_conv = din("state_conv", [2, NSS, 3, 1536])
    state_ssd = din("state_ssd", [2, NSS, 16, 64, 128])
    state_mc = din("state_mlstm_c", [2, NSS, 4, 128, 128])
    state_mn = din("state_mlstm_n", [2, NSS, 4, 128])
    state_mm = din("state_mlstm_m", [2, NSS, 4])
    state_hg = din("state_hgrn", [2, NSS, 4, 128, 128])
    norm_mix = din("norm_mix", [2, D])
    w_in = din("w_in", [2, D, INC])
    conv_w = din("conv_w", [2, 4, 1536])
    conv_b = din("conv_b", [2, 1536])
    dt_bias = din("dt_bias", [2, 16])
    a_log = din("a_log", [2, 16])
    d_skip = din("d_skip", [2, 16])
    ssd_gain = din("ssd_gain", [2, 1024])
    ml_bi = din("ml_bi", [2, 4])
    ml_bf = din("ml_bf", [2, 4])
    ml_gain = din("ml_gain", [2, 512])
    hg_lb = din("hg_lb", [2, 512])
    hg_gain = din("hg_gain", [2, 512])
    w_out = din("w_out", [2, DMIX, D])
    norm_ffn = din("norm_ffn", [2, D])
    w_ffn_in = din("w_ffn_in", [2, D, 2 * DFF])
    w_ffn_out = din("w_ffn_out", [2, DFF, D])
    norm_final = din("norm_final", [D])

    w_in_b = nc.dram_tensor("w_in_b", [2, D, INC], BF16, kind="Internal").ap()
    w_out_b = nc.dram_tensor("w_out_b", [2, DMIX, D], BF16, kind="Internal").ap()
    w_ffn_in_b = nc.dram_tensor("w_ffn_in_b", [2, D, 2 * DFF], BF16, kind="Internal").ap()
    w_ffn_out_b = nc.dram_tensor("w_ffn_out_b", [2, DFF, D], BF16, kind="Internal").ap()
    O = {}
    for pre, nb_ in (("p", NPS), ("s", NSS)):
        O[pre + "_y"] = dout(pre + "_y", [nb_, TP if pre == "p" else TS, D])
        O[pre + "_conv"] = dout(pre + "_conv", [2, nb_, 3, 1536])
        O[pre + "_ssd"] = dout(pre + "_ssd", [2, nb_, 16, 64, 128])
        O[pre + "_mc"] = dout(pre + "_mc", [2, nb_, 4, 128, 128])
        O[pre + "_mn"] = dout(pre + "_mn", [2, nb_, 4, 128])
        O[pre + "_mm"] = dout(pre + "_mm", [2, nb_, 4])
        O[pre + "_hg"] = dout(pre + "_hg", [2, nb_, 4, 128, 128])
    DBG = {}

    st = contextlib.ExitStack()
    with st:
        S = Sched(nc)
        PS = [st.enter_context(nc.psum_tensor("ps%d" % i, [128, 512], F32)) for i in range(8)]
        psk = [0, 0, 0]
        pspool = [None]

        def nb():
            p = pspool[0]
            if p is None:
                i = psk[2] % 8
                psk[2] += 1
                return i
            i = 4 * p + (psk[p] % 4)
            psk[p] += 1
            return i

        A = Arena(nc, st, "arenaA", 106 * 1024)
        RB = Arena(nc, st, "arenaB", 101 * 1024)

        ident32 = A.alloc((128,), F32)
        identb = A.alloc((128,), BF16)
        onesb = A.alloc((128,), BF16)
        mask01b = A.alloc((128,), BF16)
        negmask4 = A.alloc((4, 128), BF16)
        ones32 = A.alloc((128,), F32)
        V = lambda eng, f, r=(), w=(): S.add(eng, f, r, w)
        V("pool", lambda e: e.memset(ident32, 0.0), (), ["ident32"])
        V("pool", lambda e: e.affine_select(out=ident32, in_=ident32, pattern=[[-1, 128]], compare_op=ALU.not_equal,
                                            fill=1.0, base=0, channel_multiplier=1), ["ident32"], ["ident32"])
        V("dve", lambda e: e.tensor_copy(identb, ident32), ["ident32"], ["identb"])
        V("dve", lambda e: e.memset(onesb, 1.0), (), ["onesb"])
        V("dve", lambda e: e.memset(ones32, 1.0), (), ["ones32"])
        V("pool", lambda e: e.memset(ones32, 1.0), ["ones32"], ["ones32"])
        m01f = A.alloc((128,), F32)
        V("pool", lambda e: e.memset(m01f, 1.0), (), ["m01f"])
        V("pool", lambda e: e.affine_select(out=m01f, in_=m01f, pattern=[[1, 128]], compare_op=ALU.is_ge,
                                            fill=0.0, base=0, channel_multiplier=-1), ["m01f"], ["m01f"])
        V("dve", lambda e: e.tensor_copy(mask01b, m01f), ["m01f"], ["mask01b"])
        for j in range(4):
            V("dve", lambda e, j=j: e.tensor_scalar(negmask4[:, j, :], m01f, 1.0, -NEG, ALU.subtract, ALU.mult),
              ["m01f"], ["negmask4"])
        def sel(nr, hh, L):
            return ident32[0:nr, hh:hh + 1].broadcast_to([nr, L])

        xT = A.alloc((8, TT), F32)
        hT = A.alloc((8, TT), BF16)
        mixT = A.alloc((16, TT), BF16)
        rstd = A.alloc((TT,), F32)
        NSLOT = 2
        wring = [A.alloc((8192,), BF16) for _ in range(NSLOT)]
        wk = [0]
        pv = [A.alloc((96,), F32) for _ in range(2)]
        pvf = A.alloc((8,), F32)
        rowbuf = A.alloc((128,), F32)
        hs = {}
        for l in range(2):
            hs[l] = dict(dtb=A.alloc((1,), F32), A=A.alloc((1,), F32), bi=A.alloc((1,), F32), nbf=A.alloc((1,), F32),
                         Dbc=A.alloc((16,), F32), lb=A.alloc((4,), F32), oml=A.alloc((4,), F32))
        stt = {}
        for l in range(2):
            stt[l] = dict(h32=A.alloc((1024,), F32), C32=A.alloc((4, 129), F32), S32=A.alloc((4, 128), F32),
                          m=A.alloc((1,), F32), conv=A.alloc((12, 3), F32))
        hTb = A.alloc((1024,), BF16)
        Cb = A.alloc((4, 129), BF16)
        Sb = A.alloc((4, 128), BF16)
        fence = A.alloc((4,), F32)
        w32a = A.alloc((1024,), F32)
        h0buf = A.alloc((8,), F32)
        s0buf = A.alloc((4,), F32)

        def load_rows(l):
            segs = [(norm_mix[l], 0, 8), (conv_w[l].rearrange("w c -> (w c)"), 8, 48), (conv_b[l], 56, 12), (ssd_gain[l], 68, 8),
                    (ml_gain[l], 76, 4), (hg_lb[l], 80, 4), (hg_gain[l], 84, 4), (norm_ffn[l], 88, 8)]
            for src, r0, n in segs:
                S.dma("sp", rowbuf[r0:r0 + n, :], src.rearrange("(a b) -> a b", b=128), (), ["rowbuf"])
            b = nb()
            V("pe", lambda e: e.transpose(PS[b][:, 0:96], rowbuf[0:96, :], ident32[0:96, 0:96]), ["rowbuf", "ident32"], ["P%d" % b])
            V("dve", lambda e: e.tensor_copy(pv[l], PS[b][:, 0:96]), ["P%d" % b], ["pv%d" % l])

        for l in range(2):
            load_rows(l)
            h = hs[l]
            S.dma("sp", h["dtb"][0:16, :], dt_bias[l].rearrange("(a b) -> a b", b=1), (), ["hs%d" % l])
            S.dma("sp", h["A"][0:16, :], a_log[l].rearrange("(a b) -> a b", b=1), (), ["hs%d" % l])
            S.dma("sp", h["bi"][0:4, :], ml_bi[l].rearrange("(a b) -> a b", b=1), (), ["hs%d" % l])
            S.dma("sp", h["nbf"][0:4, :], ml_bf[l].rearrange("(a b) -> a b", b=1), (), ["hs%d" % l])
            S.dma("sp", h["Dbc"], d_skip[l].rearrange("(a b) -> a b", a=1).broadcast_to([128, 16]), (), ["hs%d" % l])
            V("act", lambda e, h=h: e.activation(h["A"][0:16, :], h["A"][0:16, :], AF.Exp), ["hs%d" % l], ["hs%d" % l])
            V("dve", lambda e, h=h: e.tensor_scalar(h["A"][0:16, :], h["A"][0:16, :], -1.0, None, ALU.mult), ["hs%d" % l], ["hs%d" % l])
            V("dve", lambda e, h=h: e.tensor_scalar(h["nbf"][0:4, :], h["nbf"][0:4, :], -1.0, None, ALU.mult), ["hs%d" % l], ["hs%d" % l])
        S.dma("sp", rowbuf[0:8, :], norm_final.rearrange("(a b) -> a b", b=128), (), ["rowbuf"])
        b = nb()
        V("pe", lambda e: e.transpose(PS[b][:, 0:8], rowbuf[0:8, :], ident32[0:8, 0:8]), ["rowbuf", "ident32"], ["P%d" % b])
        V("dve", lambda e: e.tensor_copy(pvf, PS[b][:, 0:8]), ["P%d" % b], ["pvf"])
        V("dve", lambda e: e.memset(hs[0]["lb"], 0.0), (), ["hs0"])
        V("dve", lambda e: e.memset(hs[0]["oml"], 1.0), (), ["hs0"])
        V("dve", lambda e: e.tensor_tensor(hs[1]["lb"], pv[1][:, 80:84], pv[0][:, 80:84], ALU.subtract), ["pv0", "pv1"], ["hs1"])
        V("act", lambda e: e.activation(hs[1]["lb"], hs[1]["lb"], AF.Sigmoid), ["hs1"], ["hs1"])
        V("dve", lambda e: e.tensor_scalar(hs[1]["oml"], hs[1]["lb"], -1.0, 1.0, ALU.mult, ALU.add), ["hs1"], ["hs1"])

        converted = set()

        def wload(src3, src3f, a, c, wbkey):
            i = wk[0] % NSLOT
            wk[0] += 1
            v = wring[i][:, 0:a * c].rearrange("p (a c) -> p a c", a=a)
            if wbkey not in converted:
                converted.add(wbkey)
                S.dma("pool", v, src3f, (), ["w%d" % i])
                S.dma("sp", src3, v, ["w%d" % i], [wbkey])
            else:
                S.dma("sp", v, src3, [wbkey], ["w%d" % i])
            return v, "w%d" % i

        NCH = 4
        xp = RB.alloc((TT + 16,), F32)
        cacc = RB.alloc((TT,), F32)
        io_ssd = RB.alloc((8, 128), F32)
        io_conv = RB.alloc((1536,), F32)
        io_n = RB.alloc((128,), F32)
        convin = RB.alloc((4, 12, 3), F32)
        convout = RB.alloc((4, 12, 3), F32)
        CS = [dict(), dict()]
        c_sc1 = RB.alloc((1024,), F32)
        q0row = RB.alloc((512,), F32)
        for par in range(2):
            CS[par]["c_small"] = RB.alloc((256,), F32)
            CS[par]["c_sc"] = c_sc1
        rb0 = RB.off
        xbcT = RB.alloc((12, TT), BF16)
        zs = RB.alloc((NCH, 1024), BF16)
        rows = {}
        rows["dt"] = RB.alloc((TT,), F32)
        rows["a"] = RB.alloc((TT,), F32)
        for par in range(2):
            c = CS[par]
            c["rows"] = {}
            for k in ("acum", "nacum", "decend"):
                c["rows"][k] = RB.alloc((128,), F32)
            c["c_xsTM"] = RB.alloc((1024,), BF16)
            c["c_xdt"] = RB.alloc((1024,), BF16)
            c["c_xdec"] = RB.alloc((1024,), BF16)
            c["c_BTM"] = RB.alloc((256,), BF16)
            c["c_cbT"] = RB.alloc((2, 128), F32)
            c["c_dT"] = RB.alloc((16, 128), BF16)
            c["c_MT"] = RB.alloc((16, 128), BF16)
            c["c_t2"] = RB.alloc((1024,), BF16)
            c["c_y"] = RB.alloc((1024,), F32)
        rb_max = RB.off
        RB.off = rb0
        qT = RB.alloc((4, TT), BF16)
        kT = RB.alloc((4, TT), BF16)
        kTM = RB.alloc((NCH, 512), BF16)
        vTM = RB.alloc((NCH, 4 * 129), BF16)
        oTM = RB.alloc((NCH, 512), BF16)
        rows["ig"] = RB.alloc((TT,), F32)
        rows["lf"] = RB.alloc((TT,), F32)
        c = CS[0]
        for k in ("fcum", "u", "cmm", "ncmm", "winter", "emi", "wkr", "wk2"):
            c["rows"][k] = RB.alloc((128,), F32)
        c["c_wT"] = RB.alloc((4, 128), F32)
        c["c_SwT"] = RB.alloc((4, 128), BF16)
        c["c_kw"] = RB.alloc((4, 128), BF16)
        c["c_int"] = RB.alloc((4, 129), F32)
        c["c_hm"] = RB.alloc((4, 128), F32)
        gqT = RB.alloc((4, TT), BF16)
        gkT = RB.alloc((4, TT), BF16)
        gT = RB.alloc((4, TT), F32)
        gviTM = RB.alloc((NCH, 512), BF16)
        ggTM = RB.alloc((NCH, 512), BF16)
        c = CS[1]
        c["c_gcp"] = RB.alloc((4, 129), F32)
        c["c_e"] = RB.alloc((4, 128), F32)
        c["c_qref"] = RB.alloc((4, 128), BF16)
        c["c_qdec"] = RB.alloc((4, 128), BF16)
        c["c_kref"] = [RB.alloc((4, 128), BF16) for _ in range(4)]
        c["c_kend"] = RB.alloc((4, 128), BF16)
        c["c_kendTM"] = RB.alloc((4, 128), BF16)
        c["c_ATb"] = RB.alloc((4, 128), BF16)
        c["c_o"] = RB.alloc((4, 128), F32)
        rb_max = max(rb_max, RB.off)
        RB.off = rb0
        ff0 = RB.off
        actT = RB.alloc((22, TT), BF16)
        sqs = RB.alloc((8, TT), BF16)
        ostg = [RB.t[:, ff0:ff0 + 1024], RB.t[:, ff0 + 1024:ff0 + 2048]]
        xin = [CS[0]["c_y"], CS[1]["c_y"]]
        rb_max = max(rb_max, RB.off)
        RB.off = rb_max
        CK = ["c_xsTM", "c_xdt", "c_xdec", "c_BTM", "c_cbT", "c_dT", "c_MT", "c_t2", "c_y", "rows",
              "c_wT", "c_SwT", "c_kw", "c_int", "c_hm", "c_gcp", "c_e", "c_qref", "c_qdec", "kref0", "kref1", "kref2", "kref3",
              "c_kend", "c_kendTM", "c_ATb", "c_o"]
        ALLK = ["xbcT", "zs", "rowsT", "qT", "kT", "kTM", "vTM", "oTM", "gqT", "gkT", "gT", "gviTM", "ggTM", "actT", "sqsF", "ostg0", "ostg1", "xbcT_b", "sqsF_b"] + \
               ["p%d_%s" % (par, k) for par in range(2) for k in CK]

        def fence_all():
            V("dve", lambda e: e.memset(fence, 0.0), (), ALLK + ["fence"])

        def dbg_out(name, ap_sb, shape, reads):
            if not dbg:
                return
            d = dout("dbg_" + name, shape)
            DBG[name] = shape
            S.dma("pool", d, ap_sb, reads, ())

        def rmsnorm_fm(n, gain_cols, dst, sq_buf, sqkey):
            V("act", lambda e: e.activation(sq_buf[:, 0:4, 0:n], xT[:, 0:4, 0:n], AF.Square), ["xT"], [sqkey])
            V("dve", lambda e: e.tensor_tensor(sq_buf[:, 4:8, 0:n], xT[:, 4:8, 0:n], xT[:, 4:8, 0:n], ALU.mult), ["xT"], [sqkey + "_b"])
            b = nb()

            def f(e):
                for dc in range(8):
                    ins = e.matmul(PS[b][:, 0:n], onesb, sq_buf[:, dc, 0:n], start=(dc == 0), stop=(dc == 7))
                return ins
            V("pe", f, [sqkey, sqkey + "_b", "onesb"], ["P%d" % b])
            V("act", lambda e: e.activation(rstd[:, 0:n], PS[b][:, 0:n], AF.Ln, bias=EPS, scale=1.0 / D), ["P%d" % b], ["rstd"])
            V("act", lambda e: e.activation(rstd[:, 0:n], rstd[:, 0:n], AF.Exp, scale=-0.5), ["rstd"], ["rstd"])
            for dc in range(8):
                V("dve", lambda e, dc=dc: e.scalar_tensor_tensor(dst[:, dc, 0:n], xT[:, dc, 0:n], gain_cols[:, dc:dc + 1], rstd[:, 0:n],
                                                                  ALU.mult, ALU.mult), ["xT", "rstd"], ["hT"])

        def fm_proj(wv, wkey, c0, M, n, src, srckey, ndc):
            b = nb()

            def f(e):
                for dc in range(ndc):
                    ins = e.matmul(PS[b][0:M, 0:n], wv[:, dc, c0:c0 + M], src[:, dc, 0:n], start=(dc == 0), stop=(dc == ndc - 1))
                return ins
            V("pe", f, [wkey, srckey], ["P%d" % b])
            return b

        def tm_proj(wv, wkey, c0, ncol, t0, L):
            b = nb()

            def f(e):
                for dc in range(8):
                    ins = e.matmul(PS[b][0:L, 0:ncol], hT[:, dc, t0:t0 + L], wv[:, dc, c0:c0 + ncol], start=(dc == 0), stop=(dc == 7))
                return ins
            V("pe", f, [wkey, "hT"], ["P%d" % b])
            return b

        def bcast_rows(src_rows_ap, nrow, key_r, c_small, kp):
            b = nb()
            V("dve", lambda e: e.tensor_scalar(c_small[0:nrow, 108:108 + nrow], ident32[0:nrow, 0:nrow], src_rows_ap, None, ALU.mult),
              [key_r, "ident32"], [kp + "c_small_d"])
            V("pe", lambda e: e.matmul(PS[b][:, 0:nrow], ones32[0:nrow, :], c_small[0:nrow, 108:108 + nrow], start=True, stop=True),
              [kp + "c_small_d", "ones32"], ["P%d" % b])
            return b

        def one_chunk(l, tile, ci, ch, which, fn):
            pspool[0] = {"ssd": ci % 2, "ml": 0, "hg": 1}[which]
            try:
                one_chunk_(l, tile, ci, ch, which, fn)
            finally:
                pspool[0] = None

        def one_chunk_(l, tile, ci, ch, which, fn):
            if ch[3]:
                tile["state_init"](l, ch, which)
            fn(l, tile, ci, ch)
            if ch[4]:
                tile["state_store"](l, ch, which)

        def pe_segs(sl):
            return [[spec] for spec in sl]

        def interleave2(la, lb):
            A_, B_ = pe_segs(la), pe_segs(lb)
            i = j = 0
            while i < len(A_) or j < len(B_):
                if i < len(A_):
                    for spec in A_[i]:
                        S.commit(*spec)
                    i += 1
                want = len(B_) if i >= len(A_) else (i * len(B_)) // max(1, len(A_))
                while j < want:
                    for spec in B_[j]:
                        S.commit(*spec)
                    j += 1

        def pipelined(lists, l):
            skeys = set(["h32_%d" % l, "hTb", "C32_%d" % l, "Cb", "m%d" % l, "S32_%d" % l, "Sb", "io_ssd", "io_n", "io_conv"])

            def segs(sl):
                return [[spec] for spec in sl]
            parts = []
            for sl in lists:
                idx = len(sl)
                for i, spec in enumerate(sl):
                    if skeys.intersection(spec[2]) or skeys.intersection(spec[3]):
                        idx = i
                        break
                if LAG <= 0:
                    idx = 0
                parts.append((segs(sl[:idx]), segs(sl[idx:])))
            for seg in parts[0][0]:
                for spec in seg:
                    S.commit(*spec)
            for c in range(len(parts)):
                Bc = parts[c][1]
                An = parts[c + 1][0] if c + 1 < len(parts) else []
                i = j = 0
                while i < len(Bc) or j < len(An):
                    if i < len(Bc):
                        for spec in Bc[i]:
                            S.commit(*spec)
                        i += 1
                    want = len(An) if i >= len(Bc) else (i * len(An)) // max(1, len(Bc))
                    while j < want:
                        for spec in An[j]:
                            S.commit(*spec)
                        j += 1

        def ml_s0_prep(l):
            V("dve", lambda e: e.tensor_tensor(h0buf, xT[:, :, 0], pv[l][:, 0:8], ALU.mult), ["xT", "pv%d" % l], ["h0buf"])
            V("dve", lambda e: e.tensor_scalar(h0buf, h0buf, rstd[:, 0:1], None, ALU.mult), ["h0buf", "rstd"], ["h0buf"])

        def ml_s0_exact(l):
            bq, bk = nb(), nb()
            wbufs = [(w32a, ["w32a"]), (c_sc1, ["p0_c_sc", "p1_c_sc2"])]
            for dc in range(8):
                wb_, wk_ = wbufs[dc % 2]
                S.dma("sp", wb_, w_in[l, dc * 128:(dc + 1) * 128, 2576:3600], (), wk_)

                def f(e, dc=dc, wb_=wb_):
                    e.matmul(PS[bq][0:1, 0:512], h0buf[:, dc:dc + 1], wb_[:, 0:512], start=(dc == 0), stop=(dc == 7))
                    return e.matmul(PS[bk][0:1, 0:512], h0buf[:, dc:dc + 1], wb_[:, 512:1024], start=(dc == 0), stop=(dc == 7))
                V("pe", f, wk_ + ["h0buf"], ["P%d" % bq, "P%d" % bk])
            V("act", lambda e: e.copy(q0row[0:1, :], PS[bq][0:1, 0:512]), ["P%d" % bq], ["q0row"])
            V("dve", lambda e: e.tensor_tensor(q0row[0:1, :], q0row[0:1, :], PS[bk][0:1, 0:512], ALU.mult), ["q0row", "P%d" % bk], ["q0row"])
            V("dve", lambda e: e.tensor_reduce(s0buf[0:1, :], q0row[0:1, :].rearrange("p (h c) -> p h c", h=4), AX.X, ALU.add), ["q0row"], ["s0buf"])
            V("dve", lambda e: e.tensor_scalar(s0buf[0:1, :], s0buf[0:1, :], 128.0 ** -0.5, None, ALU.mult), ["s0buf"], ["s0buf"])

        def layer(l, tile):
            n = tile["n"]
            segs = tile["segs"]
            chunks = tile["chunks"]
            nseg = len(segs)
            ls = segs[0][1]
            P = pv[l]
            H = hs[l]
            ST = stt[l]
            w_in_v = w_in_b[l].rearrange("(dc p) c -> p dc c", p=128)
            w_in_f = w_in[l].rearrange("(dc p) c -> p dc c", p=128)
            fence_all()
            rmsnorm_fm(n, P[:, 0:8], hT, xbcT[:, 0:8, :], "xbcT")
            seq_start = (tile["kind"] == "p" and chunks[0][3])
            if seq_start:
                ml_s0_prep(l)
            wvz, wkeyz = wload(w_in_v[:, :, 0:1024], w_in_f[:, :, 0:1024], 8, 1024, "wb_in%d_0" % l)
            zjobs = [(ci, ch, hf) for ci, ch in enumerate(chunks) for hf in range(2)]

            def zjob(ci, ch, hf):
                (c0, L, seq, first, last) = ch
                b = tm_proj(wvz, wkeyz, hf * 512, 512, c0, L)
                V("act", lambda e: e.activation(zs[0:L, ci, hf * 512:(hf + 1) * 512], PS[b][0:L, :], AF.Silu), ["P%d" % b], ["zs"])

            def conv_chunk(b, cc):
                if cc % 2 == 0:
                    xpb, caccb, kx, kc = xp, cacc, "xp", "cacc"
                else:
                    xpb, caccb, kx, kc = CS[0]["c_y"], CS[1]["c_y"], "p0_c_y", "p1_c_y"
                xp3 = xpb[:, 0:nseg * (ls + 3)].rearrange("p (s t) -> p s t", s=nseg)
                if tile["kind"] == "p":
                    V("dve", lambda e: e.tensor_copy(xp3[:, 0, 0:3], ST["conv"][:, cc, :]), ["conv%d" % l], [kx])
                else:
                    V("dve", lambda e: e.tensor_copy(xp3[:, :, 0:3], convin[:, 0:nseg, cc, :]), ["convin"], [kx])
                V("act", lambda e: e.copy(xp3[:, :, 3:3 + ls], PS[b][:, 0:n].rearrange("p (s t) -> p s t", s=nseg)), ["P%d" % b], [kx])
                if tile["kind"] == "p":
                    V("dve", lambda e: e.tensor_copy(ST["conv"][:, cc, :], xp3[:, 0, ls:ls + 3]), [kx], ["conv%d" % l])
                else:
                    V("dve", lambda e: e.tensor_copy(convout[:, 0:nseg, cc, :], xp3[:, :, ls:ls + 3]), [kx], ["convout"])
                acc3 = caccb[:, 0:n].rearrange("p (s t) -> p s t", s=nseg)
                V("act", lambda e: e.activation(acc3, xp3[:, :, 0:ls], AF.Identity, scale=P[:, 8 + cc:9 + cc]), [kx], [kc])
                for w in range(1, 4):
                    V("dve", lambda e, w=w: e.scalar_tensor_tensor(acc3, xp3[:, :, w:w + ls], P[:, 8 + w * 12 + cc:9 + w * 12 + cc], acc3,
                                                                    ALU.mult, ALU.add), [kx, kc], [kc])
                V("act", lambda e: e.activation(xbcT[:, cc, 0:n], caccb[:, 0:n], AF.Silu, bias=P[:, 56 + cc:57 + cc]), [kc], ["xbcT"])

            wv, wkey = wload(w_in_v[:, :, 1024:2048], w_in_f[:, :, 1024:2048], 8, 1024, "wb_in%d_1024" % l)
            for cc in range(8):
                b = fm_proj(wv, wkey, cc * 128, 128, n, hT, "hT", 8)
                conv_chunk(b, cc)
                if zjobs:
                    zjob(*zjobs.pop(0))
            while zjobs:
                zjob(*zjobs.pop(0))
            wv, wkey = wload(w_in_v[:, :, 2048:2576], w_in_f[:, :, 2048:2576], 8, 528, "wb_in%d_2048" % l)
            for cc in range(8, 12):
                b = fm_proj(wv, wkey, (cc - 8) * 128, 128, n, hT, "hT", 8)
                conv_chunk(b, cc)
            b = fm_proj(wv, wkey, 512, 16, n, hT, "hT", 8)
            R = rows
            V("act", lambda e: e.activation(R["dt"][0:16, 0:n], PS[b][0:16, 0:n], AF.Exp, bias=H["dtb"][0:16, :]), ["P%d" % b, "hs%d" % l], ["rowsT"])
            V("act", lambda e: e.activation(R["dt"][0:16, 0:n], R["dt"][0:16, 0:n], AF.Ln, bias=1.0), ["rowsT"], ["rowsT"])
            V("dve", lambda e: e.tensor_scalar(R["a"][0:16, 0:n], R["dt"][0:16, 0:n], H["A"][0:16, :], None, ALU.mult), ["rowsT", "hs%d" % l], ["rowsT"])
            pipelined([S.capture(one_chunk, l, tile, ci, ch, "ssd", ssd_chunk) for ci, ch in enumerate(chunks)], l)
            if dbg and l == 0 and tile.get("first_tile"):
                dbg_out("xbcT", xbcT[:, :, 0:n], [128, 12, n], ["xbcT"])
                dbg_out("hT", hT[:, :, 0:n], [128, 8, n], ["hT"])
                dbg_out("dtrow", R["dt"][0:16, 0:n], [16, n], ["rowsT"])

            fence_all()
            if seq_start:
                ml_s0_exact(l)
            wv, wkey = wload(w_in_v[:, :, 2576:3600], w_in_f[:, :, 2576:3600], 8, 1024, "wb_in%d_2576" % l)
            for hh in range(4):
                b = fm_proj(wv, wkey, hh * 128, 128, n, hT, "hT", 8)
                V("act", lambda e, b=b, hh=hh: e.copy(qT[:, hh, 0:n], PS[b][:, 0:n]), ["P%d" % b], ["qT"])
                b = fm_proj(wv, wkey, 512 + hh * 128, 128, n, hT, "hT", 8)
                V("act", lambda e, b=b, hh=hh: e.activation(kT[:, hh, 0:n], PS[b][:, 0:n], AF.Identity, scale=128.0 ** -0.5), ["P%d" % b], ["kT"])
            for ci, (c0, L, seq, first, last) in enumerate(chunks):
                b = tm_proj(wv, wkey, 512, 512, c0, L)
                V("act", lambda e, b=b, ci=ci, L=L: e.activation(kTM[0:L, ci, :], PS[b][0:L, :], AF.Identity, scale=128.0 ** -0.5), ["P%d" % b], ["kTM"])
            wv, wkey = wload(w_in_v[:, :, 3600:4120], w_in_f[:, :, 3600:4120], 8, 520, "wb_in%d_3600" % l)
            for ci, (c0, L, seq, first, last) in enumerate(chunks):
                b = tm_proj(wv, wkey, 0, 512, c0, L)
                v4 = vTM[0:L, ci, :].rearrange("p (h c) -> p h c", h=4)
                V("act", lambda e, b=b, v4=v4, L=L: e.copy(v4[:, :, 0:128], PS[b][0:L, :].rearrange("p (h c) -> p h c", h=4)), ["P%d" % b], ["vTM"])
                V("dve", lambda e, v4=v4: e.memset(v4[:, :, 128:129], 1.0), (), ["vTM"])
            b = fm_proj(wv, wkey, 512, 4, n, hT, "hT", 8)
            V("act", lambda e, b=b: e.activation(R["ig"][0:4, 0:n], PS[b][0:4, 0:n], AF.Identity, bias=H["bi"][0:4, :]), ["P%d" % b, "hs%d" % l], ["rowsT"])
            b = fm_proj(wv, wkey, 516, 4, n, hT, "hT", 8)
            V("act", lambda e, b=b: e.activation(R["lf"][0:4, 0:n], PS[b][0:4, 0:n], AF.Exp, bias=H["nbf"][0:4, :], scale=-1.0), ["P%d" % b, "hs%d" % l], ["rowsT"])
            V("act", lambda e: e.activation(R["lf"][0:4, 0:n], R["lf"][0:4, 0:n], AF.Ln, bias=1.0), ["rowsT"], ["rowsT"])
            V("dve", lambda e: e.tensor_scalar(R["lf"][0:4, 0:n], R["lf"][0:4, 0:n], -1.0, None, ALU.mult), ["rowsT"], ["rowsT"])
            wv, wkey = wload(w_in_v[:, :, 4120:4632], w_in_f[:, :, 4120:4632], 8, 512, "wb_in%d_4120" % l)
            for ci, (c0, L, seq, first, last) in enumerate(chunks):
                b = tm_proj(wv, wkey, 0, 512, c0, L)
                V("act", lambda e, b=b, ci=ci, L=L: e.activation(oTM[0:L, ci, :], PS[b][0:L, :], AF.Sigmoid), ["P%d" % b], ["oTM"])

            for a in range(4):
                V("dve", lambda e, a=a: e.memset(CS[1]["c_kref"][a], 0.0), (), ["p1_kref%d" % a])
            wv, wkey = wload(w_in_v[:, :, 4632:5656], w_in_f[:, :, 4632:5656], 8, 1024, "wb_in%d_4632" % l)
            for hh in range(4):
                b = fm_proj(wv, wkey, hh * 128, 128, n, hT, "hT", 8)
                V("act", lambda e, b=b, hh=hh: e.activation(gqT[:, hh, 0:n], PS[b][:, 0:n], AF.Silu), ["P%d" % b], ["gqT"])
            for hh in range(4):
                b = fm_proj(wv, wkey, 512 + hh * 128, 128, n, hT, "hT", 8)
                fb, fk = (cacc, "cacc") if hh % 2 == 0 else (xp, "xp")
                V("act", lambda e, b=b, fb=fb: e.activation(fb[:, 0:n], PS[b][:, 0:n], AF.Sigmoid), ["P%d" % b], [fk])
                V("dve", lambda e, hh=hh, fb=fb: e.tensor_scalar(fb[:, 0:n], fb[:, 0:n], H["oml"][:, hh:hh + 1], H["lb"][:, hh:hh + 1], ALU.mult, ALU.add),
                  [fk, "hs%d" % l], [fk])
                V("dve", lambda e, hh=hh, fb=fb: e.tensor_scalar(gkT[:, hh, 0:n], fb[:, 0:n], -1.0, 1.0, ALU.mult, ALU.add), [fk], ["gkT"])
                V("act", lambda e, hh=hh, fb=fb: e.activation(gT[:, hh, 0:n], fb[:, 0:n], AF.Ln), [fk], ["gT"])
            wv, wkey = wload(w_in_v[:, :, 5656:6680], w_in_f[:, :, 5656:6680], 8, 1024, "wb_in%d_5656" % l)
            for ci, (c0, L, seq, first, last) in enumerate(chunks):
                b = tm_proj(wv, wkey, 0, 512, c0, L)
                V("act", lambda e, b=b, ci=ci, L=L: e.copy(gviTM[0:L, ci, :], PS[b][0:L, :]), ["P%d" % b], ["gviTM"])
                b = tm_proj(wv, wkey, 512, 512, c0, L)
                V("act", lambda e, b=b, ci=ci, L=L: e.activation(ggTM[0:L, ci, :], PS[b][0:L, :], AF.Silu), ["P%d" % b], ["ggTM"])
            for ci, ch in enumerate(chunks):
                la = S.capture(one_chunk, l, tile, ci, ch, "ml", mlstm_chunk)
                lb = S.capture(one_chunk, l, tile, ci, ch, "hg", hgrn_chunk)
                interleave2(la, lb)
            if dbg and l == 0 and tile.get("first_tile"):
                dbg_out("mixT", mixT[:, :, 0:n], [128, 16, n], ["mixT"])

            w_out_v = w_out_b[l].rearrange("(cc p) d -> p cc d", p=128)
            w_out_f = w_out[l].rearrange("(cc p) d -> p cc d", p=128)
            for hf in range(2):
                wv, wkey = wload(w_out_v[:, :, hf * 512:(hf + 1) * 512], w_out_f[:, :, hf * 512:(hf + 1) * 512], 16, 512, "wb_out%d_%d" % (l, hf))
                for j in range(4):
                    b = fm_proj(wv, wkey, j * 128, 128, n, mixT, "mixT", 16)
                    dc = hf * 4 + j
                    V("dve", lambda e, b=b, dc=dc: e.tensor_tensor(xT[:, dc, 0:n], xT[:, dc, 0:n], PS[b][:, 0:n], ALU.add), ["P%d" % b, "xT"], ["xT"])
            fence_all()
            rmsnorm_fm(n, P[:, 88:96], hT, sqs, "sqsF")
            w1 = w_ffn_in_b[l].rearrange("(dc p) c -> p dc c", p=128)
            w1f = w_ffn_in[l].rearrange("(dc p) c -> p dc c", p=128)
            for t in range(6):
                c0 = t * 1024
                nc_ = min(1024, 2 * DFF - c0)
                wv, wkey = wload(w1[:, :, c0:c0 + nc_], w1f[:, :, c0:c0 + nc_], 8, nc_, "wb_f1%d_%d" % (l, c0))
                for j in range(nc_ // 128):
                    fch = (c0 // 128) + j
                    b = fm_proj(wv, wkey, j * 128, 128, n, hT, "hT", 8)
                    if fch < 22:
                        V("act", lambda e, b=b, fch=fch: e.activation(actT[:, fch, 0:n], PS[b][:, 0:n], AF.Silu), ["P%d" % b], ["actT"])
                    else:
                        f2 = fch - 22
                        V("dve", lambda e, b=b, f2=f2: e.tensor_tensor(actT[:, f2, 0:n], actT[:, f2, 0:n], PS[b][:, 0:n], ALU.mult),
                          ["P%d" % b, "actT"], ["actT"])
            w2 = w_ffn_out_b[l].rearrange("(fc p) d -> p fc d", p=128)
            w2f = w_ffn_out[l].rearrange("(fc p) d -> p fc d", p=128)
            for t in range(4):
                wv, wkey = wload(w2[:, :, t * 256:(t + 1) * 256], w2f[:, :, t * 256:(t + 1) * 256], 22, 256, "wb_f2%d_%d" % (l, t))
                for j in range(2):
                    b = fm_proj(wv, wkey, j * 128, 128, n, actT, "actT", 22)
                    dc = t * 2 + j
                    V("dve", lambda e, b=b, dc=dc: e.tensor_tensor(xT[:, dc, 0:n], xT[:, dc, 0:n], PS[b][:, 0:n], ALU.add), ["P%d" % b, "xT"], ["xT"])

        def ssd_chunk(l, tile, ci, ch):
            (c0, L, seq, first, last) = ch
            H = hs[l]
            ST = stt[l]
            P = pv[l]
            par = ci % 2
            kp = "p%d_" % par
            c = CS[par]
            R = dict(rows)
            R.update(c["rows"])
            sm = c["c_small"]
            c_sc = c["c_sc"]
            c_xsTM = c["c_xsTM"]
            c_xdt = c["c_xdt"]
            c_xdec = c["c_xdec"]
            c_BTM = c["c_BTM"]
            c_cbT = c["c_cbT"]
            c_dT = c["c_dT"]
            c_MT = c["c_MT"]
            c_t2 = c["c_t2"]
            c_y = c["c_y"]
            V("dve", lambda e: e.tensor_tensor_scan(R["acum"][0:16, 0:L], ones32[0:16, 0:L], R["a"][0:16, c0:c0 + L], 0.0, ALU.mult, ALU.add),
              [kp + "rows", "rowsT", "ones32"], [kp + "rows"])
            hl = sm[0:16, 128:256].bitcast(BF16)
            nhl = R["nacum"][0:16, :].bitcast(BF16)
            a_hi, a_lo, n_hi, n_lo = hl[:, 0:L], hl[:, 128:128 + L], nhl[:, 0:L], nhl[:, 128:128 + L]
            V("dve", lambda e: e.tensor_copy(a_hi, R["acum"][0:16, 0:L]), [kp + "rows"], [kp + "c_small"])
            V("dve", lambda e: e.tensor_tensor(R["decend"][0:16, 0:L], R["acum"][0:16, 0:L], a_hi, ALU.subtract), [kp + "rows", kp + "c_small"], [kp + "rows"])
            V("dve", lambda e: e.tensor_copy(a_lo, R["decend"][0:16, 0:L]), [kp + "rows"], [kp + "c_small"])
            V("dve", lambda e: e.tensor_scalar(n_hi, a_hi, -1.0, None, ALU.mult), [kp + "c_small"], [kp + "rows"])
            V("dve", lambda e: e.tensor_scalar(n_lo, a_lo, -1.0, None, ALU.mult), [kp + "c_small"], [kp + "rows"])
            V("act", lambda e: e.activation(R["decend"][0:16, 0:L], R["acum"][0:16, 0:L], AF.Exp,
                                            bias=R["acum"][0:16, L - 1:L], scale=-1.0), [kp + "rows"], [kp + "rows"])
            b = nb()

            def f(e):
                e.transpose(PS[b][0:L, 0:16], R["dt"][0:16, c0:c0 + L], ident32[0:16, 0:16])
                e.transpose(PS[b][0:L, 16:32], R["acum"][0:16, 0:L], ident32[0:16, 0:16])
                return e.transpose(PS[b][0:L, 32:48], R["decend"][0:16, 0:L], ident32[0:16, 0:16])
            V("pe", f, [kp + "rows", "rowsT", "ident32"], ["P%d" % b])
            V("dve", lambda e: e.tensor_copy(sm[0:L, 0:16], PS[b][0:L, 0:16]), ["P%d" % b], [kp + "c_small"])
            V("act", lambda e: e.activation(sm[0:L, 16:32], PS[b][0:L, 16:32], AF.Exp), ["P%d" % b], [kp + "c_small"])
            V("dve", lambda e: e.tensor_tensor(sm[0:L, 32:48], PS[b][0:L, 32:48], sm[0:L, 0:16], ALU.mult), ["P%d" % b, kp + "c_small"], [kp + "c_small"])
            bb = bcast_rows(R["acum"][0:16, L - 1:L], 16, kp + "rows", sm, kp)
            V("act", lambda e: e.activation(sm[:, 48:64], PS[bb][:, 0:16], AF.Exp), ["P%d" % bb], [kp + "c_small"])
            b1 = nb()
            pb1 = PS[b1].bitcast(BF16)

            def f(e):
                for cc in range(8):
                    ins = e.transpose(pb1[0:L, cc * 128:(cc + 1) * 128], xbcT[:, cc, c0:c0 + L], identb)
                return ins
            V("pe", f, ["xbcT", "identb"], ["P%d" % b1])
            b2 = nb()
            pb2 = PS[b2].bitcast(BF16)

            def f(e):
                for cc in range(2):
                    ins = e.transpose(pb2[0:L, cc * 128:(cc + 1) * 128], xbcT[:, 8 + cc, c0:c0 + L], identb)
                return ins
            V("pe", f, ["xbcT", "identb"], ["P%d" % b2])
            V("act", lambda e: e.copy(c_xsTM[0:L, :], pb1[0:L, :]), ["P%d" % b1], [kp + "c_xsTM"])
            V("act", lambda e: e.copy(c_BTM[0:L, :], pb2[0:L, 0:256]), ["P%d" % b2], [kp + "c_BTM"])
            xs3 = c_xsTM[0:L, :].rearrange("p (h c) -> p h c", h=16)
            V("dve", lambda e: e.tensor_tensor(c_xdt[0:L, :].rearrange("p (h c) -> p h c", h=16), xs3,
                                               sm[0:L, 0:16].unsqueeze(2).broadcast_to([L, 16, 64]), ALU.mult), [kp + "c_xsTM", kp + "c_small"], [kp + "c_xdt"])
            V("dve", lambda e: e.tensor_tensor(c_xdec[0:L, :].rearrange("p (h c) -> p h c", h=16), xs3,
                                               sm[0:L, 32:48].unsqueeze(2).broadcast_to([L, 16, 64]), ALU.mult), [kp + "c_xsTM", kp + "c_small"], [kp + "c_xdec"])
            b = nb()

            def f(e):
                for g in range(2):
                    ins = e.matmul(PS[b][0:L, g * 128:g * 128 + L], xbcT[:, 8 + g, c0:c0 + L], xbcT[:, 10 + g, c0:c0 + L], start=True, stop=True)
                return ins
            V("pe", f, ["xbcT"], ["P%d" % b])
            V("act", lambda e: e.copy(c_cbT[0:L, :, 0:L], PS[b][0:L, 0:256].rearrange("p (g c) -> p g c", g=2)[:, :, 0:L]), ["P%d" % b], [kp + "c_cbT"])
            for q4 in range(4):
                b = nb()

                def f(e, q4=q4, b=b):
                    if L == 128:
                        rhs4 = identb[0:16, 4 * q4:4 * q4 + 4].unsqueeze(2).broadcast_to([16, 4, 128])
                        e.matmul(PS[b][0:L, :], n_hi, rhs4, start=True, stop=False)
                        e.matmul(PS[b][0:L, :], n_lo, rhs4, start=False, stop=False)
                        e.matmul(PS[b][0:L, :], identb[0:L, 0:L], negmask4[0:L, :, :], start=False, stop=False)
                    for j in range(4):
                        hh = q4 * 4 + j
                        o = PS[b][0:L, j * 128:j * 128 + L]
                        selb = identb[0:16, hh:hh + 1].broadcast_to([16, L])
                        if L != 128:
                            e.matmul(o, n_hi, selb, start=True, stop=False)
                            e.matmul(o, n_lo, selb, start=False, stop=False)
                            e.matmul(o, identb[0:L, 0:L], negmask4[0:L, 0, 0:L], start=False, stop=False)
                        e.matmul(o, selb, a_hi, start=False, stop=False)
                        ins = e.matmul(o, selb, a_lo, start=False, stop=(j == 3 or L != 128))
                    return ins
                V("pe", f, [kp + "rows", kp + "c_small", "identb", "negmask4"], ["P%d" % b])
                V("act", lambda e, q4=q4, b=b: e.activation(c_dT[0:L, q4 * 4:(q4 + 1) * 4, 0:L],
                                                            PS[b][0:L, :].rearrange("p (j c) -> p j c", j=4)[:, :, 0:L], AF.Exp), ["P%d" % b], [kp + "c_dT"])
            for g in range(2):
                V("dve", lambda e, g=g: e.tensor_tensor(c_MT[0:L, g * 8:(g + 1) * 8, 0:L], c_dT[0:L, g * 8:(g + 1) * 8, 0:L],
                                                        c_cbT[0:L, g:g + 1, 0:L].broadcast_to([L, 8, L]), ALU.mult), [kp + "c_dT", kp + "c_cbT"], [kp + "c_MT"])
            V("act", lambda e: e.copy(hTb, ST["h32"]), ["h32_%d" % l], ["hTb"])
            byi = [nb(), nb()]
            bys = [nb(), nb()]
            for g in range(2):
                def f(e, g=g):
                    for j in range(8):
                        hh = g * 8 + j
                        ins = e.matmul(PS[byi[g]][0:L, j * 64:(j + 1) * 64], c_MT[0:L, hh, 0:L], c_xdt[0:L, hh * 64:(hh + 1) * 64], start=True, stop=True)
                    return ins
                V("pe", f, [kp + "c_MT", kp + "c_xdt"], ["P%d" % byi[g]])
                V("pe", lambda e, g=g: e.matmul(PS[bys[g]][0:L, :], xbcT[:, 10 + g, c0:c0 + L], hTb[:, g * 512:(g + 1) * 512], start=True, stop=True),
                  ["xbcT", "hTb"], ["P%d" % bys[g]])
            V("dve", lambda e: e.tensor_tensor(c_t2[0:L, :].rearrange("p (h c) -> p h c", h=16), xs3,
                                               H["Dbc"][0:L, :].unsqueeze(2).broadcast_to([L, 16, 64]), ALU.mult), [kp + "c_xsTM", "hs%d" % l], [kp + "c_t2"])
            for g in range(2):
                sl = slice(g * 512, (g + 1) * 512)
                V("dve", lambda e, g=g, sl=sl: e.tensor_tensor(c_y[0:L, sl].rearrange("p (h c) -> p h c", h=8),
                                                               PS[bys[g]][0:L, :].rearrange("p (h c) -> p h c", h=8),
                                                               sm[0:L, 16 + g * 8:24 + g * 8].unsqueeze(2).broadcast_to([L, 8, 64]), ALU.mult),
                  ["P%d" % bys[g], kp + "c_small"], [kp + "c_y"])
                V("dve", lambda e, g=g, sl=sl: e.tensor_tensor(c_y[0:L, sl], c_y[0:L, sl], PS[byi[g]][0:L, :], ALU.add), ["P%d" % byi[g], kp + "c_y"], [kp + "c_y"])
            V("dve", lambda e: e.tensor_tensor(c_y[0:L, :], c_y[0:L, :], c_t2[0:L, :], ALU.add), [kp + "c_y", kp + "c_t2"], [kp + "c_y"])
            V("dve", lambda e: e.tensor_tensor(c_y[0:L, :], c_y[0:L, :], zs[0:L, ci, :], ALU.mult), [kp + "c_y", "zs"], [kp + "c_y"])
            V("act", lambda e: e.activation(c_t2[0:L, :], c_y[0:L, :], AF.Square, accum_out=sm[0:L, 64:65]), [kp + "c_y"], [kp + "c_t2", kp + "c_small"])
            V("act", lambda e: e.activation(sm[0:L, 64:65], sm[0:L, 64:65], AF.Ln, bias=EPS, scale=1.0 / 1024), [kp + "c_small"], [kp + "c_small"])
            V("act", lambda e: e.activation(sm[0:L, 64:65], sm[0:L, 64:65], AF.Exp, scale=-0.5), [kp + "c_small"], [kp + "c_small"])
            V("act", lambda e: e.activation(c_y[0:L, :], c_y[0:L, :], AF.Identity, scale=sm[0:L, 64:65]), [kp + "c_y", kp + "c_small"], [kp + "c_y"])
            for hf in range(2):
                b = nb()

                def f(e, hf=hf, b=b):
                    for j in range(4):
                        cc = hf * 4 + j
                        ins = e.transpose(PS[b][:, j * 128:j * 128 + L], c_y[0:L, cc * 128:(cc + 1) * 128], ident32[0:L, 0:L])
                    return ins
                V("pe", f, [kp + "c_y", "ident32"], ["P%d" % b])
                if hf == 0:
                    V("dve", lambda e, b=b: e.tensor_tensor(mixT[:, 0:4, c0:c0 + L], PS[b][:, :].rearrange("p (j c) -> p j c", j=4)[:, :, 0:L],
                                                            pv[l][:, 68:72].unsqueeze(2).broadcast_to([128, 4, L]), ALU.mult), ["P%d" % b, "pv%d" % l], ["mixT"])
                else:
                    for j in range(4):
                        cc = hf * 4 + j
                        V("act", lambda e, b=b, j=j, cc=cc: e.activation(mixT[:, cc, c0:c0 + L], PS[b][:, j * 128:j * 128 + L], AF.Copy,
                                                                         scale=pv[l][:, 68 + cc:69 + cc]), ["P%d" % b], ["mixT"])
            bu = [nb(), nb()]
            for g in range(2):
                V("pe", lambda e, g=g: e.matmul(PS[bu[g]][:, :], c_BTM[0:L, g * 128:(g + 1) * 128], c_xdec[0:L, g * 512:(g + 1) * 512], start=True, stop=True),
                  [kp + "c_BTM", kp + "c_xdec"], ["P%d" % bu[g]])
            V("dve", lambda e: e.tensor_tensor(ST["h32"].rearrange("p (h c) -> p h c", h=16), ST["h32"].rearrange("p (h c) -> p h c", h=16),
                                               sm[:, 48:64].unsqueeze(2).broadcast_to([128, 16, 64]), ALU.mult), ["h32_%d" % l, kp + "c_small", "hTb"], ["h32_%d" % l])
            for g in range(2):
                sl = slice(g * 512, (g + 1) * 512)
                V("dve", lambda e, g=g, sl=sl: e.tensor_tensor(ST["h32"][:, sl], ST["h32"][:, sl], PS[bu[g]][:, :], ALU.add),
                  ["h32_%d" % l, "P%d" % bu[g]], ["h32_%d" % l])

        def mlstm_chunk(l, tile, ci, ch):
            (c0, L, seq, first, last) = ch
            H = hs[l]
            ST = stt[l]
            cs = slice(c0, c0 + L)
            par = 0
            kp = "p%d_" % par
            c = CS[par]
            R = dict(rows)
            R.update(c["rows"])
            sm = c["c_small"]
            c_sc = c["c_sc"]
            c_wT = c["c_wT"]
            c_SwT = c["c_SwT"]
            c_kw = c["c_kw"]
            c_int = c["c_int"]
            c_hm = c["c_hm"]
            m0 = ST["m"][0:4, :]
            V("dve", lambda e: e.tensor_tensor_scan(R["fcum"][0:4, 0:L], ones32[0:4, 0:L], R["lf"][0:4, cs], 0.0, ALU.mult, ALU.add), [kp + "rows", "rowsT", "ones32"], [kp + "rows"])
            V("dve", lambda e: e.tensor_tensor(R["u"][0:4, 0:L], R["ig"][0:4, cs], R["fcum"][0:4, 0:L], ALU.subtract), [kp + "rows", "rowsT"], [kp + "rows"])
            V("dve", lambda e: e.tensor_tensor_scan(R["cmm"][0:4, 0:L], ones32[0:4, 0:L], R["u"][0:4, 0:L], m0, ALU.mult, ALU.max), [kp + "rows", "ones32", "m%d" % l], [kp + "rows"])
            V("dve", lambda e: e.tensor_scalar(R["ncmm"][0:4, 0:L], R["cmm"][0:4, 0:L], -1.0, None, ALU.mult), [kp + "rows"], [kp + "rows"])
            hl = sm[0:4, 128:256].bitcast(BF16)
            nhl = R["wkr"][0:4, :].bitcast(BF16)
            u_hi, u_lo, c_hi, c_lo = hl[:, 0:L], hl[:, 128:128 + L], nhl[:, 0:L], nhl[:, 128:128 + L]
            V("act", lambda e: e.activation(R["winter"][0:4, 0:L], R["cmm"][0:4, 0:L], AF.Exp, bias=m0, scale=-1.0), [kp + "rows", "m%d" % l], [kp + "rows"])
            V("dve", lambda e: e.tensor_tensor(R["emi"][0:4, 0:L], R["fcum"][0:4, 0:L], R["cmm"][0:4, 0:L], ALU.add), [kp + "rows"], [kp + "rows"])
            V("act", lambda e: e.activation(R["emi"][0:4, 0:L], R["emi"][0:4, 0:L], AF.Exp, scale=-1.0), [kp + "rows"], [kp + "rows"])
            V("act", lambda e: e.activation(R["wk2"][0:4, 0:L], R["u"][0:4, 0:L], AF.Exp, bias=R["ncmm"][0:4, L - 1:L]), [kp + "rows"], [kp + "rows"])
            V("act", lambda e: e.activation(sm[0:4, 70:71], R["cmm"][0:4, L - 1:L], AF.Exp, bias=m0, scale=-1.0), [kp + "rows", "m%d" % l], [kp + "c_small"])
            V("dve", lambda e: e.tensor_tensor(ST["m"][0:4, :], R["fcum"][0:4, L - 1:L], R["cmm"][0:4, L - 1:L], ALU.add),
              [kp + "rows"], ["m%d" % l])
            b = nb()

            def f(e):
                e.transpose(PS[b][0:L, 0:4], R["winter"][0:4, 0:L], ident32[0:4, 0:4])
                e.transpose(PS[b][0:L, 4:8], R["emi"][0:4, 0:L], ident32[0:4, 0:4])
                return e.transpose(PS[b][0:L, 8:12], R["wk2"][0:4, 0:L], ident32[0:4, 0:4])
            V("pe", f, [kp + "rows", "ident32"], ["P%d" % b])
            V("dve", lambda e: e.tensor_copy(sm[0:L, 72:84], PS[b][0:L, 0:12]), ["P%d" % b], [kp + "c_small"])
            b = nb()

            V("dve", lambda e: e.tensor_copy(u_hi, R["u"][0:4, 0:L]), [kp + "rows"], [kp + "c_small"])
            V("dve", lambda e: e.tensor_tensor(R["fcum"][0:4, 0:L], R["u"][0:4, 0:L], u_hi, ALU.subtract), [kp + "rows", kp + "c_small"], [kp + "rows"])
            V("dve", lambda e: e.tensor_copy(u_lo, R["fcum"][0:4, 0:L]), [kp + "rows"], [kp + "c_small"])
            V("dve", lambda e: e.tensor_copy(c_hi, R["ncmm"][0:4, 0:L]), [kp + "rows"], [kp + "rows"])
            V("dve", lambda e: e.tensor_tensor(R["fcum"][0:4, 0:L], R["ncmm"][0:4, 0:L], c_hi, ALU.subtract), [kp + "rows"], [kp + "rows"])
            V("dve", lambda e: e.tensor_copy(c_lo, R["fcum"][0:4, 0:L]), [kp + "rows"], [kp + "rows"])

            def f(e):
                if L == 128:
                    rhs4 = identb[0:4, 0:4].unsqueeze(2).broadcast_to([4, 4, 128])
                    e.matmul(PS[b][0:L, :], u_hi, rhs4, start=True, stop=False)
                    e.matmul(PS[b][0:L, :], u_lo, rhs4, start=False, stop=False)
                    e.matmul(PS[b][0:L, :], identb[0:L, 0:L], negmask4[0:L, :, :], start=False, stop=False)
                for hh in range(4):
                    o = PS[b][0:L, hh * 128:hh * 128 + L]
                    selb = identb[0:4, hh:hh + 1].broadcast_to([4, L])
                    if L != 128:
                        e.matmul(o, u_hi, selb, start=True, stop=False)
                        e.matmul(o, u_lo, selb, start=False, stop=False)
                        e.matmul(o, identb[0:L, 0:L], negmask4[0:L, 0, 0:L], start=False, stop=False)
                    e.matmul(o, selb, c_hi, start=False, stop=False)
                    ins = e.matmul(o, selb, c_lo, start=False, stop=(hh == 3 or L != 128))
                return ins
            V("pe", f, [kp + "rows", kp + "c_small", "identb", "negmask4"], ["P%d" % b])
            V("act", lambda e: e.activation(c_wT[0:L, :, 0:L], PS[b][0:L, :].rearrange("p (j c) -> p j c", j=4)[:, :, 0:L], AF.Exp), ["P%d" % b], [kp + "c_wT"])
            b = nb()

            def f(e):
                for hh in range(4):
                    ins = e.matmul(PS[b][0:L, hh * 128:hh * 128 + L], kT[:, hh, cs], qT[:, hh, cs], start=True, stop=True)
                return ins
            V("pe", f, ["kT", "qT"], ["P%d" % b])
            if first and tile["kind"] == "p":
                V("dve", lambda e: e.tensor_copy(PS[b][0:1, 0:512:128], s0buf[0:1, :]), ["s0buf", "P%d" % b], ["P%d" % b])
            V("dve", lambda e: e.tensor_tensor(c_SwT[0:L, :, 0:L], PS[b][0:L, :].rearrange("p (j c) -> p j c", j=4)[:, :, 0:L], c_wT[0:L, :, 0:L], ALU.mult),
              ["P%d" % b, kp + "c_wT"], [kp + "c_SwT"])
            V("act", lambda e: e.copy(Cb, ST["C32"]), ["C32_%d" % l], ["Cb"])
            bi_ = [nb(), nb()]
            bx_ = [nb(), nb()]
            v4 = vTM[0:L, ci, :].rearrange("p (h c) -> p h c", h=4)
            for pr in range(2):
                def f(e, pr=pr):
                    for j in range(2):
                        hh = pr * 2 + j
                        ins = e.matmul(PS[bi_[pr]][0:L, j * 256:j * 256 + 129], c_SwT[0:L, hh, 0:L], v4[:, hh, :], start=True, stop=True)
                    return ins
                V("pe", f, [kp + "c_SwT", "vTM"], ["P%d" % bi_[pr]])

                def f(e, pr=pr):
                    for j in range(2):
                        hh = pr * 2 + j
                        ins = e.matmul(PS[bx_[pr]][0:L, j * 256:j * 256 + 129], qT[:, hh, cs], Cb[:, hh, :], start=True, stop=True)
                    return ins
                V("pe", f, ["qT", "Cb"], ["P%d" % bx_[pr]])
                for j in range(2):
                    hh = pr * 2 + j
                    V("act", lambda e, pr=pr, j=j, hh=hh: e.activation(c_int[0:L, hh, :], PS[bx_[pr]][0:L, j * 256:j * 256 + 129], AF.Copy,
                                                                       scale=sm[0:L, 72 + hh:73 + hh]), ["P%d" % bx_[pr], kp + "c_small"], [kp + "c_int"])
                V("dve", lambda e, pr=pr: e.tensor_tensor(c_int[0:L, pr * 2:pr * 2 + 2, :], c_int[0:L, pr * 2:pr * 2 + 2, :],
                                                          PS[bi_[pr]][0:L, :].rearrange("p (j c) -> p j c", j=2)[:, :, 0:129], ALU.add),
                  [kp + "c_int", "P%d" % bi_[pr]], [kp + "c_int"])
            V("act", lambda e: e.activation(sm[0:L, 84:88], c_int[0:L, :, 128], AF.Abs), [kp + "c_int"], [kp + "c_small"])
            V("dve", lambda e: e.tensor_tensor(sm[0:L, 84:88], sm[0:L, 84:88], sm[0:L, 76:80], ALU.max), [kp + "c_small"], [kp + "c_small"])
            V("dve", lambda e: e.reciprocal(sm[0:L, 84:88], sm[0:L, 84:88]), [kp + "c_small"], [kp + "c_small"])
            V("dve", lambda e: e.tensor_tensor(c_hm[0:L, :, :], c_int[0:L, :, 0:128], sm[0:L, 84:88].unsqueeze(2).broadcast_to([L, 4, 128]), ALU.mult),
              [kp + "c_int", kp + "c_small"], [kp + "c_hm"])
            V("act", lambda e: e.activation(c_sc[0:L, 0:512], c_hm[0:L, :, :].rearrange("p h c -> p (h c)"), AF.Square), [kp + "c_hm"], [kp + "c_sc"])
            V("dve", lambda e: e.tensor_reduce(sm[0:L, 88:92], c_sc[0:L, 0:512].rearrange("p (h c) -> p h c", h=4), AX.X, ALU.add), [kp + "c_sc"], [kp + "c_small"])
            V("act", lambda e: e.activation(sm[0:L, 88:92], sm[0:L, 88:92], AF.Ln, bias=EPS, scale=1.0 / 128), [kp + "c_small"], [kp + "c_small"])
            V("act", lambda e: e.activation(sm[0:L, 88:92], sm[0:L, 88:92], AF.Exp, scale=-0.5), [kp + "c_small"], [kp + "c_small"])
            V("dve", lambda e: e.tensor_tensor(c_hm[0:L, :, :], c_hm[0:L, :, :], sm[0:L, 88:92].unsqueeze(2).broadcast_to([L, 4, 128]), ALU.mult),
              [kp + "c_hm", kp + "c_small"], [kp + "c_hm"])
            V("dve", lambda e: e.tensor_tensor(c_hm[0:L, :, :], c_hm[0:L, :, :], oTM[0:L, ci, :].rearrange("p (h c) -> p h c", h=4), ALU.mult),
              [kp + "c_hm", "oTM"], [kp + "c_hm"])
            b = nb()

            def f(e):
                for hh in range(4):
                    ins = e.transpose(PS[b][:, hh * 128:hh * 128 + L], c_hm[0:L, hh, :], ident32[0:L, 0:L])
                return ins
            V("pe", f, [kp + "c_hm", "ident32"], ["P%d" % b])
            for hh in range(4):
                V("act", lambda e, hh=hh: e.activation(mixT[:, 8 + hh, cs], PS[b][:, hh * 128:hh * 128 + L], AF.Identity, scale=pv[l][:, 76 + hh:77 + hh]),
                  ["P%d" % b], ["mixT"])
            V("dve", lambda e: e.tensor_tensor(c_kw[0:L, :, :], kTM[0:L, ci, :].rearrange("p (h c) -> p h c", h=4),
                                               sm[0:L, 80:84].unsqueeze(2).broadcast_to([L, 4, 128]), ALU.mult), ["kTM", kp + "c_small"], [kp + "c_kw"])
            bu = [nb(), nb()]
            for pr in range(2):
                def f(e, pr=pr):
                    for j in range(2):
                        hh = pr * 2 + j
                        ins = e.matmul(PS[bu[pr]][:, j * 256:j * 256 + 129], c_kw[0:L, hh, :], v4[:, hh, :], start=True, stop=True)
                    return ins
                V("pe", f, [kp + "c_kw", "vTM"], ["P%d" % bu[pr]])
            bb = bcast_rows(sm[0:4, 70:71], 4, kp + "c_small", sm, kp)
            V("dve", lambda e: e.tensor_copy(sm[:, 92:96], PS[bb][:, 0:4]), ["P%d" % bb], [kp + "c_small"])
            V("dve", lambda e: e.tensor_tensor(ST["C32"], ST["C32"], sm[:, 92:96].unsqueeze(2).broadcast_to([128, 4, 129]), ALU.mult),
              ["C32_%d" % l, kp + "c_small", "Cb"], ["C32_%d" % l])
            for pr in range(2):
                V("dve", lambda e, pr=pr: e.tensor_tensor(ST["C32"][:, pr * 2:pr * 2 + 2, :], ST["C32"][:, pr * 2:pr * 2 + 2, :],
                                                          PS[bu[pr]][:, :].rearrange("p (j c) -> p j c", j=2)[:, :, 0:129], ALU.add),
                  ["C32_%d" % l, "P%d" % bu[pr]], ["C32_%d" % l])

        def hgrn_chunk(l, tile, ci, ch):
            (c0, L, seq, first, last) = ch
            ST = stt[l]
            cs = slice(c0, c0 + L)
            par = 1
            kp = "p%d_" % par
            c = CS[par]
            R = dict(rows)
            R.update(c["rows"])
            sm = c["c_small"]
            c_sc = c["c_sc"]
            c_gcp = c["c_gcp"]
            c_e = c["c_e"]
            c_qref = c["c_qref"]
            c_qdec = c["c_qdec"]
            c_kref = c["c_kref"]
            c_kend = c["c_kend"]
            c_kendTM = c["c_kendTM"]
            c_ATb = c["c_ATb"]
            c_o = c["c_o"]
            nsb = max(1, L // 32)
            V("dve", lambda e: e.memset(c_gcp[:, :, 0:1], 0.0), (), [kp + "c_gcp"])
            for hh in range(4):
                V("dve", lambda e, hh=hh: e.tensor_tensor_scan(c_gcp[:, hh, 1:1 + L], ones32[:, 0:L], gT[:, hh, cs], 0.0, ALU.mult, ALU.add),
                  ["gT", "ones32"], [kp + "c_gcp"])
            gc = c_gcp[:, :, 1:1 + L]
            gref_full = c_gcp[:, :, 0:L:32].unsqueeze(3).broadcast_to([128, 4, nsb, 32])
            V("dve", lambda e: e.tensor_tensor(c_e[:, :, 0:L].rearrange("p h (a j) -> p h a j", a=nsb), gc.rearrange("p h (a j) -> p h a j", a=nsb),
                                               gref_full, ALU.subtract), [kp + "c_gcp"], [kp + "c_e"])
            V("act", lambda e: e.activation(c_e[:, :, 0:L], c_e[:, :, 0:L], AF.Exp), [kp + "c_e"], [kp + "c_e"])
            V("dve", lambda e: e.tensor_tensor(c_qref[:, :, 0:L], c_e[:, :, 0:L], gqT[:, :, cs], ALU.mult), [kp + "c_e", "gqT"], [kp + "c_qref"])
            V("act", lambda e: e.activation(c_e[:, :, 0:L], gc, AF.Exp), [kp + "c_gcp", kp + "c_qref"], [kp + "c_e"])
            V("dve", lambda e: e.tensor_tensor(c_qdec[:, :, 0:L], c_e[:, :, 0:L], gqT[:, :, cs], ALU.mult), [kp + "c_e", "gqT"], [kp + "c_qdec"])
            for a in range(nsb):
                na = min(L, 32 * (a + 1))
                V("dve", lambda e, a=a, na=na: e.tensor_tensor(c_e[:, :, 0:na], c_gcp[:, :, 32 * a:32 * a + 1].broadcast_to([128, 4, na]),
                                                               c_gcp[:, :, 1:1 + na], ALU.subtract), [kp + "c_gcp", kp + "c_qdec"], [kp + "c_e"])
                V("act", lambda e, na=na: e.activation(c_e[:, :, 0:na], c_e[:, :, 0:na], AF.Exp), [kp + "c_e"], [kp + "c_e"])
                V("dve", lambda e, a=a, na=na: e.tensor_tensor(c_kref[a][:, :, 0:na], c_e[:, :, 0:na], gkT[:, :, c0:c0 + na], ALU.mult),
                  [kp + "c_e", "gkT"], [kp + "kref%d" % a])
            V("dve", lambda e: e.tensor_tensor(c_e[:, :, 0:L], c_gcp[:, :, L:L + 1].broadcast_to([128, 4, L]), gc, ALU.subtract),
              [kp + "c_gcp", kp + "kref%d" % (nsb - 1)], [kp + "c_e"])
            V("act", lambda e: e.activation(c_e[:, :, 0:L], c_e[:, :, 0:L], AF.Exp), [kp + "c_e"], [kp + "c_e"])
            V("dve", lambda e: e.tensor_tensor(c_kend[:, :, 0:L], c_e[:, :, 0:L], gkT[:, :, cs], ALU.mult), [kp + "c_e", "gkT"], [kp + "c_kend"])
            V("act", lambda e: e.activation(sm[:, 100:104], c_gcp[:, :, L], AF.Exp), [kp + "c_gcp"], [kp + "c_small"])
            b = nb()

            def f(e):
                for hh in range(4):
                    for a in range(nsb):
                        ins = e.matmul(PS[b][0:L, hh * 128 + 32 * a:hh * 128 + 32 * a + 32], c_kref[a][:, hh, 0:L], c_qref[:, hh, 32 * a:32 * a + 32],
                                       start=True, stop=True)
                return ins
            V("pe", f, [kp + "kref%d" % a for a in range(nsb)] + [kp + "c_qref"], ["P%d" % b])
            V("dve", lambda e: e.tensor_tensor(c_ATb[0:L, :, 0:L], PS[b][0:L, :].rearrange("p (j c) -> p j c", j=4)[:, :, 0:L],
                                               mask01b[0:L, 0:L].unsqueeze(1).broadcast_to([L, 4, L]), ALU.mult), ["P%d" % b, "mask01b"], [kp + "c_ATb"])
            V("act", lambda e: e.copy(Sb, ST["S32"]), ["S32_%d" % l], ["Sb"])
            bo = nb()

            def f(e):
                for hh in range(4):
                    o = PS[bo][0:L, hh * 128:(hh + 1) * 128]
                    e.matmul(o, c_ATb[0:L, hh, 0:L], gviTM[0:L, ci, hh * 128:(hh + 1) * 128], start=True, stop=False)
                    ins = e.matmul(o, c_qdec[:, hh, 0:L], Sb[:, hh, :], start=False, stop=True)
                return ins
            V("pe", f, [kp + "c_ATb", "gviTM", kp + "c_qdec", "Sb"], ["P%d" % bo])
            V("act", lambda e: e.copy(c_o[0:L, :, :], PS[bo][0:L, :].rearrange("p (h c) -> p h c", h=4)), ["P%d" % bo], [kp + "c_o"])
            V("act", lambda e: e.activation(c_sc[0:L, 512:1024], c_o[0:L, :, :].rearrange("p h c -> p (h c)"), AF.Square), [kp + "c_o"], [kp + "c_sc2"])
            V("dve", lambda e: e.tensor_reduce(sm[0:L, 104:108], c_sc[0:L, 512:1024].rearrange("p (h c) -> p h c", h=4), AX.X, ALU.add), [kp + "c_sc2"], [kp + "c_small"])
            V("act", lambda e: e.activation(sm[0:L, 104:108], sm[0:L, 104:108], AF.Ln, bias=EPS, scale=1.0 / 128), [kp + "c_small"], [kp + "c_small"])
            V("act", lambda e: e.activation(sm[0:L, 104:108], sm[0:L, 104:108], AF.Exp, scale=-0.5), [kp + "c_small"], [kp + "c_small"])
            V("dve", lambda e: e.tensor_tensor(c_o[0:L, :, :], c_o[0:L, :, :], sm[0:L, 104:108].unsqueeze(2).broadcast_to([L, 4, 128]), ALU.mult),
              [kp + "c_o", kp + "c_small"], [kp + "c_o"])
            V("dve", lambda e: e.tensor_tensor(c_o[0:L, :, :], c_o[0:L, :, :], ggTM[0:L, ci, :].rearrange("p (h c) -> p h c", h=4), ALU.mult),
              [kp + "c_o", "ggTM"], [kp + "c_o"])
            b = nb()

            def f(e):
                for hh in range(4):
                    ins = e.transpose(PS[b][:, hh * 128:hh * 128 + L], c_o[0:L, hh, :], ident32[0:L, 0:L])
                return ins
            V("pe", f, [kp + "c_o", "ident32"], ["P%d" % b])
            for hh in range(4):
                V("act", lambda e, hh=hh, b=b: e.activation(mixT[:, 12 + hh, cs], PS[b][:, hh * 128:hh * 128 + L], AF.Identity, scale=pv[l][:, 84 + hh:85 + hh]),
                  ["P%d" % b], ["mixT"])
            b = nb()
            pbt = PS[b].bitcast(BF16)

            def f(e):
                for hh in range(4):
                    ins = e.transpose(pbt[0:L, hh * 128:(hh + 1) * 128], c_kend[:, hh, 0:L], identb)
                return ins
            V("pe", f, [kp + "c_kend", "identb"], ["P%d" % b])
            V("act", lambda e: e.copy(c_kendTM[0:L, :, :].rearrange("p h c -> p (h c)"), pbt[0:L, 0:512]), ["P%d" % b], [kp + "c_kendTM"])
            bu = nb()

            def f(e):
                for hh in range(4):
                    ins = e.matmul(PS[bu][:, hh * 128:(hh + 1) * 128], c_kendTM[0:L, hh, :], gviTM[0:L, ci, hh * 128:(hh + 1) * 128], start=True, stop=True)
                return ins
            V("pe", f, [kp + "c_kendTM", "gviTM"], ["P%d" % bu])
            V("dve", lambda e: e.tensor_tensor(ST["S32"], ST["S32"], sm[:, 100:104].unsqueeze(2).broadcast_to([128, 4, 128]), ALU.mult),
              ["S32_%d" % l, kp + "c_small", "Sb"], ["S32_%d" % l])
            V("dve", lambda e: e.tensor_tensor(ST["S32"], ST["S32"], PS[bu][:, :].rearrange("p (h c) -> p h c", h=4), ALU.add),
              ["S32_%d" % l, "P%d" % bu], ["S32_%d" % l])

        def state_init_prompt(l, ch, which):
            ST = stt[l]
            if which == "ssd":
                V("pool", lambda e: e.memset(ST["h32"], 0.0), (), ["h32_%d" % l])
            elif which == "ml":
                V("pool", lambda e: e.memset(ST["C32"], 0.0), (), ["C32_%d" % l])
                V("pool", lambda e: e.memset(ST["m"], 0.0), (), ["m%d" % l])
            else:
                V("pool", lambda e: e.memset(ST["S32"], 0.0), (), ["S32_%d" % l])

        def state_init_sample(l, ch, which):
            (c0, L, seq, first, last) = ch
            ST = stt[l]
            if which == "ssd":
                S.dma("sp", io_ssd, state_ssd[l, seq].rearrange("(pr h2) p n -> (h2 p) pr n", h2=2), (), ["io_ssd"])
                for hf in range(2):
                    b = nb()

                    def f(e, hf=hf, b=b):
                        for j in range(4):
                            ins = e.transpose(PS[b][:, j * 128:(j + 1) * 128], io_ssd[:, hf * 4 + j, :], ident32)
                        return ins
                    V("pe", f, ["io_ssd", "ident32"], ["P%d" % b])
                    V("dve", lambda e, hf=hf, b=b: e.tensor_copy(ST["h32"][:, hf * 512:(hf + 1) * 512], PS[b][:, :]), ["P%d" % b], ["h32_%d" % l])
            elif which == "ml":
                S.dma("sp", ST["C32"][:, :, 0:128], state_mc[l, seq].rearrange("h k v -> k h v"), (), ["C32_%d" % l])
                S.dma("sp", io_n[0:4, :], state_mn[l, seq], (), ["io_n"])
                b = nb()
                V("pe", lambda e: e.transpose(PS[b][:, 0:4], io_n[0:4, :], ident32[0:4, 0:4]), ["io_n", "ident32"], ["P%d" % b])
                V("dve", lambda e: e.tensor_copy(ST["C32"][:, :, 128], PS[b][:, 0:4]), ["P%d" % b], ["C32_%d" % l])
                S.dma("sp", ST["m"][0:4, :], state_mm[l, seq].rearrange("(a b) -> a b", b=1), (), ["m%d" % l])
            else:
                S.dma("sp", ST["S32"], state_hg[l, seq].rearrange("h k v -> k h v"), (), ["S32_%d" % l])

        def state_store(pre):
            def fn(l, ch, which):
                (c0, L, seq, first, last) = ch
                ST = stt[l]
                if which == "ssd":
                    for hf in range(2):
                        b = nb()

                        def f(e, hf=hf, b=b):
                            for j in range(4):
                                ins = e.transpose(PS[b][:, j * 128:(j + 1) * 128], ST["h32"][:, (hf * 4 + j) * 128:(hf * 4 + j + 1) * 128], ident32)
                            return ins
                        V("pe", f, ["h32_%d" % l, "ident32"], ["P%d" % b])
                        V("dve", lambda e, hf=hf, b=b: e.tensor_copy(io_ssd[:, hf * 4:(hf + 1) * 4, :], PS[b][:, :].rearrange("p (j c) -> p j c", j=4)),
                          ["P%d" % b], ["io_ssd"])
                    S.dma("sp", O[pre + "_ssd"][l, seq].rearrange("(pr h2) p n -> (h2 p) pr n", h2=2), io_ssd, ["io_ssd"], ())
                elif which == "ml":
                    S.dma("sp", O[pre + "_mc"][l, seq].rearrange("h k v -> k h v"), ST["C32"][:, :, 0:128], ["C32_%d" % l], ())
                    b = nb()
                    V("dve", lambda e: e.tensor_copy(CS[0]["c_small"][:, 124:128], ST["C32"][:, :, 128]), ["C32_%d" % l], ["c_small_n"])
                    V("pe", lambda e: e.transpose(PS[b][0:4, 0:128], CS[0]["c_small"][:, 124:128], ident32), ["c_small_n", "ident32"], ["P%d" % b])
                    V("dve", lambda e: e.tensor_copy(io_n[0:4, :], PS[b][0:4, 0:128]), ["P%d" % b], ["io_n"])
                    S.dma("sp", O[pre + "_mn"][l, seq], io_n[0:4, :], ["io_n"], ())
                    S.dma("sp", O[pre + "_mm"][l, seq].rearrange("(a b) -> a b", b=1), ST["m"][0:4, :], ["m%d" % l], ())
                else:
                    S.dma("sp", O[pre + "_hg"][l, seq].rearrange("h k v -> k h v"), ST["S32"], ["S32_%d" % l], ())
            return fn

        def conv_store(pre, l, seq, src3, key):
            for q in range(3):
                b = nb()

                def f(e, q=q, b=b):
                    for j in range(4):
                        cc = q * 4 + j
                        ins = e.transpose(PS[b][0:3, j * 128:(j + 1) * 128], src3[:, cc, :], ident32)
                    return ins
                V("pe", f, [key, "ident32"], ["P%d" % b])
                V("dve", lambda e, q=q, b=b: e.tensor_copy(io_conv[0:3, q * 512:(q + 1) * 512], PS[b][0:3, :]), ["P%d" % b], ["io_conv"])
            S.dma("sp", O[pre + "_conv"][l, seq], io_conv[0:3, :], ["io_conv"], ())

        XSTG = [(w32a, ["w32a"]), (c_sc1, ["p0_c_sc", "p1_c_sc2"]), (io_ssd.rearrange("p a b -> p (a b)"), ["io_ssd"]),
                (io_conv[:, 0:1024], ["io_conv"])]

        def prefetch_x(tile):
            for ci, (c0, L, seq, first, last) in enumerate(tile["chunks"]):
                xi, xks = XSTG[ci % 4]
                S.dma("sp", xi[0:L, :], tile["xsrc"](ci), (), xks)
            tile["prefetched"] = True

        def run_tile(tile, next_tile=None):
            n = tile["n"]
            if not tile.get("prefetched"):
                prefetch_x(tile)
            for ci, (c0, L, seq, first, last) in enumerate(tile["chunks"]):
                xi, xks = XSTG[ci % 4]
                xk = xks[0]
                for hf in range(2):
                    b = nb()

                    def f(e, hf=hf, b=b, xi=xi, L=L):
                        for j in range(4):
                            ins = e.transpose(PS[b][:, j * 128:j * 128 + L], xi[0:L, (hf * 4 + j) * 128:(hf * 4 + j + 1) * 128], ident32[0:L, 0:L])
                        return ins
                    V("pe", f, xks + ["ident32"], ["P%d" % b])
                    V("dve", lambda e, hf=hf, b=b, c0=c0, L=L: e.tensor_copy(xT[:, hf * 4:(hf + 1) * 4, c0:c0 + L],
                                                                             PS[b][:, :].rearrange("p (j c) -> p j c", j=4)[:, :, 0:L]),
                      ["P%d" % b], ["xT"])
            if tile["kind"] == "s":
                for l in range(2):
                    pass
            for l in range(2):
                if tile["kind"] == "s":
                    for si, (off, ln, seq) in enumerate(tile["segs"]):
                        S.dma("sp", io_conv[0:3, :], state_conv[l, seq], (), ["io_conv"])
                        b = nb()

                        def f(e, b=b):
                            for cc in range(12):
                                ins = e.transpose(PS[b][:, cc * 3:cc * 3 + 3], io_conv[0:3, cc * 128:(cc + 1) * 128], ident32[0:3, 0:3])
                            return ins
                        V("pe", f, ["io_conv", "ident32"], ["P%d" % b])
                        V("dve", lambda e, si=si, b=b: e.tensor_copy(convin[:, si, :, :], PS[b][:, 0:36].rearrange("p (c t) -> p c t", c=12)),
                          ["P%d" % b], ["convin"])
                layer(l, tile)
                if tile["kind"] == "s":
                    for si, (off, ln, seq) in enumerate(tile["segs"]):
                        conv_store("s", l, seq, convout[:, si, :, :], "convout")
                elif tile["last_tile"]:
                    conv_store("p", l, tile["seq"], stt[l]["conv"], "conv%d" % l)
            if next_tile is not None:
                prefetch_x(next_tile)
            fence_all()
            V("act", lambda e: e.activation(sqs[:, 0:4, 0:n], xT[:, 0:4, 0:n], AF.Square), ["xT"], ["sqsF"])
            V("dve", lambda e: e.tensor_tensor(sqs[:, 4:8, 0:n], xT[:, 4:8, 0:n], xT[:, 4:8, 0:n], ALU.mult), ["xT"], ["sqsF_b"])
            b = nb()

            def f(e):
                for dc in range(8):
                    ins = e.matmul(PS[b][:, 0:n], onesb, sqs[:, dc, 0:n], start=(dc == 0), stop=(dc == 7))
                return ins
            V("pe", f, ["sqsF", "sqsF_b", "onesb"], ["P%d" % b])
            V("act", lambda e: e.activation(rstd[:, 0:n], PS[b][:, 0:n], AF.Ln, bias=EPS, scale=1.0 / D), ["P%d" % b], ["rstd"])
            V("act", lambda e: e.activation(rstd[:, 0:n], rstd[:, 0:n], AF.Exp, scale=-0.5), ["rstd"], ["rstd"])
            for dc in range(8):
                V("dve", lambda e, dc=dc: e.scalar_tensor_tensor(xT[:, dc, 0:n], xT[:, dc, 0:n], pvf[:, dc:dc + 1], rstd[:, 0:n], ALU.mult, ALU.mult),
                  ["xT", "rstd", "pvf"], ["xT"])
            for ci, (c0, L, seq, first, last) in enumerate(tile["chunks"]):
                xi = ostg[ci % 2]
                ok = "ostg%d" % (ci % 2)
                for hf in range(2):
                    b = nb()

                    def f(e, hf=hf, b=b, c0=c0, L=L):
                        for j in range(4):
                            ins = e.transpose(PS[b][0:L, j * 128:(j + 1) * 128], xT[:, hf * 4 + j, c0:c0 + L], ident32)
                        return ins
                    V("pe", f, ["xT", "ident32"], ["P%d" % b])
                    V("act", lambda e, hf=hf, b=b, xi=xi, L=L: e.copy(xi[0:L, hf * 512:(hf + 1) * 512], PS[b][0:L, :]), ["P%d" % b], [ok])
                S.dma("sp", tile["ydst"](ci), xi[0:L, :], [ok], ())

        tiles = []
        for p in range(NPS):
            for t0 in range(0, TP, TT):
                n = min(TT, TP - t0)
                chunks = []
                for c0 in range(0, n, 128):
                    L = min(128, n - c0)
                    chunks.append((c0, L, p, (t0 == 0 and c0 == 0), (t0 + c0 + L == TP)))
                tiles.append(dict(kind="p", seq=p, n=n, segs=[(0, n, p)], chunks=chunks, last_tile=(t0 + n == TP),
                                  first_tile=(p == 0 and t0 == 0),
                                  xsrc=(lambda ci, p=p, t0=t0, chunks=chunks: x_prompt[p, t0 + chunks[ci][0]:t0 + chunks[ci][0] + chunks[ci][1], :]),
                                  ydst=(lambda ci, p=p, t0=t0, chunks=chunks: O["p_y"][p, t0 + chunks[ci][0]:t0 + chunks[ci][0] + chunks[ci][1], :]),
                                  state_init=state_init_prompt, state_store=state_store("p")))
        if NSS > 0:
            chunks = [(i * TS, TS, i, True, True) for i in range(NSS)]
            tiles.append(dict(kind="s", seq=None, n=NSS * TS, segs=[(i * TS, TS, i) for i in range(NSS)], chunks=chunks, last_tile=True,
                              first_tile=False, convin=convin, convout=convout,
                              xsrc=(lambda ci: x_sample[ci, :, :]), ydst=(lambda ci: O["s_y"][ci, :, :]),
                              state_init=state_init_sample, state_store=state_store("s")))
        for tile in tiles:
            if tile["kind"] == "p" and tile["chunks"][0][3]:
                for l in range(2):
                    V("pool", lambda e, l=l: e.memset(stt[l]["conv"], 0.0), (), ["conv%d" % l])
            ti = tiles.index(tile)
            run_tile(tile, tiles[ti + 1] if ti + 1 < len(tiles) else None)
        S.emit()
    return nc, DBG


OUT_ORDER = ["_y", "_conv", "_ssd", "_mc", "_mn", "_mm", "_hg"]
_CACHE = {}


def kernel(**inputs):
    NC = 8
    x_prompt = np.ascontiguousarray(inputs["x_prompt"], dtype=np.float32)
    x_sample = np.ascontiguousarray(inputs["x_sample"], dtype=np.float32)
    B, T, _ = x_prompt.shape
    BS = x_sample.shape[0]
    NPS, NSS = B // NC, BS // NC
    key = (NPS, T, NSS)
    if key not in _CACHE:
        _CACHE[key] = build(NPS, T, NSS)[0]
    nc = _CACHE[key]
    wnames = ["norm_mix", "w_in", "conv_w", "conv_b", "dt_bias", "a_log", "d_skip", "ssd_gain", "ml_bi", "ml_bf", "ml_gain",
              "hg_lb", "hg_gain", "w_out", "norm_ffn", "w_ffn_in", "w_ffn_out", "norm_final"]
    snames = ["state_conv", "state_ssd", "state_mlstm_c", "state_mlstm_n", "state_mlstm_m", "state_hgrn"]
    in_maps = []
    for c in range(NC):
        m = {"x_prompt": x_prompt[c * NPS:(c + 1) * NPS], "x_sample": x_sample[c * NSS:(c + 1) * NSS]}
        for s in snames:
            m[s] = np.ascontiguousarray(np.asarray(inputs[s], dtype=np.float32)[:, c * NSS:(c + 1) * NSS])
        for w in wnames:
            m[w] = np.ascontiguousarray(inputs[w], dtype=np.float32)
        in_maps.append(m)
    res = run_bass_kernel_spmd(nc, in_maps, core_ids=list(range(NC)))
    outs = []
    for pre in ("p", "s"):
        for nm in OUT_ORDER:
            ax = 0 if nm == "_y" else 1
            outs.append(np.concatenate([np.asarray(r[pre + nm]) for r in res.results], axis=ax))
    yp, pc, pssd, pmc, pmn, pmm, phg, ys, sc, sssd, smc, smn, smm, shg = outs
    return (yp, ys, pc, pssd, pmc, pmn, pmm, phg, sc, sssd, smc, smn, smm, shg)
```
